# Optimizing a Trainium2 kernel written in Bass

```python
import math
import jax, jax.numpy as jnp
from jax import lax
import numpy as np

D_MODEL = 2048
BATCH = 4
SEQ = 2048
DEPTH = 1

MIX_WIDTH = D_MODEL
SSM_WIDTH = MIX_WIDTH // 2
ATTN_WIDTH = MIX_WIDTH - SSM_WIDTH
SSM_GROUP = 16
SSM_GROUPS = SSM_WIDTH // SSM_GROUP
SSM_STATE = 64
DT_MIN = 0.001
DT_MAX = 0.1
HEAD_DIM = 128
N_HEADS = ATTN_WIDTH // HEAD_DIM
DILATION_PAIRS = ((128, 1), (512, 4), (2048, 16))
Q_BLOCK = 128
ROPE_THETA = 10000.0
D_FF = -(-8 * D_MODEL // (3 * 256)) * 256
PROJ_WIDTH = SSM_WIDTH + 3 * ATTN_WIDTH
N_MOD = 6
EPS = 1e-6
NEG = -1e30

kernel_name = "hymba_s5_dilated_attn_block"


def rmsnorm(x, g):
    xf = x.astype(jnp.float32)
    y = xf * lax.rsqrt(jnp.mean(xf * xf, axis=-1, keepdims=True) + EPS)
    return (y * g.astype(jnp.float32)).astype(x.dtype)


def rope(t, pos):
    half = t.shape[-1] // 2
    inv_freq = ROPE_THETA ** (-jnp.arange(half, dtype=jnp.float32) / half)
    ang = pos.astype(jnp.float32)[:, :, None] * inv_freq
    cos = jnp.cos(ang)[:, :, None, :]
    sin = jnp.sin(ang)[:, :, None, :]
    t1, t2 = t[..., :half], t[..., half:]
    return jnp.concatenate([t1 * cos - t2 * sin, t1 * sin + t2 * cos], axis=-1)


def s5_mixer(u, a_re, a_im, log_dt, b_re, b_im, c_re, c_im, d_skip, w_glu, b_glu):
    bsz, s, _ = u.shape
    f32 = jnp.float32
    uf = u.astype(f32).reshape(bsz, s, SSM_GROUPS, SSM_GROUP)
    a_re = a_re.astype(f32); a_im = a_im.astype(f32)
    b_re = b_re.astype(f32); b_im = b_im.astype(f32)
    c_re = c_re.astype(f32); c_im = c_im.astype(f32)
    dt = jnp.exp(log_dt.astype(f32))[:, None]
    mag = jnp.exp(a_re * dt)
    abar_re = mag * jnp.cos(a_im * dt)
    abar_im = mag * jnp.sin(a_im * dt)
    den = a_re * a_re + a_im * a_im
    nr = abar_re - 1.0
    ni = abar_im
    f_re = (nr * a_re + ni * a_im) / den
    f_im = (ni * a_re - nr * a_im) / den
    bb_re = f_re[..., None] * b_re - f_im[..., None] * b_im
    bb_im = f_re[..., None] * b_im + f_im[..., None] * b_re
    bu_re = jnp.einsum('bsgc,gpc->bsgp', uf, bb_re)
    bu_im = jnp.einsum('bsgc,gpc->bsgp', uf, bb_im)
    ar_full = jnp.broadcast_to(abar_re, bu_re.shape)
    ai_full = jnp.broadcast_to(abar_im, bu_re.shape)

    def combine(e1, e2):
        a1r, a1i, b1r, b1i = e1
        a2r, a2i, b2r, b2i = e2
        return (a2r * a1r - a2i * a1i,
                a2r * a1i + a2i * a1r,
                a2r * b1r - a2i * b1i + b2r,
                a2r * b1i + a2i * b1r + b2i)

    _, _, st_re, st_im = lax.associative_scan(combine, (ar_full, ai_full, bu_re, bu_im), axis=1)
    y = (jnp.einsum('bsgp,gcp->bsgc', st_re, c_re)
         - jnp.einsum('bsgp,gcp->bsgc', st_im, c_im))
    y = y + d_skip.astype(f32).reshape(SSM_GROUPS, SSM_GROUP) * uf
    y = y.reshape(bsz, s, SSM_WIDTH)
    vg = jax.nn.gelu(y)
    out = vg * jax.nn.sigmoid(vg @ w_glu.astype(f32) + b_glu.astype(f32))
    return out.astype(u.dtype)


def dilated_branch(q, k, v, dilation, steps):
    bsz, s, h, e = q.shape
    seg = s // dilation
    nb = -(-seg // Q_BLOCK)
    segp = nb * Q_BLOCK

    def to_blocks(t):
        t = t.reshape(bsz, seg, dilation, h, e)
        t = jnp.pad(t, ((0, 0), (0, segp - seg), (0, 0), (0, 0), (0, 0)))
        return t.reshape(bsz, nb, Q_BLOCK, dilation, h, e)

    def with_prev(t):
        prev = jnp.pad(t[:, :-1], ((0, 0), (1, 0), (0, 0), (0, 0), (0, 0), (0, 0)))
        return jnp.concatenate([prev, t], axis=2)

    qb = to_blocks(q)
    kc = with_prev(to_blocks(k))
    vc = with_prev(to_blocks(v))
    sc = jnp.einsum('bnqrhe,bnkrhe->bnrhqk', qb, kc) * (e ** -0.5)
    qi = jnp.arange(Q_BLOCK)[:, None] + Q_BLOCK
    ki = jnp.arange(2 * Q_BLOCK)[None, :]
    dist = qi - ki
    band = (dist >= 0) & (dist <= steps)
    kglob = jnp.arange(nb)[:, None] * Q_BLOCK + ki - Q_BLOCK
    mask = band[None] & (kglob >= 0)[:, None, :]
    sc = jnp.where(mask[None, :, None, None], sc, NEG)
    m = jnp.max(sc, axis=-1, keepdims=True)
    p = jnp.exp(sc - m)
    l = jnp.sum(p, axis=-1, keepdims=True)
    o = jnp.einsum('bnrhqk,bnkrhe->bnrhqe', p, vc) / l
    lse = (m + jnp.log(l))[..., 0]
    o = o.transpose(0, 1, 4, 2, 3, 5).reshape(bsz, segp, dilation, h, e)[:, :seg]
    lse = lse.transpose(0, 1, 4, 2, 3).reshape(bsz, segp, dilation, h)[:, :seg]
    return o.reshape(bsz, s, h, e), lse.reshape(bsz, s, h)


def dilated_attention(q, k, v, positions):
    bsz, s, _ = q.shape
    f32 = jnp.float32
    q = rope(q.astype(f32).reshape(bsz, s, N_HEADS, HEAD_DIM), positions)
    k = rope(k.astype(f32).reshape(bsz, s, N_HEADS, HEAD_DIM), positions)
    vh = v.astype(f32).reshape(bsz, s, N_HEADS, HEAD_DIM)
    outs = []
    lses = []
    for window, dilation in DILATION_PAIRS:
        o, lse = dilated_branch(q, k, vh, dilation, window // dilation)
        outs.append(o)
        lses.append(lse)
    wts = jax.nn.softmax(jnp.stack(lses, axis=0), axis=0)
    o = jnp.sum(wts[..., None] * jnp.stack(outs, axis=0), axis=0)
    return o.reshape(bsz, s, ATTN_WIDTH).astype(v.dtype)


def setup_inputs(seed: int = 0) -> dict:
    key = jax.random.key(seed)
    ks = jax.random.split(key, 32)
    f32 = jnp.float32
    nrm = lambda k, shape, scale: jax.random.normal(k, shape, f32) * scale
    x = nrm(ks[0], (BATCH, SEQ, D_MODEL), 1.0)
    c = nrm(ks[1], (BATCH, D_MODEL), 1.0)
    offs = jax.random.randint(ks[2], (BATCH, 1), 0, 4096, dtype=jnp.int32)
    positions = offs + jnp.arange(SEQ, dtype=jnp.int32)[None, :]
    w_mod = nrm(ks[3], (DEPTH, D_MODEL, N_MOD * D_MODEL), 0.5 * D_MODEL ** -0.5)
    b_mod = nrm(ks[4], (DEPTH, N_MOD * D_MODEL), 0.02)
    g_mix = 1.0 + nrm(ks[5], (DEPTH, D_MODEL), 0.01)
    w_in = nrm(ks[6], (DEPTH, D_MODEL, PROJ_WIDTH), D_MODEL ** -0.5)
    n_idx = jnp.arange(SSM_STATE, dtype=f32)
    ssm_a_re = -0.5 + nrm(ks[7], (DEPTH, SSM_GROUPS, SSM_STATE), 0.01)
    ssm_a_im = math.pi * n_idx + nrm(ks[8], (DEPTH, SSM_GROUPS, SSM_STATE), 0.01)
    ssm_log_dt = jax.random.uniform(ks[9], (DEPTH, SSM_GROUPS), f32,
                                    math.log(DT_MIN), math.log(DT_MAX))
    ssm_b_re = nrm(ks[10], (DEPTH, SSM_GROUPS, SSM_STATE, SSM_GROUP), (2 * SSM_GROUP) ** -0.5)
    ssm_b_im = nrm(ks[11], (DEPTH, SSM_GROUPS, SSM_STATE, SSM_GROUP), (2 * SSM_GROUP) ** -0.5)
    ssm_c_re = nrm(ks[12], (DEPTH, SSM_GROUPS, SSM_GROUP, SSM_STATE), 1.0)
    ssm_c_im = nrm(ks[13], (DEPTH, SSM_GROUPS, SSM_GROUP, SSM_STATE), 1.0)
    ssm_d = nrm(ks[14], (DEPTH, SSM_WIDTH), 0.5)
    w_glu = nrm(ks[15], (DEPTH, SSM_WIDTH, SSM_WIDTH), SSM_WIDTH ** -0.5)
    b_glu = nrm(ks[16], (DEPTH, SSM_WIDTH), 0.01)
    g_ssm_out = 1.0 + nrm(ks[17], (DEPTH, SSM_WIDTH), 0.01)
    g_attn_out = 1.0 + nrm(ks[18], (DEPTH, ATTN_WIDTH), 0.01)
    w_out = nrm(ks[19], (DEPTH, MIX_WIDTH, D_MODEL), MIX_WIDTH ** -0.5)
    g_ffn = 1.0 + nrm(ks[20], (DEPTH, D_MODEL), 0.01)
    w_gate = nrm(ks[21], (DEPTH, D_MODEL, D_FF), D_MODEL ** -0.5)
    w_up = nrm(ks[22], (DEPTH, D_MODEL, D_FF), D_MODEL ** -0.5)
    w_down = nrm(ks[23], (DEPTH, D_FF, D_MODEL), D_FF ** -0.5)
    g_final = 1.0 + nrm(ks[24], (D_MODEL,), 0.01)
    return {"x": x, "c": c, "positions": positions, "w_mod": w_mod, "b_mod": b_mod,
            "g_mix": g_mix, "w_in": w_in, "ssm_a_re": ssm_a_re, "ssm_a_im": ssm_a_im,
            "ssm_log_dt": ssm_log_dt, "ssm_b_re": ssm_b_re, "ssm_b_im": ssm_b_im,
            "ssm_c_re": ssm_c_re, "ssm_c_im": ssm_c_im, "ssm_d": ssm_d, "w_glu": w_glu,
            "b_glu": b_glu, "g_ssm_out": g_ssm_out, "g_attn_out": g_attn_out, "w_out": w_out,
            "g_ffn": g_ffn, "w_gate": w_gate, "w_up": w_up, "w_down": w_down,
            "g_final": g_final}


def reference(x, c, positions, w_mod, b_mod, g_mix, w_in, ssm_a_re, ssm_a_im, ssm_log_dt,
              ssm_b_re, ssm_b_im, ssm_c_re, ssm_c_im, ssm_d, w_glu, b_glu, g_ssm_out,
              g_attn_out, w_out, g_ffn, w_gate, w_up, w_down, g_final):
    for l in range(DEPTH):
        mod = (jax.nn.silu(c) @ w_mod[l] + b_mod[l])[:, None, :]
        sh1, sc1, gt1, sh2, sc2, gt2 = jnp.split(mod, N_MOD, axis=-1)
        h = rmsnorm(x, g_mix[l]) * (1.0 + sc1) + sh1
        proj = h @ w_in[l]
        u, q, k, v = jnp.split(proj, [SSM_WIDTH, SSM_WIDTH + ATTN_WIDTH,
                                      SSM_WIDTH + 2 * ATTN_WIDTH], axis=-1)
        y_ssm = s5_mixer(u, ssm_a_re[l], ssm_a_im[l], ssm_log_dt[l], ssm_b_re[l], ssm_b_im[l],
                         ssm_c_re[l], ssm_c_im[l], ssm_d[l], w_glu[l], b_glu[l])
        y_att = dilated_attention(q, k, v, positions)
        y = jnp.concatenate([rmsnorm(y_ssm, g_ssm_out[l]), rmsnorm(y_att, g_attn_out[l])], axis=-1)
        x = x + gt1 * (y @ w_out[l])
        h = rmsnorm(x, g_ffn[l]) * (1.0 + sc2) + sh2
        f = (jax.nn.silu(h @ w_gate[l]) * (h @ w_up[l])) @ w_down[l]
        x = x + gt2 * f
    return rmsnorm(x, g_final)
```

```python
import math
from contextlib import ExitStack
import numpy as np
import concourse.bass as bass
import concourse.mybir as mybir
from concourse.bass_utils import run_bass_kernel_spmd

F32 = mybir.dt.float32
BF16 = mybir.dt.bfloat16
I32 = mybir.dt.int32
AF = mybir.ActivationFunctionType
ALU = mybir.AluOpType
AX = mybir.AxisListType

D = 2048
S = 2048
KT = D // 128
NMOD = 6
DFF = 5632
EPS = 1e-6
TWO_PI = 2.0 * math.pi


class Prog:
    ENGS = ("pe", "act", "dve", "pool", "sp")

    def __init__(self, nc, es):
        self.nc = nc
        self.es = es
        self.eng_ops = {e: [] for e in self.ENGS}
        self.last_write = {}
        self.readers = {}
        self.sem = {}
        self.cnt = {}
        self.nsem = 0
        for e in self.ENGS:
            self._new_eng_sem(e)
        self.known = {e: {} for e in self.ENGS}
        self.dma_pool = {}
        self.dma_val = {}
        self.dma_rr = {}
        for q, n in (("sp", 12), ("pool", 12), ("act", 6)):
            self.dma_pool[q] = [self._alloc_sem() for _ in range(n)]
            self.dma_rr[q] = 0
            for s in self.dma_pool[q]:
                self.dma_val[s] = 0
        self.pending = {e: {} for e in self.ENGS}
        self.sem_objs = {}

    def _alloc_sem(self):
        self.nsem += 1
        return self.es.enter_context(self.nc.semaphore("sm%d" % self.nsem))

    def _new_eng_sem(self, e):
        self.sem[e] = self._alloc_sem()
        self.cnt[e] = 0

    def _deps(self, eng, reads, writes):
        deps = {}

        def add(tok):
            if tok is None:
                return
            s, v = tok
            if deps.get(s, 0) < v:
                deps[s] = v
        for r in reads:
            add(self.last_write.get(r))
        for w in writes:
            add(self.last_write.get(w))
            for s, v in self.readers.get(w, {}).items():
                add((s, v))
        for s, v in self.pending[eng].items():
            add((s, v))
        self.pending[eng] = {}
        return deps

    def _finish(self, eng, deps, tok, reads, writes):
        waits = []
        for s, v in deps.items():
            if eng == "pe" and s is self.sem["pe"]:
                continue
            if self.known[eng].get(s, 0) >= v:
                continue
            self.known[eng][s] = v
            waits.append((s, v))
        for w in writes:
            self.last_write[w] = tok
            self.readers[w] = {}
        for r in reads:
            d = self.readers.setdefault(r, {})
            if d.get(tok[0], 0) < tok[1]:
                d[tok[0]] = tok[1]
        return waits

    def op(self, eng, fn, reads=(), writes=()):
        deps = self._deps(eng, reads, writes)
        if self.cnt[eng] >= 6000:
            self._new_eng_sem(eng)
        self.cnt[eng] += 1
        tok = (self.sem[eng], self.cnt[eng])
        waits = self._finish(eng, deps, tok, reads, writes)
        self.eng_ops[eng].append((waits, fn, tok[0], 1))
        return tok

    def dma(self, q, out, in_, reads=(), writes=(), **kw):
        deps = self._deps(q, reads, writes)
        pool = self.dma_pool[q]
        s = pool[self.dma_rr[q] % len(pool)]
        self.dma_rr[q] += 1
        if self.dma_val[s] > 0:
            if deps.get(s, 0) < self.dma_val[s]:
                deps[s] = self.dma_val[s]
        self.dma_val[s] += 16
        tok = (s, self.dma_val[s])
        waits = self._finish(q, deps, tok, reads, writes)
        self.eng_ops[q].append((waits, lambda e: e.dma_start(out=out, in_=in_, **kw), s, 16))
        return tok

    def barrier(self):
        toks = {}
        for e in self.ENGS:
            if self.cnt[e] > 0:
                toks[self.sem[e]] = self.cnt[e]
        for s, v in self.dma_val.items():
            if v > 0:
                toks[s] = v
        for e in self.ENGS:
            for s, v in toks.items():
                if self.pending[e].get(s, 0) < v:
                    self.pending[e][s] = v

    def simulate(self):
        val = {}
        pc = {e: 0 for e in self.ENGS}
        total = sum(len(v) for v in self.eng_ops.values())
        done = 0
        while done < total:
            prog = False
            for e in self.ENGS:
                ops = self.eng_ops[e]
                while pc[e] < len(ops):
                    waits, fn, s, inc = ops[pc[e]]
                    if all(val.get(id(ws), 0) >= wv for ws, wv in waits):
                        val[id(s)] = val.get(id(s), 0) + inc
                        pc[e] += 1
                        done += 1
                        prog = True
                    else:
                        break
            if not prog:
                st = {e: (pc[e], len(self.eng_ops[e])) for e in self.ENGS}
                raise RuntimeError("semaphore deadlock in program: %s" % st)
        return total

    def emit(self, block, final_tokens):
        nc = self.nc
        fin = {}
        for s, v in final_tokens:
            fin[s] = max(fin.get(s, 0), v)

        def run(e, name):
            for waits, fn, s, inc in self.eng_ops[name]:
                for ws, wv in waits:
                    e.wait_ge(ws, wv)
                fn(e).then_inc(s, inc)
            if name == "sp":
                for s, v in fin.items():
                    e.wait_ge(s, v)

        @block.sync
        def _(e):
            run(e, "sp")

        @block.scalar
        def _(e):
            run(e, "act")

        @block.vector
        def _(e):
            run(e, "dve")

        @block.gpsimd
        def _(e):
            run(e, "pool")

        @block.tensor
        def _(e):
            run(e, "pe")


def UTdump(P, src, UT):
    P.op("dve", lambda e: e.tensor_copy(out=UT[:, 0:1024], in_=src[:, 0:1024]), reads=["qT"], writes=["UT"])
    return UT[:, 0:1024]


def sincos(P, ang, angr, sdst, sr, cdst, cr, tf, tfr, ti, tir):
    P.op("dve", lambda e: e.tensor_scalar(out=tf[:], in0=ang[:], scalar1=1.0 / TWO_PI, scalar2=None, op0=ALU.mult),
         reads=[angr], writes=[tfr])
    P.op("dve", lambda e: e.tensor_copy(out=ti[:], in_=tf[:]), reads=[tfr], writes=[tir])
    P.op("dve", lambda e: e.tensor_copy(out=tf[:], in_=ti[:]), reads=[tir], writes=[tfr])
    P.op("dve", lambda e: e.scalar_tensor_tensor(out=sdst[:], in0=tf[:], scalar=-TWO_PI, in1=ang[:],
                                                 op0=ALU.mult, op1=ALU.add), reads=[tfr, angr], writes=[sr])

    def wrap(t, tr):
        P.op("dve", lambda e: e.tensor_scalar(out=tf[:], in0=t[:], scalar1=math.pi, scalar2=-TWO_PI,
                                              op0=ALU.is_gt, op1=ALU.mult), reads=[tr], writes=[tfr])
        P.op("dve", lambda e: e.tensor_tensor(out=t[:], in0=t[:], in1=tf[:], op=ALU.add), reads=[tr, tfr], writes=[tr])
        P.op("dve", lambda e: e.tensor_scalar(out=tf[:], in0=t[:], scalar1=-math.pi, scalar2=TWO_PI,
                                              op0=ALU.is_lt, op1=ALU.mult), reads=[tr], writes=[tfr])
        P.op("dve", lambda e: e.tensor_tensor(out=t[:], in0=t[:], in1=tf[:], op=ALU.add), reads=[tr, tfr], writes=[tr])
    wrap(sdst, sr)
    P.op("dve", lambda e: e.tensor_scalar_add(out=cdst[:], in0=sdst[:], scalar1=math.pi / 2), reads=[sr], writes=[cr])
    wrap(cdst, cr)
    for t, tr in ((sdst, sr), (cdst, cr)):
        P.op("dve", lambda e, t=t: e.tensor_scalar(out=t[:], in0=t[:], scalar1=math.pi, scalar2=-math.pi,
                                                   op0=ALU.min, op1=ALU.max), reads=[tr], writes=[tr])
        P.op("act", lambda e, t=t: e.activation(out=t[:], in_=t[:], func=AF.Sin), reads=[tr], writes=[tr])


def build(debug=None, att_level=9, ssm_full=False, full_tail=False):
    nc = bass.Bass("TRN2", target_bir_lowering=False)
    dram = {}

    def din(name, shape, dt=F32):
        dram[name] = nc.dram_tensor(name, list(shape), dt, kind="ExternalInput").ap()
        return dram[name]

    xb = din("xb", [S, D])
    xh = din("xh", [S // 2, D])
    c_l = din("c_l", [128, KT])
    pos = din("pos", [1, S], I32)
    sel = din("sel", [128, 2])
    w_mod = din("w_mod", [D, NMOD * D])
    b_mod = din("b_mod", [1, NMOD * D])
    g_mix = din("g_mix", [1, D])
    w_in = din("w_in", [D, 4096])
    out = nc.dram_tensor("out", [S // 2, D], F32, kind="ExternalOutput").ap()
    modrow = nc.dram_tensor("modrow", [1, 4 * D], F32, kind="Internal").ap()
    dbg = None
    if debug is not None:
        dbg = nc.dram_tensor("dbg", list(debug), F32, kind="ExternalOutput").ap()

    es = ExitStack()
    with es:
        P = Prog(nc, es)

        def sb(name, shape, dt=F32, st=es):
            return st.enter_context(nc.sbuf_tensor(name, list(shape), dt))

        ps = [es.enter_context(nc.psum_tensor("ps%d" % i, [128, 512], F32)) for i in range(8)]

        hT = sb("hT", [128, KT, S], BF16)
        ident = sb("ident", [128, 128], BF16)
        identf = sb("identf", [128, 128], F32)
        ones_bf = sb("ones_bf", [128, 128], BF16)
        yT = sb("yT", [128, 16, 1024], BF16)
        sqatt = sb("sqatt", [128, 1024], F32)
        sqssm = sb("sqssm", [128, 1024], F32)
        NH_RUN = 8 if debug is None else (1 if debug == (128, 1024) else 0)

        P.op("pool", lambda e: e.memset(identf[:], 1.0), writes=["identf"])
        P.op("pool", lambda e: e.affine_select(out=identf[:], in_=identf[:], pattern=[[-1, 128]],
                                               compare_op=ALU.is_equal, fill=0.0, base=0,
                                               channel_multiplier=1),
             reads=["identf"], writes=["identf"])
        P.op("dve", lambda e: e.tensor_copy(out=ident[:], in_=identf[:]), reads=["identf"], writes=["ident"])
        P.op("dve", lambda e: e.memset(ones_bf[:], 1.0), writes=["ones_bf"])

        final_tokens = []
        ph = ExitStack()
        with ph:
            cs = sb("cs", [128, KT], F32, ph)
            csr = sb("csr", [128, KT, 128], BF16, ph)
            G1 = sb("G1", [128, D], F32, ph)
            sh1 = sb("sh1", [128, D], F32, ph)
            gm = sb("gm", [128, D], F32, ph)
            wm = [sb("wm%d" % i, [128, KT, 512], BF16, ph) for i in range(2)]
            bm = [sb("bm%d" % i, [128, 512], F32, ph) for i in range(2)]
            xt = [sb("xt%d" % i, [128, D], F32, ph) for i in range(2)]
            hf = sb("hf", [128, D], F32, ph)
            hb = [sb("hb%d" % i, [128, D], BF16, ph) for i in range(2)]
            ss = sb("ss", [128, 16], F32, ph)
            rstd = sb("rstd", [128, 16], F32, ph)
            junk = sb("junk", [128, D], BF16, ph)

            P.dma("sp", cs[:], c_l[:, :], writes=["cs"])
            P.dma("sp", gm[:], g_mix[0:1, :].partition_broadcast(128), writes=["gm"])
            P.op("act", lambda e: e.activation(out=cs[:], in_=cs[:], func=AF.Silu), reads=["cs"], writes=["cs"])
            P.op("dve", lambda e: e.tensor_copy(out=csr[:], in_=cs[:].unsqueeze(2).to_broadcast([128, KT, 128])),
                 reads=["cs"], writes=["csr"])
            wmv = w_mod.rearrange("(kt p) n -> p kt n", p=128)

            def mod_chunk(j, evac):
                bi = j % 2
                P.dma("pool", wm[bi][:], wmv[:, :, j * 512:(j + 1) * 512], writes=["wm%d" % bi])
                P.dma("sp", bm[bi][:], b_mod[0:1, j * 512:(j + 1) * 512].partition_broadcast(128),
                      writes=["bm%d" % bi])
                pb = ps[bi]
                for kt in range(KT):
                    P.op("pe", lambda e, kt=kt: e.matmul(pb[:], lhsT=csr[:, kt, :], rhs=wm[bi][:, kt, :],
                                                         start=(kt == 0), stop=(kt == KT - 1)),
                         reads=["csr", "wm%d" % bi], writes=["ps%d" % bi])
                evac(j, pb, bm[bi], "ps%d" % bi, "bm%d" % bi)

            def evac1(j, pb, bmt, pr, br):
                if j < 4:
                    dst = sh1[:, j * 512:(j + 1) * 512]
                    P.op("dve", lambda e: e.tensor_tensor(out=dst, in0=pb[:], in1=bmt[:], op=ALU.add),
                         reads=[pr, br], writes=["sh1"])
                else:
                    jj = j - 4
                    dst = G1[:, jj * 512:(jj + 1) * 512]
                    P.op("dve", lambda e: e.tensor_tensor(out=dst, in0=pb[:], in1=bmt[:], op=ALU.add),
                         reads=[pr, br], writes=["G1"])
                    P.op("dve", lambda e: e.scalar_tensor_tensor(out=dst, in0=dst, scalar=1.0,
                                                                 in1=gm[:, jj * 512:(jj + 1) * 512],
                                                                 op0=ALU.add, op1=ALU.mult),
                         reads=["G1", "gm"], writes=["G1"])
            for j in range(8):
                mod_chunk(j, evac1)

            for tt in range(16):
                bi = tt % 2
                xtb = xt[bi]
                P.dma("sp", xtb[:], xb[tt * 128:(tt + 1) * 128, :], writes=["xt%d" % bi])
                P.op("act", lambda e, tt=tt, xtb=xtb: e.activation(out=junk[:], in_=xtb[:], func=AF.Square,
                                                                    accum_out=ss[:, tt:tt + 1]),
                     reads=["xt%d" % bi], writes=["junk", "ss%d" % tt])
                P.op("dve", lambda e, tt=tt: e.tensor_scalar(out=rstd[:, tt:tt + 1], in0=ss[:, tt:tt + 1],
                                                             scalar1=1.0 / D, scalar2=EPS,
                                                             op0=ALU.mult, op1=ALU.add),
                     reads=["ss%d" % tt], writes=["rstd%d" % tt])
                P.op("act", lambda e, tt=tt: e.activation(out=rstd[:, tt:tt + 1], in_=rstd[:, tt:tt + 1], func=AF.Sqrt),
                     reads=["rstd%d" % tt], writes=["rstd%d" % tt])
                P.op("dve", lambda e, tt=tt: e.reciprocal(out=rstd[:, tt:tt + 1], in_=rstd[:, tt:tt + 1]),
                     reads=["rstd%d" % tt], writes=["rstd%d" % tt])
                P.op("dve", lambda e, tt=tt, xtb=xtb: e.scalar_tensor_tensor(
                    out=hf[:], in0=xtb[:], scalar=rstd[:, tt:tt + 1], in1=G1[:], op0=ALU.mult, op1=ALU.mult),
                    reads=["xt%d" % bi, "rstd%d" % tt, "G1"], writes=["hf"])
                hbb = hb[bi]
                P.op("dve", lambda e, hbb=hbb: e.tensor_tensor(out=hbb[:], in0=hf[:], in1=sh1[:], op=ALU.add),
                     reads=["hf", "sh1"], writes=["hb%d" % bi])
                for g in range(2):
                    pi = 2 + (tt * 2 + g) % 2
                    pT = ps[pi][:].bitcast(BF16)
                    for k8 in range(8):
                        kt = g * 8 + k8
                        P.op("pe", lambda e, kt=kt, k8=k8, pT=pT, hbb=hbb: e.transpose(
                            out=pT[:, k8 * 128:(k8 + 1) * 128], in_=hbb[:, kt * 128:(kt + 1) * 128],
                            identity=ident[:]),
                            reads=["hb%d" % bi, "ident"], writes=["ps%d" % pi])
                    P.op("act", lambda e, g=g, tt=tt, pT=pT: e.activation(
                        out=hT[:, g * 8:(g + 1) * 8, tt * 128:(tt + 1) * 128],
                        in_=pT.rearrange("p (k t) -> p k t", k=8), func=AF.Copy),
                        reads=["ps%d" % pi], writes=["hT"])
            if debug is not None and debug == (128, KT * S):
                dtile = sb("dtile", [128, S], F32, ph)
                for kt in range(KT):
                    P.op("dve", lambda e, kt=kt: e.tensor_copy(out=dtile[:], in_=hT[:, kt, :]),
                         reads=["hT"], writes=["dtile"])
                    final_tokens.append(P.dma("sp", dbg[:, kt * S:(kt + 1) * S], dtile[:], reads=["dtile"]))
        P.barrier()


        g_att_l = din("g_att_l", [128, 8])
        ph = ExitStack()
        with ph:
            cosT = sb("cosT", [128, S], F32, ph)
            sinT = sb("sinT", [128, S], F32, ph)
            pidx = sb("pidx", [128, 1], F32, ph)
            invf = sb("invf", [128, 1], F32, ph)
            nsgn = sb("nsgn", [128, 1], F32, ph)
            neg1 = sb("neg1", [128, 1], F32, ph)
            selt = sb("selt", [128, 2], F32, ph)
            gat = sb("gat", [128, 8], F32, ph)
            permf = sb("permf", [128, 128], F32, ph)
            maskf = sb("maskf", [128, 256], F32, ph)
            maskb = sb("maskb", [128, 256], BF16, ph)
            permb = sb("permb", [128, 128], BF16, ph)
            ph0 = ExitStack()
            posi = sb("posi", [128, S], I32, ph0)
            angf = sb("angf", [128, S], F32, ph0)
            P.dma("sp", posi[:], pos[0:1, :].partition_broadcast(128), writes=["posi"])
            P.dma("sp", selt[:], sel[:, :], writes=["selt"])
            P.dma("sp", gat[:], g_att_l[:, :], writes=["gat"])
            P.op("pool", lambda e: e.iota(pidx[:], pattern=[[0, 1]], base=0, channel_multiplier=1,
                                          allow_small_or_imprecise_dtypes=True), writes=["pidx"])
            P.op("dve", lambda e: e.tensor_scalar(out=invf[:], in0=pidx[:], scalar1=64.0, scalar2=-64.0,
                                                  op0=ALU.is_ge, op1=ALU.mult), reads=["pidx"], writes=["invf"])
            P.op("dve", lambda e: e.tensor_tensor(out=invf[:], in0=invf[:], in1=pidx[:], op=ALU.add),
                 reads=["invf", "pidx"], writes=["invf"])
            P.op("act", lambda e: e.activation(out=invf[:], in_=invf[:], func=AF.Exp, scale=-math.log(10000.0) / 64.0),
                 reads=["invf"], writes=["invf"])
            P.op("dve", lambda e: e.tensor_scalar(out=nsgn[:], in0=pidx[:], scalar1=64.0, scalar2=2.0,
                                                  op0=ALU.is_ge, op1=ALU.mult), reads=["pidx"], writes=["nsgn"])
            P.op("dve", lambda e: e.tensor_scalar_add(out=nsgn[:], in0=nsgn[:], scalar1=-1.0),
                 reads=["nsgn"], writes=["nsgn"])
            P.op("dve", lambda e: e.tensor_copy(out=cosT[:], in_=posi[:]), reads=["posi"], writes=["cosT"])
            P.op("dve", lambda e: e.tensor_scalar(out=cosT[:], in0=cosT[:], scalar1=invf[:, 0:1], scalar2=None,
                                                  op0=ALU.mult), reads=["cosT", "invf"], writes=["cosT"])
            sincos(P, cosT, "cosT", sinT, "sinT", cosT, "cosT", angf, "angf", posi, "posi")
            P.op("dve", lambda e: e.tensor_scalar(out=sinT[:], in0=sinT[:], scalar1=nsgn[:, 0:1], scalar2=None,
                                                  op0=ALU.mult), reads=["sinT", "nsgn"], writes=["sinT"])
            P.op("dve", lambda e: e.tensor_copy(out=permf[:, 0:64], in_=identf[:, 64:128]), reads=["identf"], writes=["permf"])
            P.op("dve", lambda e: e.tensor_copy(out=permf[:, 64:128], in_=identf[:, 0:64]), reads=["identf"], writes=["permf"])
            P.op("pool", lambda e: e.memset(maskf[:], 1.0), writes=["maskf"])
            P.op("pool", lambda e: e.affine_select(out=maskf[:, 0:128], in_=maskf[:, 0:128], pattern=[[1, 128]],
                                                   compare_op=ALU.is_ge, fill=0.0, base=0, channel_multiplier=-1),
                 reads=["maskf"], writes=["maskf"])
            P.op("pool", lambda e: e.affine_select(out=maskf[:, 128:256], in_=maskf[:, 128:256], pattern=[[-1, 128]],
                                                   compare_op=ALU.is_ge, fill=0.0, base=0, channel_multiplier=1),
                 reads=["maskf"], writes=["maskf"])
            P.op("dve", lambda e: e.tensor_copy(out=maskb[:], in_=maskf[:]), reads=["maskf"], writes=["maskb"])
            P.op("dve", lambda e: e.tensor_copy(out=permb[:, 0:64], in_=ident[:, 64:128]), reads=["ident"], writes=["permb"])
            P.op("dve", lambda e: e.tensor_copy(out=permb[:, 64:128], in_=ident[:, 0:64]), reads=["ident"], writes=["permb"])
            qfb = maskf[:].bitcast(BF16)

            if debug is not None and debug == (128, 2048):
                final_tokens.append(P.dma("sp", dbg[:, :], sinT[:], reads=["sinT"]))
            ph0.close()
            P.barrier()
            wq = [sb("wq%d" % i, [128, KT, 3, 128], BF16, ph) for i in range(2)]
            qf = sb("qf", [128, 512], F32, ph)
            t1 = sb("t1", [128, 512], F32, ph)
            t2 = sb("t2", [128, 512], F32, ph)
            qT = sb("qT", [128, S], BF16, ph)
            kT = sb("kT", [128, S], BF16, ph)
            vT = sb("vT", [128, S], BF16, ph)
            Vd = sb("Vd", [128, 48, 128], BF16, ph)
            qd = sb("qd", [128, S], BF16, ph)
            kd = sb("kd", [128, S], BF16, ph)
            print("sbuf remaining (att)", nc.sbuf_bytes_remaining)
            Pm = [sb("Pm%d" % i, [128, 256], BF16, ph) for i in range(2)]
            UT = sb("UT", [128, S], F32, ph)
            LT = sb("LT", [128, S], F32, ph)

            wiv = w_in.rearrange("(kt p) n -> p kt n", p=128)
            scale = 1.0 / math.sqrt(128.0)
            cs2m = sb("cs2m", [128, KT], F32, ph)
            cs2b = sb("cs2b", [128, KT], BF16, ph)
            wm2m = sb("wm2m", [128, KT, 128], BF16, ph)
            bm2m = sb("bm2m", [1, 128], F32, ph)
            rowm = sb("rowm", [1, 128], F32, ph)
            print("sbuf remaining (att2)", nc.sbuf_bytes_remaining)
            P.dma("sp", cs2m[:], c_l[:, :], writes=["cs2m"])
            P.op("act", lambda e: e.activation(out=cs2b[:], in_=cs2m[:], func=AF.Silu), reads=["cs2m"], writes=["cs2b"])
            wmv2 = w_mod.rearrange("(kt p) n -> p kt n", p=128)
            MODN = [0]

            def mod_prefetch(idx):
                c0 = 4096 + idx * 128
                P.dma("pool", wm2m[:], wmv2[:, :, c0:c0 + 128], writes=["wm2m"])
                P.dma("sp", bm2m[:], b_mod[0:1, c0:c0 + 128], writes=["bm2m"])

            def mod_step():
                if NH_RUN != 8 or MODN[0] >= 64:
                    return
                idx = MODN[0]
                MODN[0] += 1
                c0 = idx * 128
                for kt in range(KT):
                    P.op("pe", lambda e, kt=kt: e.matmul(ps[3][0:1, 0:128], lhsT=cs2b[:, kt:kt + 1], rhs=wm2m[:, kt, :],
                                                         start=(kt == 0), stop=(kt == KT - 1)),
                         reads=["cs2b", "wm2m"], writes=["ps3"])
                P.op("dve", lambda e: e.tensor_tensor(out=rowm[:], in0=ps[3][0:1, 0:128], in1=bm2m[:], op=ALU.add),
                     reads=["ps3", "bm2m"], writes=["rowm"])
                P.dma("sp", modrow[0:1, c0:c0 + 128], rowm[0:1, :], reads=["rowm"], writes=["modrow"])
                if idx + 1 < 64:
                    mod_prefetch(idx + 1)
            if NH_RUN == 8:
                mod_prefetch(0)
            for h in range(NH_RUN):
                wb = wq[h % 2]
                wr = "wq%d" % (h % 2)
                for i3 in range(3):
                    c0 = 1024 * (i3 + 1) + h * 128
                    P.dma("pool", wb[:, :, i3, :], wiv[:, :, c0:c0 + 128], writes=[wr])
                for i3, dstT in ((0, qT), (1, kT), (2, vT)):
                    dn = ("qT", "kT", "vT")[i3]
                    for tc in range(4):
                        if tc % 2 == 0:
                            mod_step()
                        pb = ps[tc % 2]
                        pr = "ps%d" % (tc % 2)
                        for kt in range(KT):
                            P.op("pe", lambda e, kt=kt, pb=pb, i3=i3, tc=tc, wb=wb: e.matmul(
                                pb[:], lhsT=wb[:, kt, i3, :], rhs=hT[:, kt, tc * 512:(tc + 1) * 512],
                                start=(kt == 0), stop=(kt == KT - 1)), reads=[wr, "hT"], writes=[pr])
                        if i3 == 2:
                            P.op("act", lambda e, pb=pb, tc=tc: e.activation(out=vT[:, tc * 512:(tc + 1) * 512], in_=pb[:],
                                                                            func=AF.Copy), reads=[pr], writes=["vT"])
                        else:
                            P.op("act", lambda e, pb=pb: e.activation(out=qf[:], in_=pb[:], func=AF.Copy),
                                 reads=[pr], writes=["qf"])
                            p2 = ps[2 + tc % 2]
                            p2r = "ps%d" % (2 + tc % 2)
                            P.op("act", lambda e, pb=pb: e.activation(out=qfb, in_=pb[:], func=AF.Copy),
                                 reads=[pr], writes=["maskf"])
                            P.op("pe", lambda e, p2=p2: e.matmul(p2[:], lhsT=permb[:], rhs=qfb, start=True, stop=True),
                                 reads=["permb", "maskf"], writes=[p2r])
                            P.op("dve", lambda e, tc=tc: e.tensor_tensor(out=t1[:], in0=qf[:], in1=cosT[:, tc * 512:(tc + 1) * 512],
                                                                        op=ALU.mult), reads=["qf", "cosT"], writes=["t1"])
                            P.op("dve", lambda e, tc=tc, p2=p2: e.tensor_tensor(out=t2[:], in0=p2[:], in1=sinT[:, tc * 512:(tc + 1) * 512],
                                                                               op=ALU.mult), reads=[p2r, "sinT"], writes=["t2"])
                            P.op("dve", lambda e, tc=tc, dstT=dstT: e.tensor_tensor(out=dstT[:, tc * 512:(tc + 1) * 512], in0=t1[:], in1=t2[:],
                                                                                   op=ALU.add), reads=["t1", "t2"], writes=[dn])
                if att_level < 2:
                    final_tokens.append(P.dma("sp", dbg[:, :], t1[:].bitcast(F32)[:, 0:512] if False else UTdump(P, qT, UT), reads=["UT"]))
                    break
                mod_step()
                tiles = []
                for d in (1, 4, 16):
                    nb = 16 // d
                    for r in range(d):
                        for kb in range(nb):
                            tiles.append((d, r, kb))
                for g in range(6):
                    pi = 2 + g % 2
                    pT = ps[pi][:].bitcast(BF16)
                    for j8 in range(8):
                        d, r, kb = tiles[g * 8 + j8]
                        st = r + d * 128 * kb
                        P.op("pe", lambda e, pT=pT, j8=j8, st=st, d=d: e.transpose(
                            out=pT[:, j8 * 128:(j8 + 1) * 128], in_=vT[:, st:st + 127 * d + 1:d], identity=ident[:]),
                            reads=["vT", "ident"], writes=["ps%d" % pi])
                    P.op("act", lambda e, g=g, pT=pT: e.activation(out=Vd[:, g * 8:(g + 1) * 8, :],
                                                                 in_=pT.rearrange("p (k t) -> p k t", k=8), func=AF.Copy),
                         reads=["ps%d" % pi], writes=["Vd"])
                if att_level < 3:
                    break
                mod_step()
                ti = 0
                sc_i = 0
                for d in (1, 4, 16):
                    nb = 16 // d
                    if d == 1:
                        qsrc, ksrc, qn, kn = qT, kT, "qT", "kT"
                    else:
                        P.op("act", lambda e, d=d: e.activation(out=qd[:].rearrange("p (r j) -> p r j", r=d),
                                                               in_=qT[:].rearrange("p (j r) -> p r j", r=d), func=AF.Copy),
                             reads=["qT"], writes=["qd"])
                        P.op("pool", lambda e, d=d: e.tensor_copy(out=kd[:].rearrange("p (r j) -> p r j", r=d),
                                                                 in_=kT[:].rearrange("p (j r) -> p r j", r=d)),
                             reads=["kT"], writes=["kd"])
                        qsrc, ksrc, qn, kn = qd, kd, "qd", "kd"
                    for r in range(d):
                        for kb in range(nb):
                            nq = 256 if kb + 1 < nb else 128
                            st = r + d * 128 * kb
                            c0 = r * (S // d) + 128 * kb
                            psS = ps[sc_i % 2]
                            psr = "ps%d" % (sc_i % 2)
                            pmb = Pm[sc_i % 2]
                            pmr = "Pm%d" % (sc_i % 2)
                            sc_i += 1
                            P.op("pe", lambda e, psS=psS, c0=c0, nq=nq, qsrc=qsrc, ksrc=ksrc: e.matmul(
                                psS[:, 0:nq], lhsT=ksrc[:, c0:c0 + 128], rhs=qsrc[:, c0:c0 + nq],
                                start=True, stop=True), reads=[kn, qn], writes=[psr])
                            P.op("act", lambda e, psS=psS, pmb=pmb, nq=nq: e.activation(out=pmb[:, 0:nq], in_=psS[:, 0:nq], func=AF.Exp,
                                                                                       scale=scale), reads=[psr], writes=[pmr])
                            P.op("dve", lambda e, pmb=pmb, nq=nq: e.tensor_tensor(out=pmb[:, 0:nq], in0=pmb[:, 0:nq], in1=maskb[:, 0:nq],
                                                                                 op=ALU.mult), reads=[pmr, "maskb"], writes=[pmr])
                            for part in range(nq // 128):
                                qb = kb + part
                                pU = ps[4 + qb % 2]
                                pur = "ps%d" % (4 + qb % 2)
                                pL = ps[6 + qb % 2]
                                plr = "ps%d" % (6 + qb % 2)
                                first = (part == 1) or (kb == 0)
                                last = (part == 0)
                                vt = Vd[:, ti, :]
                                P.op("pe", lambda e, pU=pU, vt=vt, pmb=pmb, part=part, first=first, last=last: e.matmul(
                                    pU[:, 0:128], lhsT=vt, rhs=pmb[:, part * 128:(part + 1) * 128], start=first, stop=last),
                                    reads=["Vd", pmr], writes=[pur])
                                P.op("pe", lambda e, pL=pL, pmb=pmb, part=part, first=first, last=last: e.matmul(
                                    pL[:, 0:128], lhsT=ones_bf[:], rhs=pmb[:, part * 128:(part + 1) * 128], start=first, stop=last),
                                    reads=["ones_bf", pmr], writes=[plr])
                                if last:
                                    qs = r + d * 128 * qb
                                    udst = UT[:, qs:qs + 127 * d + 1:d]
                                    ldst = LT[:, qs:qs + 127 * d + 1:d]
                                    if d == 1:
                                        P.op("dve", lambda e, pU=pU, udst=udst: e.tensor_copy(out=udst, in_=pU[:, 0:128]),
                                             reads=[pur], writes=["UT"])
                                        P.op("dve", lambda e, pL=pL, ldst=ldst: e.tensor_copy(out=ldst, in_=pL[:, 0:128]),
                                             reads=[plr], writes=["LT"])
                                    else:
                                        P.op("dve", lambda e, pU=pU, udst=udst: e.tensor_tensor(out=udst, in0=pU[:, 0:128], in1=udst, op=ALU.add),
                                             reads=[pur, "UT"], writes=["UT"])
                                        P.op("dve", lambda e, pL=pL, ldst=ldst: e.tensor_tensor(out=ldst, in0=pL[:, 0:128], in1=ldst, op=ALU.add),
                                             reads=[plr, "LT"], writes=["LT"])
                            ti += 1
                P.op("dve", lambda e: e.reciprocal(out=LT[:], in_=LT[:]), reads=["LT"], writes=["LT"])
                P.op("dve", lambda e: e.tensor_tensor(out=UT[:], in0=UT[:], in1=LT[:], op=ALU.mult), reads=["UT", "LT"], writes=["UT"])
                P.op("dve", lambda e: e.tensor_scalar(out=LT[:, 0:1024], in0=UT[:, 1024:2048], scalar1=selt[:, 1:2], scalar2=None, op0=ALU.mult),
                     reads=["UT", "selt"], writes=["LT"])
                P.op("dve", lambda e: e.scalar_tensor_tensor(out=LT[:, 1024:2048], in0=UT[:, 0:1024], scalar=selt[:, 0:1], in1=LT[:, 0:1024],
                                                             op0=ALU.mult, op1=ALU.add), reads=["UT", "selt", "LT"], writes=["LT"])
                P.op("act", lambda e: e.activation(out=LT[:, 0:1024], in_=LT[:, 1024:2048], func=AF.Square), reads=["LT"], writes=["LT"])
                if h == 0:
                    P.op("pool", lambda e: e.tensor_copy(out=sqatt[:], in_=LT[:, 0:1024]), reads=["LT"], writes=["sqatt"])
                else:
                    P.op("pool", lambda e: e.tensor_tensor(out=sqatt[:], in0=sqatt[:], in1=LT[:, 0:1024], op=ALU.add),
                         reads=["LT", "sqatt"], writes=["sqatt"])
                P.op("act", lambda e, h=h: e.activation(out=yT[:, 8 + h, :], in_=LT[:, 1024:2048], func=AF.Identity, scale=gat[:, h:h + 1]),
                     reads=["LT", "gat"], writes=["yT"])
                if debug is not None and debug == (128, 1024) and h == 0:
                    final_tokens.append(P.dma("sp", dbg[:, :], LT[:, 1024:2048], reads=["LT"]))
        P.barrier()


        a_re_l = din("a_re_l", [128, 32])
        a_im_l = din("a_im_l", [128, 32])
        ldt_l = din("ldt_l", [128, 32])
        BTr = din("BTr", [64, 16, 64])
        BTi = din("BTi", [64, 16, 64])
        CTr = din("CTr", [64, 64, 16])
        CTi = din("CTi", [64, 64, 16])
        d_l = din("d_l", [128, 8])
        w_glu = din("w_glu", [1024, 1024])
        b_glu_l = din("b_glu_l", [128, 8])
        g_ssm_l = din("g_ssm_l", [128, 8])
        NCT = 8 if (debug is None or ssm_full) else 1
        ph = ExitStack()
        with ph:
            are = sb("are", [128, 32], F32, ph)
            aim = sb("aim", [128, 32], F32, ph)
            dtt = sb("dtt", [128, 32], F32, ph)
            th = sb("th", [128, 32], F32, ph)
            mag = sb("mag", [128, 32], F32, ph)
            cth = sb("cth", [128, 32], F32, ph)
            sth = sb("sth", [128, 32], F32, ph)
            fre = sb("fre", [128, 32], F32, ph)
            fim = sb("fim", [128, 32], F32, ph)
            sm1 = sb("sm1", [128, 32], F32, ph)
            sm2 = sb("sm2", [128, 32], F32, ph)
            smi = sb("smi", [128, 32], I32, ph)
            dcol = sb("dcol", [128, 8], F32, ph)
            bgl = sb("bgl", [128, 8], F32, ph)
            gsl = sb("gsl", [128, 8], F32, ph)
            selt2 = sb("selt2", [128, 2], F32, ph)
            rowmask = sb("rowmask", [128, 4], F32, ph)
            iotaN = sb("iotaN", [128, 32], F32, ph)
            iotaD = sb("iotaD", [128, 64], F32, ph)
            cN = sb("cN", [128, 32, 32], F32, ph)
            sN = sb("sN", [128, 32, 32], F32, ph)
            cD = sb("cD", [128, 32, 64], F32, ph)
            sD = sb("sD", [128, 32, 64], F32, ph)
            Bc = [sb("Bc%d" % i, [128, 8, 128], BF16, ph) for i in range(2)]
            Cc = [sb("Cc%d" % i, [128, 32, 32], BF16, ph) for i in range(2)]
            Clp = [[sb("Clp%d_%d" % (ri, q), [128, 128], BF16, ph) for q in range(4)] for ri in range(2)]
            Blp = [[sb("Blp%d_%d" % (ri, k), [128, 128], BF16, ph) for k in range(2)] for ri in range(2)]
            vgT = sb("vgT", [128, 8, 1024], BF16, ph)
            ysel = sb("ysel", [128, 1024], F32, ph)

            P.dma("sp", are[:], a_re_l[:, :], writes=["are"])
            P.dma("sp", aim[:], a_im_l[:, :], writes=["aim"])
            P.dma("sp", dtt[:], ldt_l[:, :], writes=["dtt"])
            P.dma("sp", dcol[:], d_l[:, :], writes=["dcol"])
            P.dma("sp", bgl[:], b_glu_l[:, :], writes=["bgl"])
            P.dma("sp", gsl[:], g_ssm_l[:, :], writes=["gsl"])
            P.dma("sp", selt2[:], sel[:, :], writes=["selt2"])
            P.op("act", lambda e: e.activation(out=dtt[:], in_=dtt[:], func=AF.Exp), reads=["dtt"], writes=["dtt"])
            P.op("dve", lambda e: e.tensor_tensor(out=th[:], in0=aim[:], in1=dtt[:], op=ALU.mult), reads=["aim", "dtt"], writes=["th"])
            P.op("dve", lambda e: e.tensor_tensor(out=mag[:], in0=are[:], in1=dtt[:], op=ALU.mult), reads=["are", "dtt"], writes=["mag"])
            P.op("act", lambda e: e.activation(out=mag[:], in_=mag[:], func=AF.Exp), reads=["mag"], writes=["mag"])
            P.op("dve", lambda e: e.tensor_copy(out=cth[:], in_=th[:]), reads=["th"], writes=["cth"])
            sincos(P, cth, "cth", sth, "sth", cth, "cth", sm1, "sm1", smi, "smi")
            P.op("dve", lambda e: e.tensor_tensor(out=cth[:], in0=cth[:], in1=mag[:], op=ALU.mult), reads=["cth", "mag"], writes=["cth"])
            P.op("dve", lambda e: e.tensor_scalar_add(out=cth[:], in0=cth[:], scalar1=-1.0), reads=["cth"], writes=["cth"])
            P.op("dve", lambda e: e.tensor_tensor(out=sth[:], in0=sth[:], in1=mag[:], op=ALU.mult), reads=["sth", "mag"], writes=["sth"])
            P.op("dve", lambda e: e.tensor_tensor(out=sm1[:], in0=are[:], in1=are[:], op=ALU.mult), reads=["are"], writes=["sm1"])
            P.op("dve", lambda e: e.tensor_tensor(out=sm2[:], in0=aim[:], in1=aim[:], op=ALU.mult), reads=["aim"], writes=["sm2"])
            P.op("dve", lambda e: e.tensor_tensor(out=sm1[:], in0=sm1[:], in1=sm2[:], op=ALU.add), reads=["sm1", "sm2"], writes=["sm1"])
            P.op("dve", lambda e: e.reciprocal(out=sm1[:], in_=sm1[:]), reads=["sm1"], writes=["sm1"])
            P.op("dve", lambda e: e.tensor_tensor(out=fre[:], in0=cth[:], in1=are[:], op=ALU.mult), reads=["cth", "are"], writes=["fre"])
            P.op("dve", lambda e: e.tensor_tensor(out=sm2[:], in0=sth[:], in1=aim[:], op=ALU.mult), reads=["sth", "aim"], writes=["sm2"])
            P.op("dve", lambda e: e.tensor_tensor(out=fre[:], in0=fre[:], in1=sm2[:], op=ALU.add), reads=["fre", "sm2"], writes=["fre"])
            P.op("dve", lambda e: e.tensor_tensor(out=fre[:], in0=fre[:], in1=sm1[:], op=ALU.mult), reads=["fre", "sm1"], writes=["fre"])
            P.op("dve", lambda e: e.tensor_tensor(out=fim[:], in0=sth[:], in1=are[:], op=ALU.mult), reads=["sth", "are"], writes=["fim"])
            P.op("dve", lambda e: e.tensor_tensor(out=sm2[:], in0=cth[:], in1=aim[:], op=ALU.mult), reads=["cth", "aim"], writes=["sm2"])
            P.op("dve", lambda e: e.tensor_tensor(out=fim[:], in0=fim[:], in1=sm2[:], op=ALU.subtract), reads=["fim", "sm2"], writes=["fim"])
            P.op("dve", lambda e: e.tensor_tensor(out=fim[:], in0=fim[:], in1=sm1[:], op=ALU.mult), reads=["fim", "sm1"], writes=["fim"])
            P.op("pool", lambda e: e.iota(iotaN[:], pattern=[[64, 32]], base=0, channel_multiplier=0,
                                          allow_small_or_imprecise_dtypes=True), writes=["iotaN"])
            P.op("pool", lambda e: e.iota(iotaD[:], pattern=[[1, 64]], base=0, channel_multiplier=0,
                                          allow_small_or_imprecise_dtypes=True), writes=["iotaD"])
            P.op("pool", lambda e: e.memset(rowmask[:], 1.0), writes=["rowmask"])
            P.op("pool", lambda e: e.affine_select(out=rowmask[:], in_=rowmask[:], pattern=[[-32, 4]], compare_op=ALU.is_ge,
                                                   fill=0.0, base=0, channel_multiplier=1), reads=["rowmask"], writes=["rowmask"])
            P.op("pool", lambda e: e.affine_select(out=rowmask[:], in_=rowmask[:], pattern=[[32, 4]], compare_op=ALU.is_ge,
                                                   fill=0.0, base=31, channel_multiplier=-1), reads=["rowmask"], writes=["rowmask"])
            phs = ExitStack()
            tbf = sb("tbf", [128, 32 * 64], F32, phs)
            tbi = sb("tbi", [128, 32 * 64], I32, phs)
            Cf = [sb("Cf%d" % i, [128, 32, 32], F32, phs) for i in range(2)]
            Ct = [sb("Ct%d" % i, [128, 32, 32], F32, phs) for i in range(2)]
            cNf = cN[:].rearrange("p a b -> p (a b)")
            sNf = sN[:].rearrange("p a b -> p (a b)")
            cDf = cD[:].rearrange("p a b -> p (a b)")
            sDf = sD[:].rearrange("p a b -> p (a b)")
            P.op("dve", lambda e: e.tensor_tensor(out=cN[:], in0=th[:].unsqueeze(2).to_broadcast([128, 32, 32]),
                                                  in1=iotaN[:].unsqueeze(1).to_broadcast([128, 32, 32]), op=ALU.mult),
                 reads=["th", "iotaN"], writes=["cN"])
            sincos(P, cNf, "cN", sNf, "sN", cNf, "cN", tbf[:, 0:1024], "tbf", tbi[:, 0:1024], "tbi")
            P.op("dve", lambda e: e.tensor_tensor(out=cD[:], in0=th[:].unsqueeze(2).to_broadcast([128, 32, 64]),
                                                  in1=iotaD[:].unsqueeze(1).to_broadcast([128, 32, 64]), op=ALU.mult),
                 reads=["th", "iotaD"], writes=["cD"])
            sincos(P, cDf, "cD", sDf, "sD", cDf, "cD", tbf, "tbf", tbi, "tbi")
            for ri, CT in ((0, CTr), (1, CTi)):
                P.op("pool", lambda e, ri=ri: e.memset(Cf[ri][:], 0.0), writes=["Cf%d" % ri])
                for g2 in range(2):
                    src = CT.rearrange("(i r) p c -> r p i c", r=2)[g2]
                    P.dma("sp", Cf[ri][64 * g2:64 * g2 + 64, :, 16 * g2:16 * g2 + 16], src, writes=["Cf%d" % ri])
            fre_b = fre[:].unsqueeze(2).to_broadcast([128, 32, 32])
            fim_b = fim[:].unsqueeze(2).to_broadcast([128, 32, 32])
            P.op("dve", lambda e: e.tensor_tensor(out=Ct[0][:], in0=Cf[0][:], in1=fre_b, op=ALU.mult), reads=["Cf0", "fre"], writes=["Ct0"])
            P.op("dve", lambda e: e.tensor_tensor(out=Ct[1][:], in0=Cf[1][:], in1=fim_b, op=ALU.mult), reads=["Cf1", "fim"], writes=["Ct1"])
            P.op("dve", lambda e: e.tensor_tensor(out=Cc[0][:], in0=Ct[0][:], in1=Ct[1][:], op=ALU.subtract), reads=["Ct0", "Ct1"], writes=["Cc0"])
            P.op("dve", lambda e: e.tensor_tensor(out=Ct[0][:], in0=Cf[0][:], in1=fim_b, op=ALU.mult), reads=["Cf0", "fim"], writes=["Ct0"])
            P.op("dve", lambda e: e.tensor_tensor(out=Ct[1][:], in0=Cf[1][:], in1=fre_b, op=ALU.mult), reads=["Cf1", "fre"], writes=["Ct1"])
            P.op("dve", lambda e: e.tensor_tensor(out=Ct[0][:], in0=Ct[0][:], in1=Ct[1][:], op=ALU.add), reads=["Ct0", "Ct1"], writes=["Ct0"])
            P.op("dve", lambda e: e.tensor_scalar(out=Cc[1][:], in0=Ct[0][:], scalar1=-1.0, scalar2=None, op0=ALU.mult),
                 reads=["Ct0"], writes=["Cc1"])
            for ri, BT in ((0, BTr), (1, BTi)):
                P.op("pool", lambda e, ri=ri: e.memset(Bc[ri][:], 0.0), writes=["Bc%d" % ri])
                for q in range(4):
                    for g2 in range(2):
                        src = BT.rearrange("(Q r) c p -> r c Q p", r=8)[2 * q + g2]
                        P.dma("pool", Bc[ri][32 * q + 16 * g2:32 * q + 16 * g2 + 16, :, 64 * g2:64 * g2 + 64], src,
                              writes=["Bc%d" % ri])
                for q in range(4):
                    P.op("pool", lambda e, ri=ri, q=q: e.memset(Clp[ri][q][:], 0.0), writes=["Clp%d_%d" % (ri, q)])
            phs.close()
            P.barrier()

            phl = ExitStack()
            wu = [sb("wu%d" % i, [128, KT, 128], BF16, phl) for i in range(1)]
            uAB = sb("uAB", [128, S], BF16, phl)
            uT = [sb("uT%d" % i, [128, S], BF16, phl) for i in range(1)]
            tabC = [sb("tabC%d" % i, [128, 512], F32, phl) for i in range(2)]
            tabS = [sb("tabS%d" % i, [128, 512], F32, phl) for i in range(2)]
            ttmp = sb("ttmp", [128, 512], F32, phl)
            Wre = [sb("Wre%d" % i, [128, 512], F32, phl) for i in range(2)]
            Wim = [sb("Wim%d" % i, [128, 512], F32, phl) for i in range(2)]
            sre = [sb("sre%d" % i, [128, 512], BF16, phl) for i in range(2)]
            sim = [sb("sim%d" % i, [128, 512], BF16, phl) for i in range(2)]
            r1 = [sb("r1_%d" % i, [128, 512], F32, phl) for i in range(5)]
            print("sbuf remaining (ssm)", nc.sbuf_bytes_remaining)
            wiv = w_in.rearrange("(kt p) n -> p kt n", p=128)
            for ct in range(NCT):
                wub = wu[0]
                wur = "wu0"
                uTb = uT[0]
                uTr = "uT0"
                P.dma("pool", wub[:], wiv[:, :, ct * 128:(ct + 1) * 128], writes=[wur])
                for tc in range(4):
                    pb = ps[tc % 2]
                    pr = "ps%d" % (tc % 2)
                    for kt in range(KT):
                        P.op("pe", lambda e, kt=kt, pb=pb, tc=tc, wub=wub: e.matmul(
                            pb[:], lhsT=wub[:, kt, :], rhs=hT[:, kt, tc * 512:(tc + 1) * 512],
                            start=(kt == 0), stop=(kt == KT - 1)), reads=[wur, "hT"], writes=[pr])
                    P.op("act", lambda e, pb=pb, tc=tc, uTb=uTb: e.activation(out=uTb[:, tc * 512:(tc + 1) * 512], in_=pb[:], func=AF.Copy),
                         reads=[pr], writes=[uTr])
                P.op("dve", lambda e, uTb=uTb: e.tensor_scalar(out=uAB[:, 0:1024], in0=uTb[:, 0:1024], scalar1=selt2[:, 1:2], scalar2=None, op0=ALU.mult),
                     reads=[uTr, "selt2"], writes=["uAB"])
                P.op("dve", lambda e, uTb=uTb: e.tensor_scalar(out=uAB[:, 1024:2048], in0=uTb[:, 1024:2048], scalar1=selt2[:, 1:2], scalar2=None, op0=ALU.mult),
                     reads=[uTr, "selt2", "uAB"], writes=["uAB"])
                P.op("dve", lambda e, uTb=uTb: e.scalar_tensor_tensor(out=uAB[:, 1024:2048], in0=uTb[:, 0:1024], scalar=selt2[:, 0:1], in1=uAB[:, 1024:2048],
                                                                      op0=ALU.mult, op1=ALU.add), reads=[uTr, "selt2", "uAB"], writes=["uAB"])
                for q in range(4):
                    i = 4 * ct + q
                    bl = [Blp[0][i % 2], Blp[1][i % 2]]
                    blr = ["Blp0_%d" % (i % 2), "Blp1_%d" % (i % 2)]
                    for ri in range(2):
                        P.op("dve", lambda e, ri=ri, ct=ct, q=q, bl=bl: e.tensor_scalar(
                            out=bl[ri][:], in0=Bc[ri][:, ct, :], scalar1=rowmask[:, q:q + 1], scalar2=None, op0=ALU.mult),
                            reads=["Bc%d" % ri, "rowmask"], writes=[blr[ri]])
                        P.op("pool", lambda e, ri=ri, i=i, q=q: e.tensor_copy(out=Clp[ri][q][:, 32 * q:32 * q + 32], in_=Cc[ri][:, i, :]),
                             reads=["Cc%d" % ri], writes=["Clp%d_%d" % (ri, q)])
                    for tc in range(4):
                        gc = i * 4 + tc
                        k2 = gc % 2
                        sl = slice(tc * 512, (tc + 1) * 512)
                        tC, tS = tabC[k2], tabS[k2]
                        tCr, tSr = "tabC%d" % k2, "tabS%d" % k2
                        wre, wim = Wre[k2], Wim[k2]
                        wrr, wir = "Wre%d" % k2, "Wim%d" % k2
                        srb, sib = sre[k2], sim[k2]
                        srr, sir = "sre%d" % k2, "sim%d" % k2
                        cNb = cN[:, i, 8 * tc:8 * tc + 8].unsqueeze(2).to_broadcast([128, 8, 64])
                        sNb = sN[:, i, 8 * tc:8 * tc + 8].unsqueeze(2).to_broadcast([128, 8, 64])
                        cDb = cD[:, i, :].unsqueeze(1).to_broadcast([128, 8, 64])
                        sDb = sD[:, i, :].unsqueeze(1).to_broadcast([128, 8, 64])
                        tC3 = tC[:].rearrange("p (n d) -> p n d", d=64)
                        tS3 = tS[:].rearrange("p (n d) -> p n d", d=64)
                        tt3 = ttmp[:].rearrange("p (n d) -> p n d", d=64)
                        P.op("pool", lambda e, a=cNb, b=cDb, o=tC3: e.tensor_tensor(out=o, in0=a, in1=b, op=ALU.mult), reads=["cN", "cD"], writes=[tCr])
                        P.op("pool", lambda e, a=sNb, b=sDb, o=tt3: e.tensor_tensor(out=o, in0=a, in1=b, op=ALU.mult), reads=["sN", "sD"], writes=["ttmp"])
                        P.op("pool", lambda e, tC=tC: e.tensor_tensor(out=tC[:], in0=tC[:], in1=ttmp[:], op=ALU.subtract), reads=[tCr, "ttmp"], writes=[tCr])
                        P.op("pool", lambda e, a=sNb, b=cDb, o=tS3: e.tensor_tensor(out=o, in0=a, in1=b, op=ALU.mult), reads=["sN", "cD"], writes=[tSr])
                        P.op("pool", lambda e, a=cNb, b=sDb, o=tt3: e.tensor_tensor(out=o, in0=a, in1=b, op=ALU.mult), reads=["cN", "sD"], writes=["ttmp"])
                        P.op("pool", lambda e, tS=tS: e.tensor_tensor(out=tS[:], in0=tS[:], in1=ttmp[:], op=ALU.add), reads=[tSr, "ttmp"], writes=[tSr])
                        for ri in range(2):
                            P.op("pe", lambda e, ri=ri, bl=bl, uTb=uTb, sl=sl: e.matmul(ps[ri][:], lhsT=bl[ri][:], rhs=uAB[:, sl], start=True, stop=True),
                                 reads=[blr[ri], "uAB"], writes=["ps%d" % ri])
                        P.op("dve", lambda e, wre=wre, tC=tC: e.tensor_tensor(out=wre[:], in0=ps[0][:], in1=tC[:], op=ALU.mult), reads=["ps0", tCr], writes=[wrr])
                        P.op("dve", lambda e, tS=tS: e.tensor_tensor(out=r1[0][:], in0=ps[1][:], in1=tS[:], op=ALU.mult), reads=["ps1", tSr], writes=["r1_0"])
                        P.op("dve", lambda e, wim=wim, tC=tC: e.tensor_tensor(out=wim[:], in0=ps[1][:], in1=tC[:], op=ALU.mult), reads=["ps1", tCr], writes=[wir])
                        P.op("dve", lambda e, tS=tS: e.tensor_tensor(out=r1[1][:], in0=ps[0][:], in1=tS[:], op=ALU.mult), reads=["ps0", tSr], writes=["r1_1"])
                        P.op("dve", lambda e, wre=wre: e.tensor_tensor(out=wre[:], in0=wre[:], in1=r1[0][:], op=ALU.add), reads=[wrr, "r1_0"], writes=[wrr])
                        P.op("dve", lambda e, wim=wim: e.tensor_tensor(out=wim[:], in0=wim[:], in1=r1[1][:], op=ALU.subtract), reads=[wir, "r1_1"], writes=[wir])
                        for W, Wp, wn, wpn in ((wre, Wre[1 - k2], wrr, "Wre%d" % (1 - k2)), (wim, Wim[1 - k2], wir, "Wim%d" % (1 - k2))):
                            init = 0.0 if tc == 0 else Wp[:, 511:512]
                            P.op("dve", lambda e, W=W, init=init, i=i: e.tensor_tensor_scan(
                                out=W[:], data0=mag[:, i:i + 1].to_broadcast([128, 512]), data1=W[:], initial=init,
                                op0=ALU.mult, op1=ALU.add), reads=[wn, wpn, "mag"], writes=[wn])
                        if tc < 2:
                            continue
                        P.op("dve", lambda e, wre=wre, tC=tC: e.tensor_tensor(out=ttmp[:], in0=wre[:], in1=tC[:], op=ALU.mult), reads=[wrr, tCr], writes=["ttmp"])
                        P.op("dve", lambda e, wim=wim, tS=tS: e.tensor_tensor(out=r1[2][:], in0=wim[:], in1=tS[:], op=ALU.mult), reads=[wir, tSr], writes=["r1_2"])
                        P.op("dve", lambda e, wre=wre, tS=tS: e.tensor_tensor(out=r1[3][:], in0=wre[:], in1=tS[:], op=ALU.mult), reads=[wrr, tSr], writes=["r1_3"])
                        P.op("dve", lambda e, wim=wim, tC=tC: e.tensor_tensor(out=r1[4][:], in0=wim[:], in1=tC[:], op=ALU.mult), reads=[wir, tCr], writes=["r1_4"])
                        P.op("dve", lambda e, srb=srb: e.tensor_tensor(out=srb[:], in0=ttmp[:], in1=r1[2][:], op=ALU.subtract), reads=["ttmp", "r1_2"], writes=[srr])
                        P.op("dve", lambda e, sib=sib: e.tensor_tensor(out=sib[:], in0=r1[3][:], in1=r1[4][:], op=ALU.add), reads=["r1_3", "r1_4"], writes=[sir])
                        yb = ps[4 + tc]
                        ybr = "ps%d" % (4 + tc)
                        P.op("pe", lambda e, yb=yb, q=q, srb=srb: e.matmul(yb[:], lhsT=Clp[0][q][:], rhs=srb[:], start=(q == 0), stop=False),
                             reads=["Clp0_%d" % q, srr], writes=[ybr])
                        P.op("pe", lambda e, yb=yb, q=q, sib=sib: e.matmul(yb[:], lhsT=Clp[1][q][:], rhs=sib[:], start=False, stop=(q == 3)),
                             reads=["Clp1_%d" % q, sir], writes=[ybr])
                for tc in (2, 3):
                    sl = slice(tc * 512, (tc + 1) * 512)
                    yb = ps[4 + tc]
                    ybr = "ps%d" % (4 + tc)
                    o2 = slice((tc - 2) * 512, (tc - 1) * 512)
                    P.op("dve", lambda e, yb=yb, sl=sl, ct=ct, o2=o2: e.scalar_tensor_tensor(
                        out=ysel[:, o2], in0=uAB[:, sl], scalar=dcol[:, ct:ct + 1], in1=yb[:], op0=ALU.mult, op1=ALU.add),
                        reads=["uAB", "dcol", ybr], writes=["ysel"])
                P.op("act", lambda e, ct=ct: e.activation(out=vgT[:, ct, :], in_=ysel[:], func=AF.Gelu), reads=["ysel"], writes=["vgT"])
                if debug is not None and debug == (128, 1025) and ct == 0:
                    final_tokens.append(P.dma("sp", dbg[:, 0:1024], ysel[:], reads=["ysel"]))
            phl.close()
            P.barrier()
            if NCT == 8:
                wgl = sb("wgl", [128, 8, 1024], BF16, ph)
                gte = sb("gte", [128, 512], F32, ph)
                osm = sb("osm", [128, 1024], F32, ph)
                P.dma("pool", wgl[:], w_glu.rearrange("(kt p) n -> p kt n", p=128), writes=["wgl"])
                for jt in range(8):
                    for hc in range(2):
                        sl = slice(hc * 512, (hc + 1) * 512)
                        pb = ps[hc]
                        pr = "ps%d" % hc
                        for ct in range(8):
                            P.op("pe", lambda e, pb=pb, ct=ct, jt=jt, sl=sl: e.matmul(
                                pb[:], lhsT=wgl[:, ct, jt * 128:(jt + 1) * 128], rhs=vgT[:, ct, sl], start=(ct == 0), stop=(ct == 7)),
                                reads=["wgl", "vgT"], writes=[pr])
                        P.op("act", lambda e, pb=pb, jt=jt: e.activation(out=gte[:], in_=pb[:], func=AF.Sigmoid, bias=bgl[:, jt:jt + 1]),
                             reads=[pr, "bgl"], writes=["gte"])
                        P.op("dve", lambda e, jt=jt, sl=sl: e.tensor_tensor(out=osm[:, sl], in0=vgT[:, jt, sl], in1=gte[:], op=ALU.mult),
                             reads=["vgT", "gte"], writes=["osm"])
                    P.op("act", lambda e: e.activation(out=ysel[:], in_=osm[:], func=AF.Square), reads=["osm"], writes=["ysel"])
                    if jt == 0:
                        P.op("pool", lambda e: e.tensor_copy(out=sqssm[:], in_=ysel[:]), reads=["ysel"], writes=["sqssm"])
                    else:
                        P.op("pool", lambda e: e.tensor_tensor(out=sqssm[:], in0=sqssm[:], in1=ysel[:], op=ALU.add),
                             reads=["ysel", "sqssm"], writes=["sqssm"])
                    P.op("act", lambda e, jt=jt: e.activation(out=yT[:, jt, :], in_=osm[:], func=AF.Identity, scale=gsl[:, jt:jt + 1]),
                         reads=["osm", "gsl"], writes=["yT"])
                    if debug is not None and debug == (128, 1026) and jt == 0:
                        final_tokens.append(P.dma("sp", dbg[:, 0:1024], osm[:], reads=["osm"]))
        P.barrier()


        if debug is None or full_tail:
            w_out = din("w_out", [D, D])
            g_ffn = din("g_ffn", [1, D])
            w_gate = din("w_gate", [D, DFF])
            w_up = din("w_up", [D, DFF])
            w_down = din("w_down", [DFF, D])
            g_final = din("g_final", [1, D])
            x1v = hT[:].rearrange("p a b -> p (a b)").bitcast(F32).rearrange("p (t d) -> p t d", d=D)
            h2T = yT
            wmv = w_mod.rearrange("(kt p) n -> p kt n", p=128)

            mgc = [0]

            def mod_group(st, chunks, dsts, post=None):
                mgc[0] += 1
                sfx = "_g%d" % mgc[0]
                cs2 = sb("cs2" + sfx, [128, KT], F32, st)
                csr2 = sb("csr2" + sfx, [128, KT, 128], BF16, st)
                wm2 = [sb("wm2_%d%s" % (i, sfx), [128, KT, 512], BF16, st) for i in range(2)]
                bm2 = [sb("bm2_%d%s" % (i, sfx), [128, 512], F32, st) for i in range(2)]
                P.dma("sp", cs2[:], c_l[:, :], writes=["cs2"])
                P.op("act", lambda e: e.activation(out=cs2[:], in_=cs2[:], func=AF.Silu), reads=["cs2"], writes=["cs2"])
                P.op("dve", lambda e: e.tensor_copy(out=csr2[:], in_=cs2[:].unsqueeze(2).to_broadcast([128, KT, 128])),
                     reads=["cs2"], writes=["csr2"])
                for n, j in enumerate(chunks):
                    bi = n % 2
                    dst, dn = dsts[n // 4]
                    jj = n % 4
                    P.dma("pool", wm2[bi][:], wmv[:, :, j * 512:(j + 1) * 512], writes=["wm2_%d" % bi])
                    P.dma("sp", bm2[bi][:], b_mod[0:1, j * 512:(j + 1) * 512].partition_broadcast(128), writes=["bm2_%d" % bi])
                    pb = ps[bi]
                    for kt in range(KT):
                        P.op("pe", lambda e, kt=kt, pb=pb, bi=bi: e.matmul(pb[:], lhsT=csr2[:, kt, :], rhs=wm2[bi][:, kt, :],
                                                                        start=(kt == 0), stop=(kt == KT - 1)),
                             reads=["csr2", "wm2_%d" % bi], writes=["ps%d" % bi])
                    P.op("dve", lambda e, pb=pb, bi=bi, dst=dst, jj=jj: e.tensor_tensor(
                        out=dst[:, jj * 512:(jj + 1) * 512], in0=pb[:], in1=bm2[bi][:], op=ALU.add),
                        reads=["ps%d" % bi, "bm2_%d" % bi], writes=[dn])

            ph = ExitStack()
            with ph:
                onesf = sb("onesf", [128, 1], F32, ph)
                rsq = sb("rsq", [128, 16], F32, ph)
                gt1 = sb("gt1", [128, D], F32, ph)
                P.op("dve", lambda e: e.memset(onesf[:], 1.0), writes=["onesf"])
                for k, (sq, sqn) in enumerate(((sqssm, "sqssm"), (sqatt, "sqatt"))):
                    for tt in range(8):
                        P.op("pe", lambda e, k=k, tt=tt, sq=sq: e.matmul(ps[7][:, k * 8 + tt:k * 8 + tt + 1], lhsT=sq[:, tt * 128:(tt + 1) * 128],
                                                                      rhs=onesf[:, 0:1], start=True, stop=True),
                             reads=[sqn, "onesf"], writes=["ps7"])
                P.op("dve", lambda e: e.tensor_scalar(out=rsq[:], in0=ps[7][:, 0:16], scalar1=1.0 / 1024.0, scalar2=EPS,
                                                      op0=ALU.mult, op1=ALU.add), reads=["ps7"], writes=["rsq"])
                P.op("act", lambda e: e.activation(out=rsq[:], in_=rsq[:], func=AF.Sqrt), reads=["rsq"], writes=["rsq"])
                P.op("dve", lambda e: e.reciprocal(out=rsq[:], in_=rsq[:]), reads=["rsq"], writes=["rsq"])
                P.dma("sp", gt1[:], modrow[0:1, 0:D].partition_broadcast(128), reads=["modrow"], writes=["gt1"])
                for tt in range(8):
                    P.dma("sp", x1v[:, tt, :], xh[tt * 128:(tt + 1) * 128, :], writes=["hT"])
                wo = [sb("wo%d" % i, [128, KT, 512], BF16, ph) for i in range(2)]
                tA = sb("tA", [128, 512], F32, ph)
                tB = sb("tB", [128, 512], F32, ph)
                wov = w_out.rearrange("(kt p) n -> p kt n", p=128)
                for dc in range(4):
                    wb = wo[dc % 2]
                    wr = "wo%d" % (dc % 2)
                    dsl = slice(dc * 512, (dc + 1) * 512)
                    P.dma("pool", wb[:], wov[:, :, dsl], writes=[wr])
                    for tt in range(8):
                        k2 = (dc * 8 + tt) % 2
                        pS, pA = ps[2 * k2], ps[2 * k2 + 1]
                        pSr, pAr = "ps%d" % (2 * k2), "ps%d" % (2 * k2 + 1)
                        for ct in range(8):
                            P.op("pe", lambda e, pS=pS, ct=ct, tt=tt, wb=wb: e.matmul(
                                pS[:], lhsT=yT[:, ct, tt * 128:(tt + 1) * 128], rhs=wb[:, ct, :], start=(ct == 0), stop=(ct == 7)),
                                reads=["yT", wr], writes=[pSr])
                        for ct in range(8, 16):
                            P.op("pe", lambda e, pA=pA, ct=ct, tt=tt, wb=wb: e.matmul(
                                pA[:], lhsT=yT[:, ct, tt * 128:(tt + 1) * 128], rhs=wb[:, ct, :], start=(ct == 8), stop=(ct == 15)),
                                reads=["yT", wr], writes=[pAr])
                        P.op("act", lambda e, pS=pS, tt=tt: e.activation(out=tA[:], in_=pS[:], func=AF.Identity, scale=rsq[:, tt:tt + 1]),
                             reads=[pSr, "rsq"], writes=["tA"])
                        P.op("dve", lambda e, pA=pA, tt=tt: e.scalar_tensor_tensor(out=tB[:], in0=pA[:], scalar=rsq[:, 8 + tt:9 + tt], in1=tA[:],
                                                                                 op0=ALU.mult, op1=ALU.add), reads=[pAr, "rsq", "tA"], writes=["tB"])
                        P.op("dve", lambda e, dsl=dsl: e.tensor_tensor(out=tB[:], in0=tB[:], in1=gt1[:, dsl], op=ALU.mult),
                             reads=["tB", "gt1"], writes=["tB"])
                        P.op("pool", lambda e, tt=tt, dsl=dsl: e.tensor_tensor(out=x1v[:, tt, dsl], in0=x1v[:, tt, dsl], in1=tB[:], op=ALU.add),
                             reads=["tB", "hT"], writes=["hT"])
            P.barrier()

            ph = ExitStack()
            with ph:
                gt2 = sb("gt2", [128, D], F32, ph)
                ss2 = sb("ss2", [128, 8], F32, ph)
                rstd2 = sb("rstd2", [128, 8], F32, ph)
                phn = ExitStack()
                G2 = sb("G2", [128, D], F32, phn)
                sh2 = sb("sh2", [128, D], F32, phn)
                gf = sb("gf", [128, D], F32, phn)
                hf2 = sb("hf2", [128, D], F32, phn)
                hb2 = [sb("hb2_%d" % i, [128, D], BF16, phn) for i in range(2)]
                junk2 = sb("junk2", [128, D], BF16, phn)
                P.dma("sp", sh2[:], modrow[0:1, D:2 * D].partition_broadcast(128), reads=["modrow"], writes=["sh2"])
                P.dma("sp", G2[:], modrow[0:1, 2 * D:3 * D].partition_broadcast(128), reads=["modrow"], writes=["G2"])
                P.dma("sp", gt2[:], modrow[0:1, 3 * D:4 * D].partition_broadcast(128), reads=["modrow"], writes=["gt2"])
                P.dma("sp", gf[:], g_ffn[0:1, :].partition_broadcast(128), writes=["gf"])
                P.op("dve", lambda e: e.scalar_tensor_tensor(out=G2[:], in0=G2[:], scalar=1.0, in1=gf[:], op0=ALU.add, op1=ALU.mult),
                     reads=["G2", "gf"], writes=["G2"])
                for tt in range(8):
                    bi = tt % 2
                    P.op("act", lambda e, tt=tt: e.activation(out=junk2[:], in_=x1v[:, tt, :], func=AF.Square, accum_out=ss2[:, tt:tt + 1]),
                         reads=["hT"], writes=["junk2", "ss2_%d" % tt])
                    P.op("dve", lambda e, tt=tt: e.tensor_scalar(out=rstd2[:, tt:tt + 1], in0=ss2[:, tt:tt + 1], scalar1=1.0 / D, scalar2=EPS,
                                                                 op0=ALU.mult, op1=ALU.add), reads=["ss2_%d" % tt], writes=["rstd2_%d" % tt])
                    P.op("act", lambda e, tt=tt: e.activation(out=rstd2[:, tt:tt + 1], in_=rstd2[:, tt:tt + 1], func=AF.Sqrt),
                         reads=["rstd2_%d" % tt], writes=["rstd2_%d" % tt])
                    P.op("dve", lambda e, tt=tt: e.reciprocal(out=rstd2[:, tt:tt + 1], in_=rstd2[:, tt:tt + 1]),
                         reads=["rstd2_%d" % tt], writes=["rstd2_%d" % tt])
                    P.op("dve", lambda e, tt=tt: e.scalar_tensor_tensor(out=hf2[:], in0=x1v[:, tt, :], scalar=rstd2[:, tt:tt + 1], in1=G2[:],
                                                                        op0=ALU.mult, op1=ALU.mult), reads=["hT", "rstd2_%d" % tt, "G2"], writes=["hf2"])
                    hbb = hb2[bi]
                    P.op("dve", lambda e, hbb=hbb: e.tensor_tensor(out=hbb[:], in0=hf2[:], in1=sh2[:], op=ALU.add),
                         reads=["hf2", "sh2"], writes=["hb2_%d" % bi])
                    for g in range(2):
                        pi = 2 + (tt * 2 + g) % 2
                        pT = ps[pi][:].bitcast(BF16)
                        for k8 in range(8):
                            kt = g * 8 + k8
                            P.op("pe", lambda e, kt=kt, k8=k8, pT=pT, hbb=hbb: e.transpose(
                                out=pT[:, k8 * 128:(k8 + 1) * 128], in_=hbb[:, kt * 128:(kt + 1) * 128], identity=ident[:]),
                                reads=["hb2_%d" % bi, "ident"], writes=["ps%d" % pi])
                        P.op("act", lambda e, g=g, tt=tt, pT=pT: e.activation(
                            out=h2T[:, g * 8:(g + 1) * 8, tt * 128:(tt + 1) * 128], in_=pT.rearrange("p (k t) -> p k t", k=8), func=AF.Copy),
                            reads=["ps%d" % pi], writes=["yT"])
                phn.close()
                P.barrier()
                wg = [sb("wg%d" % i, [128, KT, 256], BF16, ph) for i in range(2)]
                wu2 = [sb("wu2_%d" % i, [128, KT, 256], BF16, ph) for i in range(2)]
                wd = sb("wd", [128, 4, D], BF16, ph)
                actT = sb("actT", [128, 4, 1024], BF16, ph)
                sg = [sb("sg%d" % i, [128, 512], F32, ph) for i in range(2)]
                tD = [sb("tD%d" % i, [128, 512], F32, ph) for i in range(2)]
                print("sbuf remaining (ffn)", nc.sbuf_bytes_remaining)
                wgv = w_gate.rearrange("(kt p) n -> p kt n", p=128)
                wuv = w_up.rearrange("(kt p) n -> p kt n", p=128)
                wdv = w_down.rearrange("(f p) d -> p f d", p=128)
                NFB = DFF // 512
                wi = 0
                ei = 0
                for fb in range(NFB):
                    for half in range(2):
                        k2 = wi % 2
                        wi += 1
                        c0 = fb * 512 + half * 256
                        P.dma("pool", wg[k2][:], wgv[:, :, c0:c0 + 256], writes=["wg%d" % k2])
                        P.dma("pool", wu2[k2][:], wuv[:, :, c0:c0 + 256], writes=["wu2_%d" % k2])
                        for f2 in range(2):
                            fi = half * 2 + f2
                            for hc in range(2):
                                tsl = slice(hc * 512, (hc + 1) * 512)
                                pG, pU_ = ps[hc], ps[2 + hc]
                                for kt in range(KT):
                                    P.op("pe", lambda e, pG=pG, kt=kt, k2=k2, f2=f2, tsl=tsl: e.matmul(
                                        pG[:], lhsT=wg[k2][:, kt, f2 * 128:(f2 + 1) * 128], rhs=h2T[:, kt, tsl], start=(kt == 0), stop=(kt == KT - 1)),
                                        reads=["wg%d" % k2, "yT"], writes=["ps%d" % hc])
                                for kt in range(KT):
                                    P.op("pe", lambda e, pU_=pU_, kt=kt, k2=k2, f2=f2, tsl=tsl: e.matmul(
                                        pU_[:], lhsT=wu2[k2][:, kt, f2 * 128:(f2 + 1) * 128], rhs=h2T[:, kt, tsl], start=(kt == 0), stop=(kt == KT - 1)),
                                        reads=["wu2_%d" % k2, "yT"], writes=["ps%d" % (2 + hc)])
                                P.op("act", lambda e, pG=pG, hc=hc: e.activation(out=sg[hc][:], in_=pG[:], func=AF.Silu),
                                     reads=["ps%d" % hc], writes=["sg%d" % hc])
                                P.op("dve", lambda e, pU_=pU_, hc=hc, fi=fi, tsl=tsl: e.tensor_tensor(out=actT[:, fi, tsl], in0=pU_[:], in1=sg[hc][:], op=ALU.mult),
                                     reads=["ps%d" % (2 + hc), "sg%d" % hc], writes=["actT"])
                    P.dma("pool", wd[:], wdv[:, fb * 4:(fb + 1) * 4, :], writes=["wd"])
                    for tt in range(8):
                        for dc in range(4):
                            dsl = slice(dc * 512, (dc + 1) * 512)
                            pk = 4 + ei % 4
                            tk = ei % 2
                            ei += 1
                            pD = ps[pk]
                            for fi in range(4):
                                P.op("pe", lambda e, pD=pD, fi=fi, tt=tt, dsl=dsl: e.matmul(
                                    pD[:], lhsT=actT[:, fi, tt * 128:(tt + 1) * 128], rhs=wd[:, fi, dsl], start=(fi == 0), stop=(fi == 3)),
                                    reads=["actT", "wd"], writes=["ps%d" % pk])
                            P.op("dve", lambda e, pD=pD, tk=tk, dsl=dsl: e.tensor_tensor(out=tD[tk][:], in0=pD[:], in1=gt2[:, dsl], op=ALU.mult),
                                 reads=["ps%d" % pk, "gt2"], writes=["tD%d" % tk])
                            P.op("pool", lambda e, tk=tk, tt=tt, dsl=dsl: e.tensor_tensor(out=x1v[:, tt, dsl], in0=x1v[:, tt, dsl], in1=tD[tk][:], op=ALU.add),
                                 reads=["tD%d" % tk, "hT"], writes=["hT"])
            P.barrier()

            ph = ExitStack()
            with ph:
                gfin = sb("gfin", [128, D], F32, ph)
                ss3 = sb("ss3", [128, 8], F32, ph)
                junk3 = sb("junk3", [128, D], BF16, ph)
                ot = [sb("ot%d" % i, [128, D], F32, ph) for i in range(2)]
                P.dma("sp", gfin[:], g_final[0:1, :].partition_broadcast(128), writes=["gfin"])
                for tt in range(8):
                    bi = tt % 2
                    P.op("act", lambda e, tt=tt: e.activation(out=junk3[:], in_=x1v[:, tt, :], func=AF.Square, accum_out=ss3[:, tt:tt + 1]),
                         reads=["hT"], writes=["junk3", "ss3_%d" % tt])
                    P.op("dve", lambda e, tt=tt: e.tensor_scalar(out=ss3[:, tt:tt + 1], in0=ss3[:, tt:tt + 1], scalar1=1.0 / D, scalar2=EPS,
                                                                 op0=ALU.mult, op1=ALU.add), reads=["ss3_%d" % tt], writes=["ss3_%d" % tt])
                    P.op("act", lambda e, tt=tt: e.activation(out=ss3[:, tt:tt + 1], in_=ss3[:, tt:tt + 1], func=AF.Sqrt),
                         reads=["ss3_%d" % tt], writes=["ss3_%d" % tt])
                    P.op("dve", lambda e, tt=tt: e.reciprocal(out=ss3[:, tt:tt + 1], in_=ss3[:, tt:tt + 1]),
                         reads=["ss3_%d" % tt], writes=["ss3_%d" % tt])
                    P.op("dve", lambda e, tt=tt, bi=bi: e.scalar_tensor_tensor(out=ot[bi][:], in0=x1v[:, tt, :], scalar=ss3[:, tt:tt + 1], in1=gfin[:],
                                                                             op0=ALU.mult, op1=ALU.mult), reads=["hT", "ss3_%d" % tt, "gfin"], writes=["ot%d" % bi])
                    final_tokens.append(P.dma("sp", out[tt * 128:(tt + 1) * 128, :], ot[bi][:], reads=["ot%d" % bi]))

        if debug is not None and not full_tail:
            zt = sb("zt", [128, D], F32)
            P.op("dve", lambda e: e.memset(zt[:], 0.0), writes=["zt"])
            for tt in range(8):
                final_tokens.append(P.dma("sp", out[tt * 128:(tt + 1) * 128, :], zt[:], reads=["zt"]))

        print('ops', P.simulate(), {e: len(v) for e, v in P.eng_ops.items()})
        with nc.Block() as block:
            P.emit(block, final_tokens)
    return nc


def make_in_maps(inputs):
    x = np.ascontiguousarray(inputs["x"], dtype=np.float32)
    c = np.asarray(inputs["c"], dtype=np.float32)
    pos = np.asarray(inputs["positions"], dtype=np.int32)
    maps = []
    for core in range(8):
        b, j = core // 2, core % 2
        selv = np.zeros((128, 2), np.float32)
        selv[:, j] = 1.0
        m = {
            "xb": x[b],
            "xh": np.ascontiguousarray(x[b, j * 1024:(j + 1) * 1024]),
            "c_l": np.ascontiguousarray(c[b].reshape(KT, 128).T),
            "pos": np.ascontiguousarray(pos[b].reshape(1, S)),
            "sel": selv,
            "w_mod": np.ascontiguousarray(inputs["w_mod"][0]),
            "b_mod": np.ascontiguousarray(inputs["b_mod"][0].reshape(1, -1)),
            "g_mix": np.ascontiguousarray(inputs["g_mix"][0].reshape(1, -1)),
            "w_in": np.ascontiguousarray(inputs["w_in"][0]),
            "g_att_l": np.ascontiguousarray(inputs["g_attn_out"][0].reshape(8, 128).T),
            "a_re_l": np.ascontiguousarray(inputs["ssm_a_re"][0].reshape(32, 128).T),
            "a_im_l": np.ascontiguousarray(inputs["ssm_a_im"][0].reshape(32, 128).T),
            "ldt_l": np.ascontiguousarray(np.repeat(inputs["ssm_log_dt"][0], 64).reshape(32, 128).T),
            "BTr": np.ascontiguousarray(inputs["ssm_b_re"][0].transpose(0, 2, 1)),
            "BTi": np.ascontiguousarray(inputs["ssm_b_im"][0].transpose(0, 2, 1)),
            "CTr": np.ascontiguousarray(inputs["ssm_c_re"][0].transpose(0, 2, 1)),
            "CTi": np.ascontiguousarray(inputs["ssm_c_im"][0].transpose(0, 2, 1)),
            "d_l": np.ascontiguousarray(inputs["ssm_d"][0].reshape(8, 128).T),
            "w_glu": np.ascontiguousarray(inputs["w_glu"][0]),
            "b_glu_l": np.ascontiguousarray(inputs["b_glu"][0].reshape(8, 128).T),
            "g_ssm_l": np.ascontiguousarray(inputs["g_ssm_out"][0].reshape(8, 128).T),
            "w_out": np.ascontiguousarray(inputs["w_out"][0]),
            "g_ffn": np.ascontiguousarray(inputs["g_ffn"][0].reshape(1, -1)),
            "w_gate": np.ascontiguousarray(inputs["w_gate"][0]),
            "w_up": np.ascontiguousarray(inputs["w_up"][0]),
            "w_down": np.ascontiguousarray(inputs["w_down"][0]),
            "g_final": np.ascontiguousarray(inputs["g_final"].reshape(1, -1)),
        }
        maps.append(m)
    return maps


def kernel(**inputs):
    nc = build()
    maps = make_in_maps(inputs)
    res = run_bass_kernel_spmd(nc, maps, core_ids=list(range(8)))
    outp = np.zeros((4, S, D), np.float32)
    for core in range(8):
        b, j = core // 2, core % 2
        outp[b, j * 1024:(j + 1) * 1024] = res.results[core]["out"]
    return outp
```

```python
import math
from contextlib import ExitStack
import numpy as np
import concourse.bass as bass
import concourse.mybir as mybir
from concourse.bass_utils import run_bass_kernel_spmd

F32 = mybir.dt.float32
BF16 = mybir.dt.bfloat16
I32 = mybir.dt.int32
AF = mybir.ActivationFunctionType
ALU = mybir.AluOpType
AX = mybir.AxisListType

D = 2048
S = 2048
KT = D // 128
NMOD = 6
DFF = 5632
EPS = 1e-6
TWO_PI = 2.0 * math.pi


class Prog:
    ENGS = ("pe", "act", "dve", "pool", "sp")

    def __init__(self, nc, es):
        self.nc = nc
        self.es = es
        self.eng_ops = {e: [] for e in self.ENGS}
        self.last_write = {}
        self.readers = {}
        self.sem = {}
        self.cnt = {}
        self.nsem = 0
        for e in self.ENGS:
            self._new_eng_sem(e)
        self.known = {e: {} for e in self.ENGS}
        self.dma_pool = {}
        self.dma_val = {}
        self.dma_rr = {}
        for q, n in (("sp", 12), ("pool", 12), ("act", 6)):
            self.dma_pool[q] = [self._alloc_sem() for _ in range(n)]
            self.dma_rr[q] = 0
            for s in self.dma_pool[q]:
                self.dma_val[s] = 0
        self.pending = {e: {} for e in self.ENGS}
        self.sem_objs = {}

    def _alloc_sem(self):
        self.nsem += 1
        return self.es.enter_context(self.nc.semaphore("sm%d" % self.nsem))

    def _new_eng_sem(self, e):
        self.sem[e] = self._alloc_sem()
        self.cnt[e] = 0

    def _deps(self, eng, reads, writes):
        deps = {}

        def add(tok):
            if tok is None:
                return
            s, v = tok
            if deps.get(s, 0) < v:
                deps[s] = v
        for r in reads:
            add(self.last_write.get(r))
        for w in writes:
            add(self.last_write.get(w))
            for s, v in self.readers.get(w, {}).items():
                add((s, v))
        for s, v in self.pending[eng].items():
            add((s, v))
        self.pending[eng] = {}
        return deps

    def _finish(self, eng, deps, tok, reads, writes):
        waits = []
        for s, v in deps.items():
            if eng == "pe" and s is self.sem["pe"]:
                continue
            if self.known[eng].get(s, 0) >= v:
                continue
            self.known[eng][s] = v
            waits.append((s, v))
        for w in writes:
            self.last_write[w] = tok
            self.readers[w] = {}
        for r in reads:
            d = self.readers.setdefault(r, {})
            if d.get(tok[0], 0) < tok[1]:
                d[tok[0]] = tok[1]
        return waits

    def op(self, eng, fn, reads=(), writes=()):
        deps = self._deps(eng, reads, writes)
        if self.cnt[eng] >= 6000:
            self._new_eng_sem(eng)
        self.cnt[eng] += 1
        tok = (self.sem[eng], self.cnt[eng])
        waits = self._finish(eng, deps, tok, reads, writes)
        self.eng_ops[eng].append((waits, fn, tok[0], 1))
        return tok

    def dma(self, q, out, in_, reads=(), writes=(), **kw):
        deps = self._deps(q, reads, writes)
        pool = self.dma_pool[q]
        s = pool[self.dma_rr[q] % len(pool)]
        self.dma_rr[q] += 1
        if self.dma_val[s] > 0:
            if deps.get(s, 0) < self.dma_val[s]:
                deps[s] = self.dma_val[s]
        self.dma_val[s] += 16
        tok = (s, self.dma_val[s])
        waits = self._finish(q, deps, tok, reads, writes)
        self.eng_ops[q].append((waits, lambda e: e.dma_start(out=out, in_=in_, **kw), s, 16))
        return tok

    def barrier(self):
        toks = {}
        for e in self.ENGS:
            if self.cnt[e] > 0:
                toks[self.sem[e]] = self.cnt[e]
        for s, v in self.dma_val.items():
            if v > 0:
                toks[s] = v
        for e in self.ENGS:
            for s, v in toks.items():
                if self.pending[e].get(s, 0) < v:
                    self.pending[e][s] = v

    def simulate(self):
        val = {}
        pc = {e: 0 for e in self.ENGS}
        total = sum(len(v) for v in self.eng_ops.values())
        done = 0
        while done < total:
            prog = False
            for e in self.ENGS:
                ops = self.eng_ops[e]
                while pc[e] < len(ops):
                    waits, fn, s, inc = ops[pc[e]]
                    if all(val.get(id(ws), 0) >= wv for ws, wv in waits):
                        val[id(s)] = val.get(id(s), 0) + inc
                        pc[e] += 1
                        done += 1
                        prog = True
                    else:
                        break
            if not prog:
                st = {e: (pc[e], len(self.eng_ops[e])) for e in self.ENGS}
                raise RuntimeError("semaphore deadlock in program: %s" % st)
        return total

    def emit(self, block, final_tokens):
        nc = self.nc
        fin = {}
        for s, v in final_tokens:
            fin[s] = max(fin.get(s, 0), v)

        def run(e, name):
            for waits, fn, s, inc in self.eng_ops[name]:
                for ws, wv in waits:
                    e.wait_ge(ws, wv)
                fn(e).then_inc(s, inc)
            if name == "sp":
                for s, v in fin.items():
                    e.wait_ge(s, v)

        @block.sync
        def _(e):
            run(e, "sp")

        @block.scalar
        def _(e):
            run(e, "act")

        @block.vector
        def _(e):
            run(e, "dve")

        @block.gpsimd
        def _(e):
            run(e, "pool")

        @block.tensor
        def _(e):
            run(e, "pe")


def UTdump(P, src, UT):
    P.op("dve", lambda e: e.tensor_copy(out=UT[:, 0:1024], in_=src[:, 0:1024]), reads=["qT"], writes=["UT"])
    return UT[:, 0:1024]


def sincos(P, ang, angr, sdst, sr, cdst, cr, tf, tfr, ti, tir):
    P.op("dve", lambda e: e.tensor_scalar(out=tf[:], in0=ang[:], scalar1=1.0 / TWO_PI, scalar2=None, op0=ALU.mult),
         reads=[angr], writes=[tfr])
    P.op("dve", lambda e: e.tensor_copy(out=ti[:], in_=tf[:]), reads=[tfr], writes=[tir])
    P.op("dve", lambda e: e.tensor_copy(out=tf[:], in_=ti[:]), reads=[tir], writes=[tfr])
    P.op("dve", lambda e: e.scalar_tensor_tensor(out=sdst[:], in0=tf[:], scalar=-TWO_PI, in1=ang[:],
                                                 op0=ALU.mult, op1=ALU.add), reads=[tfr, angr], writes=[sr])

    def wrap(t, tr):
        P.op("dve", lambda e: e.tensor_scalar(out=tf[:], in0=t[:], scalar1=math.pi, scalar2=-TWO_PI,
                                              op0=ALU.is_gt, op1=ALU.mult), reads=[tr], writes=[tfr])
        P.op("dve", lambda e: e.tensor_tensor(out=t[:], in0=t[:], in1=tf[:], op=ALU.add), reads=[tr, tfr], writes=[tr])
        P.op("dve", lambda e: e.tensor_scalar(out=tf[:], in0=t[:], scalar1=-math.pi, scalar2=TWO_PI,
                                              op0=ALU.is_lt, op1=ALU.mult), reads=[tr], writes=[tfr])
        P.op("dve", lambda e: e.tensor_tensor(out=t[:], in0=t[:], in1=tf[:], op=ALU.add), reads=[tr, tfr], writes=[tr])
    wrap(sdst, sr)
    P.op("dve", lambda e: e.tensor_scalar_add(out=cdst[:], in0=sdst[:], scalar1=math.pi / 2), reads=[sr], writes=[cr])
    wrap(cdst, cr)
    for t, tr in ((sdst, sr), (cdst, cr)):
        P.op("dve", lambda e, t=t: e.tensor_scalar(out=t[:], in0=t[:], scalar1=math.pi, scalar2=-math.pi,
                                                   op0=ALU.min, op1=ALU.max), reads=[tr], writes=[tr])
        P.op("act", lambda e, t=t: e.activation(out=t[:], in_=t[:], func=AF.Sin), reads=[tr], writes=[tr])


def build(debug=None, att_level=9, ssm_full=False, full_tail=False):
    nc = bass.Bass("TRN2", target_bir_lowering=False)
    dram = {}

    def din(name, shape, dt=F32):
        dram[name] = nc.dram_tensor(name, list(shape), dt, kind="ExternalInput").ap()
        return dram[name]

    xb = din("xb", [S, D])
    xh = din("xh", [S // 2, D])
    c_l = din("c_l", [128, KT])
    pos = din("pos", [1, S], I32)
    sel = din("sel", [128, 2])
    w_mod = din("w_mod", [D, NMOD * D])
    b_mod = din("b_mod", [1, NMOD * D])
    g_mix = din("g_mix", [1, D])
    w_in = din("w_in", [D, 4096])
    out = nc.dram_tensor("out", [S // 2, D], F32, kind="ExternalOutput").ap()
    modrow = nc.dram_tensor("modrow", [1, 4 * D], F32, kind="Internal").ap()
    dbg = None
    if debug is not None:
        dbg = nc.dram_tensor("dbg", list(debug), F32, kind="ExternalOutput").ap()

    es = ExitStack()
    with es:
        P = Prog(nc, es)

        def sb(name, shape, dt=F32, st=es):
            return st.enter_context(nc.sbuf_tensor(name, list(shape), dt))

        ps = [es.enter_context(nc.psum_tensor("ps%d" % i, [128, 512], F32)) for i in range(8)]

        hT = sb("hT", [128, KT, S], BF16)
        ident = sb("ident", [128, 128], BF16)
        identf = sb("identf", [128, 128], F32)
        ones_bf = sb("ones_bf", [128, 128], BF16)
        yT = sb("yT", [128, 16, 1024], BF16)
        sqatt = sb("sqatt", [128, 1024], F32)
        sqssm = sb("sqssm", [128, 1024], F32)
        NH_RUN = 8 if debug is None else (1 if debug == (128, 1024) else 0)

        P.op("pool", lambda e: e.memset(identf[:], 1.0), writes=["identf"])
        P.op("pool", lambda e: e.affine_select(out=identf[:], in_=identf[:], pattern=[[-1, 128]],
                                               compare_op=ALU.is_equal, fill=0.0, base=0,
                                               channel_multiplier=1),
             reads=["identf"], writes=["identf"])
        P.op("dve", lambda e: e.tensor_copy(out=ident[:], in_=identf[:]), reads=["identf"], writes=["ident"])
        P.op("dve", lambda e: e.memset(ones_bf[:], 1.0), writes=["ones_bf"])

        final_tokens = []
        ph = ExitStack()
        with ph:
            cs = sb("cs", [128, KT], F32, ph)
            csr = sb("csr", [128, KT, 128], BF16, ph)
            G1 = sb("G1", [128, D], F32, ph)
            sh1 = sb("sh1", [128, D], F32, ph)
            gm = sb("gm", [128, D], F32, ph)
            wm = [sb("wm%d" % i, [128, KT, 512], BF16, ph) for i in range(2)]
            bm = [sb("bm%d" % i, [128, 512], F32, ph) for i in range(2)]
            xt = [sb("xt%d" % i, [128, D], F32, ph) for i in range(2)]
            hf = sb("hf", [128, D], F32, ph)
            hb = [sb("hb%d" % i, [128, D], BF16, ph) for i in range(2)]
            ss = sb("ss", [128, 16], F32, ph)
            rstd = sb("rstd", [128, 16], F32, ph)
            junk = sb("junk", [128, D], BF16, ph)

            P.dma("sp", cs[:], c_l[:, :], writes=["cs"])
            P.dma("sp", gm[:], g_mix[0:1, :].partition_broadcast(128), writes=["gm"])
            P.op("act", lambda e: e.activation(out=cs[:], in_=cs[:], func=AF.Silu), reads=["cs"], writes=["cs"])
            P.op("dve", lambda e: e.tensor_copy(out=csr[:], in_=cs[:].unsqueeze(2).to_broadcast([128, KT, 128])),
                 reads=["cs"], writes=["csr"])
            wmv = w_mod.rearrange("(kt p) n -> p kt n", p=128)

            def mod_chunk(j, evac):
                bi = j % 2
                P.dma("pool", wm[bi][:], wmv[:, :, j * 512:(j + 1) * 512], writes=["wm%d" % bi])
                P.dma("sp", bm[bi][:], b_mod[0:1, j * 512:(j + 1) * 512].partition_broadcast(128),
                      writes=["bm%d" % bi])
                pb = ps[bi]
                for kt in range(KT):
                    P.op("pe", lambda e, kt=kt: e.matmul(pb[:], lhsT=csr[:, kt, :], rhs=wm[bi][:, kt, :],
                                                         start=(kt == 0), stop=(kt == KT - 1)),
                         reads=["csr", "wm%d" % bi], writes=["ps%d" % bi])
                evac(j, pb, bm[bi], "ps%d" % bi, "bm%d" % bi)

            def evac1(j, pb, bmt, pr, br):
                if j < 4:
                    dst = sh1[:, j * 512:(j + 1) * 512]
                    P.op("dve", lambda e: e.tensor_tensor(out=dst, in0=pb[:], in1=bmt[:], op=ALU.add),
                         reads=[pr, br], writes=["sh1"])
                else:
                    jj = j - 4
                    dst = G1[:, jj * 512:(jj + 1) * 512]
                    P.op("dve", lambda e: e.tensor_tensor(out=dst, in0=pb[:], in1=bmt[:], op=ALU.add),
                         reads=[pr, br], writes=["G1"])
                    P.op("dve", lambda e: e.scalar_tensor_tensor(out=dst, in0=dst, scalar=1.0,
                                                                 in1=gm[:, jj * 512:(jj + 1) * 512],
                                                                 op0=ALU.add, op1=ALU.mult),
                         reads=["G1", "gm"], writes=["G1"])
            for j in range(8):
                mod_chunk(j, evac1)

            for tt in range(16):
                bi = tt % 2
                xtb = xt[bi]
                P.dma("sp", xtb[:], xb[tt * 128:(tt + 1) * 128, :], writes=["xt%d" % bi])
                P.op("act", lambda e, tt=tt, xtb=xtb: e.activation(out=junk[:], in_=xtb[:], func=AF.Square,
                                                                    accum_out=ss[:, tt:tt + 1]),
                     reads=["xt%d" % bi], writes=["junk", "ss%d" % tt])
                P.op("dve", lambda e, tt=tt: e.tensor_scalar(out=rstd[:, tt:tt + 1], in0=ss[:, tt:tt + 1],
                                                             scalar1=1.0 / D, scalar2=EPS,
                                                             op0=ALU.mult, op1=ALU.add),
                     reads=["ss%d" % tt], writes=["rstd%d" % tt])
                P.op("act", lambda e, tt=tt: e.activation(out=rstd[:, tt:tt + 1], in_=rstd[:, tt:tt + 1], func=AF.Sqrt),
                     reads=["rstd%d" % tt], writes=["rstd%d" % tt])
                P.op("dve", lambda e, tt=tt: e.reciprocal(out=rstd[:, tt:tt + 1], in_=rstd[:, tt:tt + 1]),
                     reads=["rstd%d" % tt], writes=["rstd%d" % tt])
                P.op("dve", lambda e, tt=tt, xtb=xtb: e.scalar_tensor_tensor(
                    out=hf[:], in0=xtb[:], scalar=rstd[:, tt:tt + 1], in1=G1[:], op0=ALU.mult, op1=ALU.mult),
                    reads=["xt%d" % bi, "rstd%d" % tt, "G1"], writes=["hf"])
                hbb = hb[bi]
                P.op("dve", lambda e, hbb=hbb: e.tensor_tensor(out=hbb[:], in0=hf[:], in1=sh1[:], op=ALU.add),
                     reads=["hf", "sh1"], writes=["hb%d" % bi])
                for g in range(2):
                    pi = 2 + (tt * 2 + g) % 2
                    pT = ps[pi][:].bitcast(BF16)
                    for k8 in range(8):
                        kt = g * 8 + k8
                        P.op("pe", lambda e, kt=kt, k8=k8, pT=pT, hbb=hbb: e.transpose(
                            out=pT[:, k8 * 128:(k8 + 1) * 128], in_=hbb[:, kt * 128:(kt + 1) * 128],
                            identity=ident[:]),
                            reads=["hb%d" % bi, "ident"], writes=["ps%d" % pi])
                    P.op("act", lambda e, g=g, tt=tt, pT=pT: e.activation(
                        out=hT[:, g * 8:(g + 1) * 8, tt * 128:(tt + 1) * 128],
                        in_=pT.rearrange("p (k t) -> p k t", k=8), func=AF.Copy),
                        reads=["ps%d" % pi], writes=["hT"])
            if debug is not None and debug == (128, KT * S):
                dtile = sb("dtile", [128, S], F32, ph)
                for kt in range(KT):
                    P.op("dve", lambda e, kt=kt: e.tensor_copy(out=dtile[:], in_=hT[:, kt, :]),
                         reads=["hT"], writes=["dtile"])
                    final_tokens.append(P.dma("sp", dbg[:, kt * S:(kt + 1) * S], dtile[:], reads=["dtile"]))
        P.barrier()


        g_att_l = din("g_att_l", [128, 8])
        ph = ExitStack()
        with ph:
            cosT = sb("cosT", [128, S], F32, ph)
            sinT = sb("sinT", [128, S], F32, ph)
            pidx = sb("pidx", [128, 1], F32, ph)
            invf = sb("invf", [128, 1], F32, ph)
            nsgn = sb("nsgn", [128, 1], F32, ph)
            neg1 = sb("neg1", [128, 1], F32, ph)
            selt = sb("selt", [128, 2], F32, ph)
            gat = sb("gat", [128, 8], F32, ph)
            permf = sb("permf", [128, 128], F32, ph)
            maskf = sb("maskf", [128, 256], F32, ph)
            maskb = sb("maskb", [128, 256], BF16, ph)
            ph0 = ExitStack()
            posi = sb("posi", [128, S], I32, ph0)
            angf = sb("angf", [128, S], F32, ph0)
            P.dma("sp", posi[:], pos[0:1, :].partition_broadcast(128), writes=["posi"])
            P.dma("sp", selt[:], sel[:, :], writes=["selt"])
            P.dma("sp", gat[:], g_att_l[:, :], writes=["gat"])
            P.op("pool", lambda e: e.iota(pidx[:], pattern=[[0, 1]], base=0, channel_multiplier=1,
                                          allow_small_or_imprecise_dtypes=True), writes=["pidx"])
            P.op("dve", lambda e: e.tensor_scalar(out=invf[:], in0=pidx[:], scalar1=64.0, scalar2=-64.0,
                                                  op0=ALU.is_ge, op1=ALU.mult), reads=["pidx"], writes=["invf"])
            P.op("dve", lambda e: e.tensor_tensor(out=invf[:], in0=invf[:], in1=pidx[:], op=ALU.add),
                 reads=["invf", "pidx"], writes=["invf"])
            P.op("act", lambda e: e.activation(out=invf[:], in_=invf[:], func=AF.Exp, scale=-math.log(10000.0) / 64.0),
                 reads=["invf"], writes=["invf"])
            P.op("dve", lambda e: e.tensor_scalar(out=nsgn[:], in0=pidx[:], scalar1=64.0, scalar2=2.0,
                                                  op0=ALU.is_ge, op1=ALU.mult), reads=["pidx"], writes=["nsgn"])
            P.op("dve", lambda e: e.tensor_scalar_add(out=nsgn[:], in0=nsgn[:], scalar1=-1.0),
                 reads=["nsgn"], writes=["nsgn"])
            P.op("dve", lambda e: e.tensor_copy(out=cosT[:], in_=posi[:]), reads=["posi"], writes=["cosT"])
            P.op("dve", lambda e: e.tensor_scalar(out=cosT[:], in0=cosT[:], scalar1=invf[:, 0:1], scalar2=None,
                                                  op0=ALU.mult), reads=["cosT", "invf"], writes=["cosT"])
            sincos(P, cosT, "cosT", sinT, "sinT", cosT, "cosT", angf, "angf", posi, "posi")
            P.op("dve", lambda e: e.tensor_scalar(out=sinT[:], in0=sinT[:], scalar1=nsgn[:, 0:1], scalar2=None,
                                                  op0=ALU.mult), reads=["sinT", "nsgn"], writes=["sinT"])
            P.op("dve", lambda e: e.tensor_copy(out=permf[:, 0:64], in_=identf[:, 64:128]), reads=["identf"], writes=["permf"])
            P.op("dve", lambda e: e.tensor_copy(out=permf[:, 64:128], in_=identf[:, 0:64]), reads=["identf"], writes=["permf"])
            P.op("pool", lambda e: e.memset(maskf[:], 1.0), writes=["maskf"])
            P.op("pool", lambda e: e.affine_select(out=maskf[:, 0:128], in_=maskf[:, 0:128], pattern=[[1, 128]],
                                                   compare_op=ALU.is_ge, fill=0.0, base=0, channel_multiplier=-1),
                 reads=["maskf"], writes=["maskf"])
            P.op("pool", lambda e: e.affine_select(out=maskf[:, 128:256], in_=maskf[:, 128:256], pattern=[[-1, 128]],
                                                   compare_op=ALU.is_ge, fill=0.0, base=0, channel_multiplier=1),
                 reads=["maskf"], writes=["maskf"])
            P.op("dve", lambda e: e.tensor_copy(out=maskb[:], in_=maskf[:]), reads=["maskf"], writes=["maskb"])

            if debug is not None and debug == (128, 2048):
                final_tokens.append(P.dma("sp", dbg[:, :], sinT[:], reads=["sinT"]))
            ph0.close()
            P.barrier()
            wq = [sb("wq%d" % i, [128, KT, 3, 128], BF16, ph) for i in range(2)]
            qf = sb("qf", [128, 512], F32, ph)
            t1 = sb("t1", [128, 512], F32, ph)
            t2 = sb("t2", [128, 512], F32, ph)
            qT = sb("qT", [128, S], BF16, ph)
            kT = sb("kT", [128, S], BF16, ph)
            vT = sb("vT", [128, S], BF16, ph)
            Vd = sb("Vd", [128, 48, 128], BF16, ph)
            qd = sb("qd", [128, S], BF16, ph)
            kd = sb("kd", [128, S], BF16, ph)
            print("sbuf remaining (att)", nc.sbuf_bytes_remaining)
            Pm = [sb("Pm%d" % i, [128, 256], BF16, ph) for i in range(2)]
            UT = sb("UT", [128, S], F32, ph)
            LT = sb("LT", [128, S], F32, ph)

            wiv = w_in.rearrange("(kt p) n -> p kt n", p=128)
            scale = 1.0 / math.sqrt(128.0)
            cs2m = sb("cs2m", [128, KT], F32, ph)
            cs2b = sb("cs2b", [128, KT], BF16, ph)
            wm2m = sb("wm2m", [128, KT, 128], BF16, ph)
            bm2m = sb("bm2m", [1, 128], F32, ph)
            rowm = sb("rowm", [1, 128], F32, ph)
            print("sbuf remaining (att2)", nc.sbuf_bytes_remaining)
            P.dma("sp", cs2m[:], c_l[:, :], writes=["cs2m"])
            P.op("act", lambda e: e.activation(out=cs2b[:], in_=cs2m[:], func=AF.Silu), reads=["cs2m"], writes=["cs2b"])
            wmv2 = w_mod.rearrange("(kt p) n -> p kt n", p=128)
            MODN = [0]

            def mod_prefetch(idx):
                c0 = 4096 + idx * 128
                P.dma("pool", wm2m[:], wmv2[:, :, c0:c0 + 128], writes=["wm2m"])
                P.dma("sp", bm2m[:], b_mod[0:1, c0:c0 + 128], writes=["bm2m"])

            def mod_step():
                if NH_RUN != 8 or MODN[0] >= 64:
                    return
                idx = MODN[0]
                MODN[0] += 1
                c0 = idx * 128
                for kt in range(KT):
                    P.op("pe", lambda e, kt=kt: e.matmul(ps[3][0:1, 0:128], lhsT=cs2b[:, kt:kt + 1], rhs=wm2m[:, kt, :],
                                                         start=(kt == 0), stop=(kt == KT - 1)),
                         reads=["cs2b", "wm2m"], writes=["ps3"])
                P.op("dve", lambda e: e.tensor_tensor(out=rowm[:], in0=ps[3][0:1, 0:128], in1=bm2m[:], op=ALU.add),
                     reads=["ps3", "bm2m"], writes=["rowm"])
                P.dma("sp", modrow[0:1, c0:c0 + 128], rowm[0:1, :], reads=["rowm"], writes=["modrow"])
                if idx + 1 < 64:
                    mod_prefetch(idx + 1)
            if NH_RUN == 8:
                mod_prefetch(0)
            for h in range(NH_RUN):
                wb = wq[h % 2]
                wr = "wq%d" % (h % 2)
                for i3 in range(3):
                    c0 = 1024 * (i3 + 1) + h * 128
                    P.dma("pool", wb[:, :, i3, :], wiv[:, :, c0:c0 + 128], writes=[wr])
                for i3, dstT in ((0, qT), (1, kT), (2, vT)):
                    dn = ("qT", "kT", "vT")[i3]
                    for tc in range(4):
                        if tc % 2 == 0:
                            mod_step()
                        pb = ps[tc % 2]
                        pr = "ps%d" % (tc % 2)
                        for kt in range(KT):
                            P.op("pe", lambda e, kt=kt, pb=pb, i3=i3, tc=tc, wb=wb: e.matmul(
                                pb[:], lhsT=wb[:, kt, i3, :], rhs=hT[:, kt, tc * 512:(tc + 1) * 512],
                                start=(kt == 0), stop=(kt == KT - 1)), reads=[wr, "hT"], writes=[pr])
                        if i3 == 2:
                            P.op("act", lambda e, pb=pb, tc=tc: e.activation(out=vT[:, tc * 512:(tc + 1) * 512], in_=pb[:],
                                                                            func=AF.Copy), reads=[pr], writes=["vT"])
                        else:
                            P.op("act", lambda e, pb=pb: e.activation(out=qf[:], in_=pb[:], func=AF.Copy),
                                 reads=[pr], writes=["qf"])
                            p2 = ps[2 + tc % 2]
                            p2r = "ps%d" % (2 + tc % 2)
                            P.op("pe", lambda e, p2=p2: e.matmul(p2[:], lhsT=permf[:], rhs=qf[:], start=True, stop=True),
                                 reads=["permf", "qf"], writes=[p2r])
                            P.op("dve", lambda e, tc=tc: e.tensor_tensor(out=t1[:], in0=qf[:], in1=cosT[:, tc * 512:(tc + 1) * 512],
                                                                        op=ALU.mult), reads=["qf", "cosT"], writes=["t1"])
                            P.op("dve", lambda e, tc=tc, p2=p2: e.tensor_tensor(out=t2[:], in0=p2[:], in1=sinT[:, tc * 512:(tc + 1) * 512],
                                                                               op=ALU.mult), reads=[p2r, "sinT"], writes=["t2"])
                            P.op("dve", lambda e, tc=tc, dstT=dstT: e.tensor_tensor(out=dstT[:, tc * 512:(tc + 1) * 512], in0=t1[:], in1=t2[:],
                                                                                   op=ALU.add), reads=["t1", "t2"], writes=[dn])
                if att_level < 2:
                    final_tokens.append(P.dma("sp", dbg[:, :], t1[:].bitcast(F32)[:, 0:512] if False else UTdump(P, qT, UT), reads=["UT"]))
                    break
                mod_step()
                tiles = []
                for d in (1, 4, 16):
                    nb = 16 // d
                    for r in range(d):
                        for kb in range(nb):
                            tiles.append((d, r, kb))
                for g in range(6):
                    pi = 2 + g % 2
                    pT = ps[pi][:].bitcast(BF16)
                    for j8 in range(8):
                        d, r, kb = tiles[g * 8 + j8]
                        st = r + d * 128 * kb
                        P.op("pe", lambda e, pT=pT, j8=j8, st=st, d=d: e.transpose(
                            out=pT[:, j8 * 128:(j8 + 1) * 128], in_=vT[:, st:st + 127 * d + 1:d], identity=ident[:]),
                            reads=["vT", "ident"], writes=["ps%d" % pi])
                    P.op("act", lambda e, g=g, pT=pT: e.activation(out=Vd[:, g * 8:(g + 1) * 8, :],
                                                                 in_=pT.rearrange("p (k t) -> p k t", k=8), func=AF.Copy),
                         reads=["ps%d" % pi], writes=["Vd"])
                if att_level < 3:
                    break
                mod_step()
                ti = 0
                sc_i = 0
                for d in (1, 4, 16):
                    nb = 16 // d
                    if d == 1:
                        qsrc, ksrc, qn, kn = qT, kT, "qT", "kT"
                    else:
                        P.op("act", lambda e, d=d: e.activation(out=qd[:].rearrange("p (r j) -> p r j", r=d),
                                                               in_=qT[:].rearrange("p (j r) -> p r j", r=d), func=AF.Copy),
                             reads=["qT"], writes=["qd"])
                        P.op("pool", lambda e, d=d: e.tensor_copy(out=kd[:].rearrange("p (r j) -> p r j", r=d),
                                                                 in_=kT[:].rearrange("p (j r) -> p r j", r=d)),
                             reads=["kT"], writes=["kd"])
                        qsrc, ksrc, qn, kn = qd, kd, "qd", "kd"
                    for r in range(d):
                        for kb in range(nb):
                            nq = 256 if kb + 1 < nb else 128
                            st = r + d * 128 * kb
                            c0 = r * (S // d) + 128 * kb
                            psS = ps[sc_i % 2]
                            psr = "ps%d" % (sc_i % 2)
                            pmb = Pm[sc_i % 2]
                            pmr = "Pm%d" % (sc_i % 2)
                            sc_i += 1
                            P.op("pe", lambda e, psS=psS, c0=c0, nq=nq, qsrc=qsrc, ksrc=ksrc: e.matmul(
                                psS[:, 0:nq], lhsT=ksrc[:, c0:c0 + 128], rhs=qsrc[:, c0:c0 + nq],
                                start=True, stop=True), reads=[kn, qn], writes=[psr])
                            P.op("act", lambda e, psS=psS, pmb=pmb, nq=nq: e.activation(out=pmb[:, 0:nq], in_=psS[:, 0:nq], func=AF.Exp,
                                                                                       scale=scale), reads=[psr], writes=[pmr])
                            P.op("dve", lambda e, pmb=pmb, nq=nq: e.tensor_tensor(out=pmb[:, 0:nq], in0=pmb[:, 0:nq], in1=maskb[:, 0:nq],
                                                                                 op=ALU.mult), reads=[pmr, "maskb"], writes=[pmr])
                            for part in range(nq // 128):
                                qb = kb + part
                                pU = ps[4 + qb % 2]
                                pur = "ps%d" % (4 + qb % 2)
                                pL = ps[6 + qb % 2]
                                plr = "ps%d" % (6 + qb % 2)
                                first = (part == 1) or (kb == 0)
                                last = (part == 0)
                                vt = Vd[:, ti, :]
                                P.op("pe", lambda e, pU=pU, vt=vt, pmb=pmb, part=part, first=first, last=last: e.matmul(
                                    pU[:, 0:128], lhsT=vt, rhs=pmb[:, part * 128:(part + 1) * 128], start=first, stop=last),
                                    reads=["Vd", pmr], writes=[pur])
                                P.op("pe", lambda e, pL=pL, pmb=pmb, part=part, first=first, last=last: e.matmul(
                                    pL[:, 0:128], lhsT=ones_bf[:], rhs=pmb[:, part * 128:(part + 1) * 128], start=first, stop=last),
                                    reads=["ones_bf", pmr], writes=[plr])
                                if last:
                                    qs = r + d * 128 * qb
                                    udst = UT[:, qs:qs + 127 * d + 1:d]
                                    ldst = LT[:, qs:qs + 127 * d + 1:d]
                                    if d == 1:
                                        P.op("dve", lambda e, pU=pU, udst=udst: e.tensor_copy(out=udst, in_=pU[:, 0:128]),
                                             reads=[pur], writes=["UT"])
                                        P.op("dve", lambda e, pL=pL, ldst=ldst: e.tensor_copy(out=ldst, in_=pL[:, 0:128]),
                                             reads=[plr], writes=["LT"])
                                    else:
                                        P.op("dve", lambda e, pU=pU, udst=udst: e.tensor_tensor(out=udst, in0=pU[:, 0:128], in1=udst, op=ALU.add),
                                             reads=[pur, "UT"], writes=["UT"])
                                        P.op("dve", lambda e, pL=pL, ldst=ldst: e.tensor_tensor(out=ldst, in0=pL[:, 0:128], in1=ldst, op=ALU.add),
                                             reads=[plr, "LT"], writes=["LT"])
                            ti += 1
                P.op("dve", lambda e: e.reciprocal(out=LT[:], in_=LT[:]), reads=["LT"], writes=["LT"])
                P.op("dve", lambda e: e.tensor_tensor(out=UT[:], in0=UT[:], in1=LT[:], op=ALU.mult), reads=["UT", "LT"], writes=["UT"])
                P.op("dve", lambda e: e.tensor_scalar(out=LT[:, 0:1024], in0=UT[:, 1024:2048], scalar1=selt[:, 1:2], scalar2=None, op0=ALU.mult),
                     reads=["UT", "selt"], writes=["LT"])
                P.op("dve", lambda e: e.scalar_tensor_tensor(out=LT[:, 1024:2048], in0=UT[:, 0:1024], scalar=selt[:, 0:1], in1=LT[:, 0:1024],
                                                             op0=ALU.mult, op1=ALU.add), reads=["UT", "selt", "LT"], writes=["LT"])
                P.op("act", lambda e: e.activation(out=LT[:, 0:1024], in_=LT[:, 1024:2048], func=AF.Square), reads=["LT"], writes=["LT"])
                if h == 0:
                    P.op("pool", lambda e: e.tensor_copy(out=sqatt[:], in_=LT[:, 0:1024]), reads=["LT"], writes=["sqatt"])
                else:
                    P.op("pool", lambda e: e.tensor_tensor(out=sqatt[:], in0=sqatt[:], in1=LT[:, 0:1024], op=ALU.add),
                         reads=["LT", "sqatt"], writes=["sqatt"])
                P.op("act", lambda e, h=h: e.activation(out=yT[:, 8 + h, :], in_=LT[:, 1024:2048], func=AF.Identity, scale=gat[:, h:h + 1]),
                     reads=["LT", "gat"], writes=["yT"])
                if debug is not None and debug == (128, 1024) and h == 0:
                    final_tokens.append(P.dma("sp", dbg[:, :], LT[:, 1024:2048], reads=["LT"]))
        P.barrier()


        a_re_l = din("a_re_l", [128, 32])
        a_im_l = din("a_im_l", [128, 32])
        ldt_l = din("ldt_l", [128, 32])
        BTr = din("BTr", [64, 16, 64])
        BTi = din("BTi", [64, 16, 64])
        CTr = din("CTr", [64, 64, 16])
        CTi = din("CTi", [64, 64, 16])
        d_l = din("d_l", [128, 8])
        w_glu = din("w_glu", [1024, 1024])
        b_glu_l = din("b_glu_l", [128, 8])
        g_ssm_l = din("g_ssm_l", [128, 8])
        NCT = 8 if (debug is None or ssm_full) else 1
        ph = ExitStack()
        with ph:
            are = sb("are", [128, 32], F32, ph)
            aim = sb("aim", [128, 32], F32, ph)
            dtt = sb("dtt", [128, 32], F32, ph)
            th = sb("th", [128, 32], F32, ph)
            mag = sb("mag", [128, 32], F32, ph)
            cth = sb("cth", [128, 32], F32, ph)
            sth = sb("sth", [128, 32], F32, ph)
            fre = sb("fre", [128, 32], F32, ph)
            fim = sb("fim", [128, 32], F32, ph)
            sm1 = sb("sm1", [128, 32], F32, ph)
            sm2 = sb("sm2", [128, 32], F32, ph)
            smi = sb("smi", [128, 32], I32, ph)
            dcol = sb("dcol", [128, 8], F32, ph)
            bgl = sb("bgl", [128, 8], F32, ph)
            gsl = sb("gsl", [128, 8], F32, ph)
            selt2 = sb("selt2", [128, 2], F32, ph)
            rowmask = sb("rowmask", [128, 4], F32, ph)
            iotaN = sb("iotaN", [128, 32], F32, ph)
            iotaD = sb("iotaD", [128, 64], F32, ph)
            cN = sb("cN", [128, 32, 32], F32, ph)
            sN = sb("sN", [128, 32, 32], F32, ph)
            cD = sb("cD", [128, 32, 64], F32, ph)
            sD = sb("sD", [128, 32, 64], F32, ph)
            Bc = [sb("Bc%d" % i, [128, 8, 128], BF16, ph) for i in range(2)]
            Cc = [sb("Cc%d" % i, [128, 32, 32], BF16, ph) for i in range(2)]
            Clp = [[sb("Clp%d_%d" % (ri, q), [128, 128], BF16, ph) for q in range(4)] for ri in range(2)]
            Blp = [[sb("Blp%d_%d" % (ri, k), [128, 128], BF16, ph) for k in range(2)] for ri in range(2)]
            vgT = sb("vgT", [128, 8, 1024], BF16, ph)
            ysel = sb("ysel", [128, 1024], F32, ph)

            P.dma("sp", are[:], a_re_l[:, :], writes=["are"])
            P.dma("sp", aim[:], a_im_l[:, :], writes=["aim"])
            P.dma("sp", dtt[:], ldt_l[:, :], writes=["dtt"])
            P.dma("sp", dcol[:], d_l[:, :], writes=["dcol"])
            P.dma("sp", bgl[:], b_glu_l[:, :], writes=["bgl"])
            P.dma("sp", gsl[:], g_ssm_l[:, :], writes=["gsl"])
            P.dma("sp", selt2[:], sel[:, :], writes=["selt2"])
            P.op("act", lambda e: e.activation(out=dtt[:], in_=dtt[:], func=AF.Exp), reads=["dtt"], writes=["dtt"])
            P.op("dve", lambda e: e.tensor_tensor(out=th[:], in0=aim[:], in1=dtt[:], op=ALU.mult), reads=["aim", "dtt"], writes=["th"])
            P.op("dve", lambda e: e.tensor_tensor(out=mag[:], in0=are[:], in1=dtt[:], op=ALU.mult), reads=["are", "dtt"], writes=["mag"])
            P.op("act", lambda e: e.activation(out=mag[:], in_=mag[:], func=AF.Exp), reads=["mag"], writes=["mag"])
            P.op("dve", lambda e: e.tensor_copy(out=cth[:], in_=th[:]), reads=["th"], writes=["cth"])
            sincos(P, cth, "cth", sth, "sth", cth, "cth", sm1, "sm1", smi, "smi")
            P.op("dve", lambda e: e.tensor_tensor(out=cth[:], in0=cth[:], in1=mag[:], op=ALU.mult), reads=["cth", "mag"], writes=["cth"])
            P.op("dve", lambda e: e.tensor_scalar_add(out=cth[:], in0=cth[:], scalar1=-1.0), reads=["cth"], writes=["cth"])
            P.op("dve", lambda e: e.tensor_tensor(out=sth[:], in0=sth[:], in1=mag[:], op=ALU.mult), reads=["sth", "mag"], writes=["sth"])
            P.op("dve", lambda e: e.tensor_tensor(out=sm1[:], in0=are[:], in1=are[:], op=ALU.mult), reads=["are"], writes=["sm1"])
            P.op("dve", lambda e: e.tensor_tensor(out=sm2[:], in0=aim[:], in1=aim[:], op=ALU.mult), reads=["aim"], writes=["sm2"])
            P.op("dve", lambda e: e.tensor_tensor(out=sm1[:], in0=sm1[:], in1=sm2[:], op=ALU.add), reads=["sm1", "sm2"], writes=["sm1"])
            P.op("dve", lambda e: e.reciprocal(out=sm1[:], in_=sm1[:]), reads=["sm1"], writes=["sm1"])
            P.op("dve", lambda e: e.tensor_tensor(out=fre[:], in0=cth[:], in1=are[:], op=ALU.mult), reads=["cth", "are"], writes=["fre"])
            P.op("dve", lambda e: e.tensor_tensor(out=sm2[:], in0=sth[:], in1=aim[:], op=ALU.mult), reads=["sth", "aim"], writes=["sm2"])
            P.op("dve", lambda e: e.tensor_tensor(out=fre[:], in0=fre[:], in1=sm2[:], op=ALU.add), reads=["fre", "sm2"], writes=["fre"])
            P.op("dve", lambda e: e.tensor_tensor(out=fre[:], in0=fre[:], in1=sm1[:], op=ALU.mult), reads=["fre", "sm1"], writes=["fre"])
            P.op("dve", lambda e: e.tensor_tensor(out=fim[:], in0=sth[:], in1=are[:], op=ALU.mult), reads=["sth", "are"], writes=["fim"])
            P.op("dve", lambda e: e.tensor_tensor(out=sm2[:], in0=cth[:], in1=aim[:], op=ALU.mult), reads=["cth", "aim"], writes=["sm2"])
            P.op("dve", lambda e: e.tensor_tensor(out=fim[:], in0=fim[:], in1=sm2[:], op=ALU.subtract), reads=["fim", "sm2"], writes=["fim"])
            P.op("dve", lambda e: e.tensor_tensor(out=fim[:], in0=fim[:], in1=sm1[:], op=ALU.mult), reads=["fim", "sm1"], writes=["fim"])
            P.op("pool", lambda e: e.iota(iotaN[:], pattern=[[64, 32]], base=0, channel_multiplier=0,
                                          allow_small_or_imprecise_dtypes=True), writes=["iotaN"])
            P.op("pool", lambda e: e.iota(iotaD[:], pattern=[[1, 64]], base=0, channel_multiplier=0,
                                          allow_small_or_imprecise_dtypes=True), writes=["iotaD"])
            P.op("pool", lambda e: e.memset(rowmask[:], 1.0), writes=["rowmask"])
            P.op("pool", lambda e: e.affine_select(out=rowmask[:], in_=rowmask[:], pattern=[[-32, 4]], compare_op=ALU.is_ge,
                                                   fill=0.0, base=0, channel_multiplier=1), reads=["rowmask"], writes=["rowmask"])
            P.op("pool", lambda e: e.affine_select(out=rowmask[:], in_=rowmask[:], pattern=[[32, 4]], compare_op=ALU.is_ge,
                                                   fill=0.0, base=31, channel_multiplier=-1), reads=["rowmask"], writes=["rowmask"])
            phs = ExitStack()
            tbf = sb("tbf", [128, 32 * 64], F32, phs)
            tbi = sb("tbi", [128, 32 * 64], I32, phs)
            Cf = [sb("Cf%d" % i, [128, 32, 32], F32, phs) for i in range(2)]
            Ct = [sb("Ct%d" % i, [128, 32, 32], F32, phs) for i in range(2)]
            cNf = cN[:].rearrange("p a b -> p (a b)")
            sNf = sN[:].rearrange("p a b -> p (a b)")
            cDf = cD[:].rearrange("p a b -> p (a b)")
            sDf = sD[:].rearrange("p a b -> p (a b)")
            P.op("dve", lambda e: e.tensor_tensor(out=cN[:], in0=th[:].unsqueeze(2).to_broadcast([128, 32, 32]),
                                                  in1=iotaN[:].unsqueeze(1).to_broadcast([128, 32, 32]), op=ALU.mult),
                 reads=["th", "iotaN"], writes=["cN"])
            sincos(P, cNf, "cN", sNf, "sN", cNf, "cN", tbf[:, 0:1024], "tbf", tbi[:, 0:1024], "tbi")
            P.op("dve", lambda e: e.tensor_tensor(out=cD[:], in0=th[:].unsqueeze(2).to_broadcast([128, 32, 64]),
                                                  in1=iotaD[:].unsqueeze(1).to_broadcast([128, 32, 64]), op=ALU.mult),
                 reads=["th", "iotaD"], writes=["cD"])
            sincos(P, cDf, "cD", sDf, "sD", cDf, "cD", tbf, "tbf", tbi, "tbi")
            for ri, CT in ((0, CTr), (1, CTi)):
                P.op("pool", lambda e, ri=ri: e.memset(Cf[ri][:], 0.0), writes=["Cf%d" % ri])
                for g2 in range(2):
                    src = CT.rearrange("(i r) p c -> r p i c", r=2)[g2]
                    P.dma("sp", Cf[ri][64 * g2:64 * g2 + 64, :, 16 * g2:16 * g2 + 16], src, writes=["Cf%d" % ri])
            fre_b = fre[:].unsqueeze(2).to_broadcast([128, 32, 32])
            fim_b = fim[:].unsqueeze(2).to_broadcast([128, 32, 32])
            P.op("dve", lambda e: e.tensor_tensor(out=Ct[0][:], in0=Cf[0][:], in1=fre_b, op=ALU.mult), reads=["Cf0", "fre"], writes=["Ct0"])
            P.op("dve", lambda e: e.tensor_tensor(out=Ct[1][:], in0=Cf[1][:], in1=fim_b, op=ALU.mult), reads=["Cf1", "fim"], writes=["Ct1"])
            P.op("dve", lambda e: e.tensor_tensor(out=Cc[0][:], in0=Ct[0][:], in1=Ct[1][:], op=ALU.subtract), reads=["Ct0", "Ct1"], writes=["Cc0"])
            P.op("dve", lambda e: e.tensor_tensor(out=Ct[0][:], in0=Cf[0][:], in1=fim_b, op=ALU.mult), reads=["Cf0", "fim"], writes=["Ct0"])
            P.op("dve", lambda e: e.tensor_tensor(out=Ct[1][:], in0=Cf[1][:], in1=fre_b, op=ALU.mult), reads=["Cf1", "fre"], writes=["Ct1"])
            P.op("dve", lambda e: e.tensor_tensor(out=Ct[0][:], in0=Ct[0][:], in1=Ct[1][:], op=ALU.add), reads=["Ct0", "Ct1"], writes=["Ct0"])
            P.op("dve", lambda e: e.tensor_scalar(out=Cc[1][:], in0=Ct[0][:], scalar1=-1.0, scalar2=None, op0=ALU.mult),
                 reads=["Ct0"], writes=["Cc1"])
            for ri, BT in ((0, BTr), (1, BTi)):
                P.op("pool", lambda e, ri=ri: e.memset(Bc[ri][:], 0.0), writes=["Bc%d" % ri])
                for q in range(4):
                    for g2 in range(2):
                        src = BT.rearrange("(Q r) c p -> r c Q p", r=8)[2 * q + g2]
                        P.dma("pool", Bc[ri][32 * q + 16 * g2:32 * q + 16 * g2 + 16, :, 64 * g2:64 * g2 + 64], src,
                              writes=["Bc%d" % ri])
                for q in range(4):
                    P.op("pool", lambda e, ri=ri, q=q: e.memset(Clp[ri][q][:], 0.0), writes=["Clp%d_%d" % (ri, q)])
            phs.close()
            P.barrier()

            phl = ExitStack()
            wu = [sb("wu%d" % i, [128, KT, 128], BF16, phl) for i in range(1)]
            uAB = sb("uAB", [128, S], BF16, phl)
            uT = [sb("uT%d" % i, [128, S], BF16, phl) for i in range(1)]
            tabC = [sb("tabC%d" % i, [128, 512], F32, phl) for i in range(2)]
            tabS = [sb("tabS%d" % i, [128, 512], F32, phl) for i in range(2)]
            ttmp = sb("ttmp", [128, 512], F32, phl)
            Wre = [sb("Wre%d" % i, [128, 512], F32, phl) for i in range(2)]
            Wim = [sb("Wim%d" % i, [128, 512], F32, phl) for i in range(2)]
            sre = [sb("sre%d" % i, [128, 512], BF16, phl) for i in range(2)]
            sim = [sb("sim%d" % i, [128, 512], BF16, phl) for i in range(2)]
            r1 = [sb("r1_%d" % i, [128, 512], F32, phl) for i in range(5)]
            print("sbuf remaining (ssm)", nc.sbuf_bytes_remaining)
            wiv = w_in.rearrange("(kt p) n -> p kt n", p=128)
            for ct in range(NCT):
                wub = wu[0]
                wur = "wu0"
                uTb = uT[0]
                uTr = "uT0"
                P.dma("pool", wub[:], wiv[:, :, ct * 128:(ct + 1) * 128], writes=[wur])
                for tc in range(4):
                    pb = ps[tc % 2]
                    pr = "ps%d" % (tc % 2)
                    for kt in range(KT):
                        P.op("pe", lambda e, kt=kt, pb=pb, tc=tc, wub=wub: e.matmul(
                            pb[:], lhsT=wub[:, kt, :], rhs=hT[:, kt, tc * 512:(tc + 1) * 512],
                            start=(kt == 0), stop=(kt == KT - 1)), reads=[wur, "hT"], writes=[pr])
                    P.op("act", lambda e, pb=pb, tc=tc, uTb=uTb: e.activation(out=uTb[:, tc * 512:(tc + 1) * 512], in_=pb[:], func=AF.Copy),
                         reads=[pr], writes=[uTr])
                P.op("dve", lambda e, uTb=uTb: e.tensor_scalar(out=uAB[:, 0:1024], in0=uTb[:, 0:1024], scalar1=selt2[:, 1:2], scalar2=None, op0=ALU.mult),
                     reads=[uTr, "selt2"], writes=["uAB"])
                P.op("dve", lambda e, uTb=uTb: e.tensor_scalar(out=uAB[:, 1024:2048], in0=uTb[:, 1024:2048], scalar1=selt2[:, 1:2], scalar2=None, op0=ALU.mult),
                     reads=[uTr, "selt2", "uAB"], writes=["uAB"])
                P.op("dve", lambda e, uTb=uTb: e.scalar_tensor_tensor(out=uAB[:, 1024:2048], in0=uTb[:, 0:1024], scalar=selt2[:, 0:1], in1=uAB[:, 1024:2048],
                                                                      op0=ALU.mult, op1=ALU.add), reads=[uTr, "selt2", "uAB"], writes=["uAB"])
                for q in range(4):
                    i = 4 * ct + q
                    bl = [Blp[0][i % 2], Blp[1][i % 2]]
                    blr = ["Blp0_%d" % (i % 2), "Blp1_%d" % (i % 2)]
                    for ri in range(2):
                        P.op("dve", lambda e, ri=ri, ct=ct, q=q, bl=bl: e.tensor_scalar(
                            out=bl[ri][:], in0=Bc[ri][:, ct, :], scalar1=rowmask[:, q:q + 1], scalar2=None, op0=ALU.mult),
                            reads=["Bc%d" % ri, "rowmask"], writes=[blr[ri]])
                        P.op("pool", lambda e, ri=ri, i=i, q=q: e.tensor_copy(out=Clp[ri][q][:, 32 * q:32 * q + 32], in_=Cc[ri][:, i, :]),
                             reads=["Cc%d" % ri], writes=["Clp%d_%d" % (ri, q)])
                    for tc in range(4):
                        gc = i * 4 + tc
                        k2 = gc % 2
                        sl = slice(tc * 512, (tc + 1) * 512)
                        tC, tS = tabC[k2], tabS[k2]
                        tCr, tSr = "tabC%d" % k2, "tabS%d" % k2
                        wre, wim = Wre[k2], Wim[k2]
                        wrr, wir = "Wre%d" % k2, "Wim%d" % k2
                        srb, sib = sre[k2], sim[k2]
                        srr, sir = "sre%d" % k2, "sim%d" % k2
                        cNb = cN[:, i, 8 * tc:8 * tc + 8].unsqueeze(2).to_broadcast([128, 8, 64])
                        sNb = sN[:, i, 8 * tc:8 * tc + 8].unsqueeze(2).to_broadcast([128, 8, 64])
                        cDb = cD[:, i, :].unsqueeze(1).to_broadcast([128, 8, 64])
                        sDb = sD[:, i, :].unsqueeze(1).to_broadcast([128, 8, 64])
                        tC3 = tC[:].rearrange("p (n d) -> p n d", d=64)
                        tS3 = tS[:].rearrange("p (n d) -> p n d", d=64)
                        tt3 = ttmp[:].rearrange("p (n d) -> p n d", d=64)
                        P.op("pool", lambda e, a=cNb, b=cDb, o=tC3: e.tensor_tensor(out=o, in0=a, in1=b, op=ALU.mult), reads=["cN", "cD"], writes=[tCr])
                        P.op("pool", lambda e, a=sNb, b=sDb, o=tt3: e.tensor_tensor(out=o, in0=a, in1=b, op=ALU.mult), reads=["sN", "sD"], writes=["ttmp"])
                        P.op("pool", lambda e, tC=tC: e.tensor_tensor(out=tC[:], in0=tC[:], in1=ttmp[:], op=ALU.subtract), reads=[tCr, "ttmp"], writes=[tCr])
                        P.op("pool", lambda e, a=sNb, b=cDb, o=tS3: e.tensor_tensor(out=o, in0=a, in1=b, op=ALU.mult), reads=["sN", "cD"], writes=[tSr])
                        P.op("pool", lambda e, a=cNb, b=sDb, o=tt3: e.tensor_tensor(out=o, in0=a, in1=b, op=ALU.mult), reads=["cN", "sD"], writes=["ttmp"])
                        P.op("pool", lambda e, tS=tS: e.tensor_tensor(out=tS[:], in0=tS[:], in1=ttmp[:], op=ALU.add), reads=[tSr, "ttmp"], writes=[tSr])
                        for ri in range(2):
                            P.op("pe", lambda e, ri=ri, bl=bl, uTb=uTb, sl=sl: e.matmul(ps[ri][:], lhsT=bl[ri][:], rhs=uAB[:, sl], start=True, stop=True),
                                 reads=[blr[ri], "uAB"], writes=["ps%d" % ri])
                        P.op("dve", lambda e, wre=wre, tC=tC: e.tensor_tensor(out=wre[:], in0=ps[0][:], in1=tC[:], op=ALU.mult), reads=["ps0", tCr], writes=[wrr])
                        P.op("dve", lambda e, tS=tS: e.tensor_tensor(out=r1[0][:], in0=ps[1][:], in1=tS[:], op=ALU.mult), reads=["ps1", tSr], writes=["r1_0"])
                        P.op("dve", lambda e, wim=wim, tC=tC: e.tensor_tensor(out=wim[:], in0=ps[1][:], in1=tC[:], op=ALU.mult), reads=["ps1", tCr], writes=[wir])
                        P.op("dve", lambda e, tS=tS: e.tensor_tensor(out=r1[1][:], in0=ps[0][:], in1=tS[:], op=ALU.mult), reads=["ps0", tSr], writes=["r1_1"])
                        P.op("dve", lambda e, wre=wre: e.tensor_tensor(out=wre[:], in0=wre[:], in1=r1[0][:], op=ALU.add), reads=[wrr, "r1_0"], writes=[wrr])
                        P.op("dve", lambda e, wim=wim: e.tensor_tensor(out=wim[:], in0=wim[:], in1=r1[1][:], op=ALU.subtract), reads=[wir, "r1_1"], writes=[wir])
                        for W, Wp, wn, wpn in ((wre, Wre[1 - k2], wrr, "Wre%d" % (1 - k2)), (wim, Wim[1 - k2], wir, "Wim%d" % (1 - k2))):
                            init = 0.0 if tc == 0 else Wp[:, 511:512]
                            P.op("dve", lambda e, W=W, init=init, i=i: e.tensor_tensor_scan(
                                out=W[:], data0=mag[:, i:i + 1].to_broadcast([128, 512]), data1=W[:], initial=init,
                                op0=ALU.mult, op1=ALU.add), reads=[wn, wpn, "mag"], writes=[wn])
                        if tc < 2:
                            continue
                        P.op("dve", lambda e, wre=wre, tC=tC: e.tensor_tensor(out=ttmp[:], in0=wre[:], in1=tC[:], op=ALU.mult), reads=[wrr, tCr], writes=["ttmp"])
                        P.op("dve", lambda e, wim=wim, tS=tS: e.tensor_tensor(out=r1[2][:], in0=wim[:], in1=tS[:], op=ALU.mult), reads=[wir, tSr], writes=["r1_2"])
                        P.op("dve", lambda e, wre=wre, tS=tS: e.tensor_tensor(out=r1[3][:], in0=wre[:], in1=tS[:], op=ALU.mult), reads=[wrr, tSr], writes=["r1_3"])
                        P.op("dve", lambda e, wim=wim, tC=tC: e.tensor_tensor(out=r1[4][:], in0=wim[:], in1=tC[:], op=ALU.mult), reads=[wir, tCr], writes=["r1_4"])
                        P.op("dve", lambda e, srb=srb: e.tensor_tensor(out=srb[:], in0=ttmp[:], in1=r1[2][:], op=ALU.subtract), reads=["ttmp", "r1_2"], writes=[srr])
                        P.op("dve", lambda e, sib=sib: e.tensor_tensor(out=sib[:], in0=r1[3][:], in1=r1[4][:], op=ALU.add), reads=["r1_3", "r1_4"], writes=[sir])
                        yb = ps[4 + tc]
                        ybr = "ps%d" % (4 + tc)
                        P.op("pe", lambda e, yb=yb, q=q, srb=srb: e.matmul(yb[:], lhsT=Clp[0][q][:], rhs=srb[:], start=(q == 0), stop=False),
                             reads=["Clp0_%d" % q, srr], writes=[ybr])
                        P.op("pe", lambda e, yb=yb, q=q, sib=sib: e.matmul(yb[:], lhsT=Clp[1][q][:], rhs=sib[:], start=False, stop=(q == 3)),
                             reads=["Clp1_%d" % q, sir], writes=[ybr])
                for tc in (2, 3):
                    sl = slice(tc * 512, (tc + 1) * 512)
                    yb = ps[4 + tc]
                    ybr = "ps%d" % (4 + tc)
                    o2 = slice((tc - 2) * 512, (tc - 1) * 512)
                    P.op("dve", lambda e, yb=yb, sl=sl, ct=ct, o2=o2: e.scalar_tensor_tensor(
                        out=ysel[:, o2], in0=uAB[:, sl], scalar=dcol[:, ct:ct + 1], in1=yb[:], op0=ALU.mult, op1=ALU.add),
                        reads=["uAB", "dcol", ybr], writes=["ysel"])
                P.op("act", lambda e, ct=ct: e.activation(out=vgT[:, ct, :], in_=ysel[:], func=AF.Gelu), reads=["ysel"], writes=["vgT"])
                if debug is not None and debug == (128, 1025) and ct == 0:
                    final_tokens.append(P.dma("sp", dbg[:, 0:1024], ysel[:], reads=["ysel"]))
            phl.close()
            P.barrier()
            if NCT == 8:
                wgl = sb("wgl", [128, 8, 1024], BF16, ph)
                gte = sb("gte", [128, 512], F32, ph)
                osm = sb("osm", [128, 1024], F32, ph)
                P.dma("pool", wgl[:], w_glu.rearrange("(kt p) n -> p kt n", p=128), writes=["wgl"])
                for jt in range(8):
                    for hc in range(2):
                        sl = slice(hc * 512, (hc + 1) * 512)
                        pb = ps[hc]
                        pr = "ps%d" % hc
                        for ct in range(8):
                            P.op("pe", lambda e, pb=pb, ct=ct, jt=jt, sl=sl: e.matmul(
                                pb[:], lhsT=wgl[:, ct, jt * 128:(jt + 1) * 128], rhs=vgT[:, ct, sl], start=(ct == 0), stop=(ct == 7)),
                                reads=["wgl", "vgT"], writes=[pr])
                        P.op("act", lambda e, pb=pb, jt=jt: e.activation(out=gte[:], in_=pb[:], func=AF.Sigmoid, bias=bgl[:, jt:jt + 1]),
                             reads=[pr, "bgl"], writes=["gte"])
                        P.op("dve", lambda e, jt=jt, sl=sl: e.tensor_tensor(out=osm[:, sl], in0=vgT[:, jt, sl], in1=gte[:], op=ALU.mult),
                             reads=["vgT", "gte"], writes=["osm"])
                    P.op("act", lambda e: e.activation(out=ysel[:], in_=osm[:], func=AF.Square), reads=["osm"], writes=["ysel"])
                    if jt == 0:
                        P.op("pool", lambda e: e.tensor_copy(out=sqssm[:], in_=ysel[:]), reads=["ysel"], writes=["sqssm"])
                    else:
                        P.op("pool", lambda e: e.tensor_tensor(out=sqssm[:], in0=sqssm[:], in1=ysel[:], op=ALU.add),
                             reads=["ysel", "sqssm"], writes=["sqssm"])
                    P.op("act", lambda e, jt=jt: e.activation(out=yT[:, jt, :], in_=osm[:], func=AF.Identity, scale=gsl[:, jt:jt + 1]),
                         reads=["osm", "gsl"], writes=["yT"])
                    if debug is not None and debug == (128, 1026) and jt == 0:
                        final_tokens.append(P.dma("sp", dbg[:, 0:1024], osm[:], reads=["osm"]))
        P.barrier()


        if debug is None or full_tail:
            w_out = din("w_out", [D, D])
            g_ffn = din("g_ffn", [1, D])
            w_gate = din("w_gate", [D, DFF])
            w_up = din("w_up", [D, DFF])
            w_down = din("w_down", [DFF, D])
            g_final = din("g_final", [1, D])
            x1v = hT[:].rearrange("p a b -> p (a b)").bitcast(F32).rearrange("p (t d) -> p t d", d=D)
            h2T = yT
            wmv = w_mod.rearrange("(kt p) n -> p kt n", p=128)

            mgc = [0]

            def mod_group(st, chunks, dsts, post=None):
                mgc[0] += 1
                sfx = "_g%d" % mgc[0]
                cs2 = sb("cs2" + sfx, [128, KT], F32, st)
                csr2 = sb("csr2" + sfx, [128, KT, 128], BF16, st)
                wm2 = [sb("wm2_%d%s" % (i, sfx), [128, KT, 512], BF16, st) for i in range(2)]
                bm2 = [sb("bm2_%d%s" % (i, sfx), [128, 512], F32, st) for i in range(2)]
                P.dma("sp", cs2[:], c_l[:, :], writes=["cs2"])
                P.op("act", lambda e: e.activation(out=cs2[:], in_=cs2[:], func=AF.Silu), reads=["cs2"], writes=["cs2"])
                P.op("dve", lambda e: e.tensor_copy(out=csr2[:], in_=cs2[:].unsqueeze(2).to_broadcast([128, KT, 128])),
                     reads=["cs2"], writes=["csr2"])
                for n, j in enumerate(chunks):
                    bi = n % 2
                    dst, dn = dsts[n // 4]
                    jj = n % 4
                    P.dma("pool", wm2[bi][:], wmv[:, :, j * 512:(j + 1) * 512], writes=["wm2_%d" % bi])
                    P.dma("sp", bm2[bi][:], b_mod[0:1, j * 512:(j + 1) * 512].partition_broadcast(128), writes=["bm2_%d" % bi])
                    pb = ps[bi]
                    for kt in range(KT):
                        P.op("pe", lambda e, kt=kt, pb=pb, bi=bi: e.matmul(pb[:], lhsT=csr2[:, kt, :], rhs=wm2[bi][:, kt, :],
                                                                        start=(kt == 0), stop=(kt == KT - 1)),
                             reads=["csr2", "wm2_%d" % bi], writes=["ps%d" % bi])
                    P.op("dve", lambda e, pb=pb, bi=bi, dst=dst, jj=jj: e.tensor_tensor(
                        out=dst[:, jj * 512:(jj + 1) * 512], in0=pb[:], in1=bm2[bi][:], op=ALU.add),
                        reads=["ps%d" % bi, "bm2_%d" % bi], writes=[dn])

            ph = ExitStack()
            with ph:
                onesf = sb("onesf", [128, 1], F32, ph)
                rsq = sb("rsq", [128, 16], F32, ph)
                gt1 = sb("gt1", [128, D], F32, ph)
                P.op("dve", lambda e: e.memset(onesf[:], 1.0), writes=["onesf"])
                for k, (sq, sqn) in enumerate(((sqssm, "sqssm"), (sqatt, "sqatt"))):
                    for tt in range(8):
                        P.op("pe", lambda e, k=k, tt=tt, sq=sq: e.matmul(ps[7][:, k * 8 + tt:k * 8 + tt + 1], lhsT=sq[:, tt * 128:(tt + 1) * 128],
                                                                      rhs=onesf[:, 0:1], start=True, stop=True),
                             reads=[sqn, "onesf"], writes=["ps7"])
                P.op("dve", lambda e: e.tensor_scalar(out=rsq[:], in0=ps[7][:, 0:16], scalar1=1.0 / 1024.0, scalar2=EPS,
                                                      op0=ALU.mult, op1=ALU.add), reads=["ps7"], writes=["rsq"])
                P.op("act", lambda e: e.activation(out=rsq[:], in_=rsq[:], func=AF.Sqrt), reads=["rsq"], writes=["rsq"])
                P.op("dve", lambda e: e.reciprocal(out=rsq[:], in_=rsq[:]), reads=["rsq"], writes=["rsq"])
                P.dma("sp", gt1[:], modrow[0:1, 0:D].partition_broadcast(128), reads=["modrow"], writes=["gt1"])
                for tt in range(8):
                    P.dma("sp", x1v[:, tt, :], xh[tt * 128:(tt + 1) * 128, :], writes=["hT"])
                wo = [sb("wo%d" % i, [128, KT, 512], BF16, ph) for i in range(2)]
                tA = sb("tA", [128, 512], F32, ph)
                tB = sb("tB", [128, 512], F32, ph)
                wov = w_out.rearrange("(kt p) n -> p kt n", p=128)
                for dc in range(4):
                    wb = wo[dc % 2]
                    wr = "wo%d" % (dc % 2)
                    dsl = slice(dc * 512, (dc + 1) * 512)
                    P.dma("pool", wb[:], wov[:, :, dsl], writes=[wr])
                    for tt in range(8):
                        k2 = (dc * 8 + tt) % 2
                        pS, pA = ps[2 * k2], ps[2 * k2 + 1]
                        pSr, pAr = "ps%d" % (2 * k2), "ps%d" % (2 * k2 + 1)
                        for ct in range(8):
                            P.op("pe", lambda e, pS=pS, ct=ct, tt=tt, wb=wb: e.matmul(
                                pS[:], lhsT=yT[:, ct, tt * 128:(tt + 1) * 128], rhs=wb[:, ct, :], start=(ct == 0), stop=(ct == 7)),
                                reads=["yT", wr], writes=[pSr])
                        for ct in range(8, 16):
                            P.op("pe", lambda e, pA=pA, ct=ct, tt=tt, wb=wb: e.matmul(
                                pA[:], lhsT=yT[:, ct, tt * 128:(tt + 1) * 128], rhs=wb[:, ct, :], start=(ct == 8), stop=(ct == 15)),
                                reads=["yT", wr], writes=[pAr])
                        P.op("act", lambda e, pS=pS, tt=tt: e.activation(out=tA[:], in_=pS[:], func=AF.Identity, scale=rsq[:, tt:tt + 1]),
                             reads=[pSr, "rsq"], writes=["tA"])
                        P.op("dve", lambda e, pA=pA, tt=tt: e.scalar_tensor_tensor(out=tB[:], in0=pA[:], scalar=rsq[:, 8 + tt:9 + tt], in1=tA[:],
                                                                                 op0=ALU.mult, op1=ALU.add), reads=[pAr, "rsq", "tA"], writes=["tB"])
                        P.op("dve", lambda e, dsl=dsl: e.tensor_tensor(out=tB[:], in0=tB[:], in1=gt1[:, dsl], op=ALU.mult),
                             reads=["tB", "gt1"], writes=["tB"])
                        P.op("dve", lambda e, tt=tt, dsl=dsl: e.tensor_tensor(out=x1v[:, tt, dsl], in0=x1v[:, tt, dsl], in1=tB[:], op=ALU.add),
                             reads=["tB", "hT"], writes=["hT"])
            P.barrier()

            ph = ExitStack()
            with ph:
                gt2 = sb("gt2", [128, D], F32, ph)
                ss2 = sb("ss2", [128, 8], F32, ph)
                rstd2 = sb("rstd2", [128, 8], F32, ph)
                phn = ExitStack()
                G2 = sb("G2", [128, D], F32, phn)
                sh2 = sb("sh2", [128, D], F32, phn)
                gf = sb("gf", [128, D], F32, phn)
                hf2 = sb("hf2", [128, D], F32, phn)
                hb2 = [sb("hb2_%d" % i, [128, D], BF16, phn) for i in range(2)]
                junk2 = sb("junk2", [128, D], BF16, phn)
                P.dma("sp", sh2[:], modrow[0:1, D:2 * D].partition_broadcast(128), reads=["modrow"], writes=["sh2"])
                P.dma("sp", G2[:], modrow[0:1, 2 * D:3 * D].partition_broadcast(128), reads=["modrow"], writes=["G2"])
                P.dma("sp", gt2[:], modrow[0:1, 3 * D:4 * D].partition_broadcast(128), reads=["modrow"], writes=["gt2"])
                P.dma("sp", gf[:], g_ffn[0:1, :].partition_broadcast(128), writes=["gf"])
                P.op("dve", lambda e: e.scalar_tensor_tensor(out=G2[:], in0=G2[:], scalar=1.0, in1=gf[:], op0=ALU.add, op1=ALU.mult),
                     reads=["G2", "gf"], writes=["G2"])
                for tt in range(8):
                    bi = tt % 2
                    P.op("act", lambda e, tt=tt: e.activation(out=junk2[:], in_=x1v[:, tt, :], func=AF.Square, accum_out=ss2[:, tt:tt + 1]),
                         reads=["hT"], writes=["junk2", "ss2_%d" % tt])
                    P.op("dve", lambda e, tt=tt: e.tensor_scalar(out=rstd2[:, tt:tt + 1], in0=ss2[:, tt:tt + 1], scalar1=1.0 / D, scalar2=EPS,
                                                                 op0=ALU.mult, op1=ALU.add), reads=["ss2_%d" % tt], writes=["rstd2_%d" % tt])
                    P.op("act", lambda e, tt=tt: e.activation(out=rstd2[:, tt:tt + 1], in_=rstd2[:, tt:tt + 1], func=AF.Sqrt),
                         reads=["rstd2_%d" % tt], writes=["rstd2_%d" % tt])
                    P.op("dve", lambda e, tt=tt: e.reciprocal(out=rstd2[:, tt:tt + 1], in_=rstd2[:, tt:tt + 1]),
                         reads=["rstd2_%d" % tt], writes=["rstd2_%d" % tt])
                    P.op("dve", lambda e, tt=tt: e.scalar_tensor_tensor(out=hf2[:], in0=x1v[:, tt, :], scalar=rstd2[:, tt:tt + 1], in1=G2[:],
                                                                        op0=ALU.mult, op1=ALU.mult), reads=["hT", "rstd2_%d" % tt, "G2"], writes=["hf2"])
                    hbb = hb2[bi]
                    P.op("dve", lambda e, hbb=hbb: e.tensor_tensor(out=hbb[:], in0=hf2[:], in1=sh2[:], op=ALU.add),
                         reads=["hf2", "sh2"], writes=["hb2_%d" % bi])
                    for g in range(2):
                        pi = 2 + (tt * 2 + g) % 2
                        pT = ps[pi][:].bitcast(BF16)
                        for k8 in range(8):
                            kt = g * 8 + k8
                            P.op("pe", lambda e, kt=kt, k8=k8, pT=pT, hbb=hbb: e.transpose(
                                out=pT[:, k8 * 128:(k8 + 1) * 128], in_=hbb[:, kt * 128:(kt + 1) * 128], identity=ident[:]),
                                reads=["hb2_%d" % bi, "ident"], writes=["ps%d" % pi])
                        P.op("act", lambda e, g=g, tt=tt, pT=pT: e.activation(
                            out=h2T[:, g * 8:(g + 1) * 8, tt * 128:(tt + 1) * 128], in_=pT.rearrange("p (k t) -> p k t", k=8), func=AF.Copy),
                            reads=["ps%d" % pi], writes=["yT"])
                phn.close()
                P.barrier()
                wg = [sb("wg%d" % i, [128, KT, 256], BF16, ph) for i in range(2)]
                wu2 = [sb("wu2_%d" % i, [128, KT, 256], BF16, ph) for i in range(2)]
                wd = sb("wd", [128, 4, D], BF16, ph)
                actT = sb("actT", [128, 4, 1024], BF16, ph)
                sg = [sb("sg%d" % i, [128, 512], F32, ph) for i in range(2)]
                tD = [sb("tD%d" % i, [128, 512], F32, ph) for i in range(2)]
                print("sbuf remaining (ffn)", nc.sbuf_bytes_remaining)
                wgv = w_gate.rearrange("(kt p) n -> p kt n", p=128)
                wuv = w_up.rearrange("(kt p) n -> p kt n", p=128)
                wdv = w_down.rearrange("(f p) d -> p f d", p=128)
                NFB = DFF // 512
                wi = 0
                ei = 0
                for fb in range(NFB):
                    for half in range(2):
                        k2 = wi % 2
                        wi += 1
                        c0 = fb * 512 + half * 256
                        P.dma("pool", wg[k2][:], wgv[:, :, c0:c0 + 256], writes=["wg%d" % k2])
                        P.dma("pool", wu2[k2][:], wuv[:, :, c0:c0 + 256], writes=["wu2_%d" % k2])
                        for f2 in range(2):
                            fi = half * 2 + f2
                            for hc in range(2):
                                tsl = slice(hc * 512, (hc + 1) * 512)
                                pG, pU_ = ps[hc], ps[2 + hc]
                                for kt in range(KT):
                                    P.op("pe", lambda e, pG=pG, kt=kt, k2=k2, f2=f2, tsl=tsl: e.matmul(
                                        pG[:], lhsT=wg[k2][:, kt, f2 * 128:(f2 + 1) * 128], rhs=h2T[:, kt, tsl], start=(kt == 0), stop=(kt == KT - 1)),
                                        reads=["wg%d" % k2, "yT"], writes=["ps%d" % hc])
                                for kt in range(KT):
                                    P.op("pe", lambda e, pU_=pU_, kt=kt, k2=k2, f2=f2, tsl=tsl: e.matmul(
                                        pU_[:], lhsT=wu2[k2][:, kt, f2 * 128:(f2 + 1) * 128], rhs=h2T[:, kt, tsl], start=(kt == 0), stop=(kt == KT - 1)),
                                        reads=["wu2_%d" % k2, "yT"], writes=["ps%d" % (2 + hc)])
                                P.op("act", lambda e, pG=pG, hc=hc: e.activation(out=sg[hc][:], in_=pG[:], func=AF.Silu),
                                     reads=["ps%d" % hc], writes=["sg%d" % hc])
                                P.op("dve", lambda e, pU_=pU_, hc=hc, fi=fi, tsl=tsl: e.tensor_tensor(out=actT[:, fi, tsl], in0=pU_[:], in1=sg[hc][:], op=ALU.mult),
                                     reads=["ps%d" % (2 + hc), "sg%d" % hc], writes=["actT"])
                    P.dma("pool", wd[:], wdv[:, fb * 4:(fb + 1) * 4, :], writes=["wd"])
                    for tt in range(8):
                        for dc in range(4):
                            dsl = slice(dc * 512, (dc + 1) * 512)
                            pk = 4 + ei % 4
                            tk = ei % 2
                            ei += 1
                            pD = ps[pk]
                            for fi in range(4):
                                P.op("pe", lambda e, pD=pD, fi=fi, tt=tt, dsl=dsl: e.matmul(
                                    pD[:], lhsT=actT[:, fi, tt * 128:(tt + 1) * 128], rhs=wd[:, fi, dsl], start=(fi == 0), stop=(fi == 3)),
                                    reads=["actT", "wd"], writes=["ps%d" % pk])
                            P.op("dve", lambda e, pD=pD, tk=tk, dsl=dsl: e.tensor_tensor(out=tD[tk][:], in0=pD[:], in1=gt2[:, dsl], op=ALU.mult),
                                 reads=["ps%d" % pk, "gt2"], writes=["tD%d" % tk])
                            P.op("dve", lambda e, tk=tk, tt=tt, dsl=dsl: e.tensor_tensor(out=x1v[:, tt, dsl], in0=x1v[:, tt, dsl], in1=tD[tk][:], op=ALU.add),
                                 reads=["tD%d" % tk, "hT"], writes=["hT"])
            P.barrier()

            ph = ExitStack()
            with ph:
                gfin = sb("gfin", [128, D], F32, ph)
                ss3 = sb("ss3", [128, 8], F32, ph)
                junk3 = sb("junk3", [128, D], BF16, ph)
                ot = [sb("ot%d" % i, [128, D], F32, ph) for i in range(2)]
                P.dma("sp", gfin[:], g_final[0:1, :].partition_broadcast(128), writes=["gfin"])
                for tt in range(8):
                    bi = tt % 2
                    P.op("act", lambda e, tt=tt: e.activation(out=junk3[:], in_=x1v[:, tt, :], func=AF.Square, accum_out=ss3[:, tt:tt + 1]),
                         reads=["hT"], writes=["junk3", "ss3_%d" % tt])
                    P.op("dve", lambda e, tt=tt: e.tensor_scalar(out=ss3[:, tt:tt + 1], in0=ss3[:, tt:tt + 1], scalar1=1.0 / D, scalar2=EPS,
                                                                 op0=ALU.mult, op1=ALU.add), reads=["ss3_%d" % tt], writes=["ss3_%d" % tt])
                    P.op("act", lambda e, tt=tt: e.activation(out=ss3[:, tt:tt + 1], in_=ss3[:, tt:tt + 1], func=AF.Sqrt),
                         reads=["ss3_%d" % tt], writes=["ss3_%d" % tt])
                    P.op("dve", lambda e, tt=tt: e.reciprocal(out=ss3[:, tt:tt + 1], in_=ss3[:, tt:tt + 1]),
                         reads=["ss3_%d" % tt], writes=["ss3_%d" % tt])
                    P.op("dve", lambda e, tt=tt, bi=bi: e.scalar_tensor_tensor(out=ot[bi][:], in0=x1v[:, tt, :], scalar=ss3[:, tt:tt + 1], in1=gfin[:],
                                                                             op0=ALU.mult, op1=ALU.mult), reads=["hT", "ss3_%d" % tt, "gfin"], writes=["ot%d" % bi])
                    final_tokens.append(P.dma("sp", out[tt * 128:(tt + 1) * 128, :], ot[bi][:], reads=["ot%d" % bi]))

        if debug is not None and not full_tail:
            zt = sb("zt", [128, D], F32)
            P.op("dve", lambda e: e.memset(zt[:], 0.0), writes=["zt"])
            for tt in range(8):
                final_tokens.append(P.dma("sp", out[tt * 128:(tt + 1) * 128, :], zt[:], reads=["zt"]))

        print('ops', P.simulate(), {e: len(v) for e, v in P.eng_ops.items()})
        with nc.Block() as block:
            P.emit(block, final_tokens)
    return nc


def make_in_maps(inputs):
    x = np.ascontiguousarray(inputs["x"], dtype=np.float32)
    c = np.asarray(inputs["c"], dtype=np.float32)
    pos = np.asarray(inputs["positions"], dtype=np.int32)
    maps = []
    for core in range(8):
        b, j = core // 2, core % 2
        selv = np.zeros((128, 2), np.float32)
        selv[:, j] = 1.0
        m = {
            "xb": x[b],
            "xh": np.ascontiguousarray(x[b, j * 1024:(j + 1) * 1024]),
            "c_l": np.ascontiguousarray(c[b].reshape(KT, 128).T),
            "pos": np.ascontiguousarray(pos[b].reshape(1, S)),
            "sel": selv,
            "w_mod": np.ascontiguousarray(inputs["w_mod"][0]),
            "b_mod": np.ascontiguousarray(inputs["b_mod"][0].reshape(1, -1)),
            "g_mix": np.ascontiguousarray(inputs["g_mix"][0].reshape(1, -1)),
            "w_in": np.ascontiguousarray(inputs["w_in"][0]),
            "g_att_l": np.ascontiguousarray(inputs["g_attn_out"][0].reshape(8, 128).T),
            "a_re_l": np.ascontiguousarray(inputs["ssm_a_re"][0].reshape(32, 128).T),
            "a_im_l": np.ascontiguousarray(inputs["ssm_a_im"][0].reshape(32, 128).T),
            "ldt_l": np.ascontiguousarray(np.repeat(inputs["ssm_log_dt"][0], 64).reshape(32, 128).T),
            "BTr": np.ascontiguousarray(inputs["ssm_b_re"][0].transpose(0, 2, 1)),
            "BTi": np.ascontiguousarray(inputs["ssm_b_im"][0].transpose(0, 2, 1)),
            "CTr": np.ascontiguousarray(inputs["ssm_c_re"][0].transpose(0, 2, 1)),
            "CTi": np.ascontiguousarray(inputs["ssm_c_im"][0].transpose(0, 2, 1)),
            "d_l": np.ascontiguousarray(inputs["ssm_d"][0].reshape(8, 128).T),
            "w_glu": np.ascontiguousarray(inputs["w_glu"][0]),
            "b_glu_l": np.ascontiguousarray(inputs["b_glu"][0].reshape(8, 128).T),
            "g_ssm_l": np.ascontiguousarray(inputs["g_ssm_out"][0].reshape(8, 128).T),
            "w_out": np.ascontiguousarray(inputs["w_out"][0]),
            "g_ffn": np.ascontiguousarray(inputs["g_ffn"][0].reshape(1, -1)),
            "w_gate": np.ascontiguousarray(inputs["w_gate"][0]),
            "w_up": np.ascontiguousarray(inputs["w_up"][0]),
            "w_down": np.ascontiguousarray(inputs["w_down"][0]),
            "g_final": np.ascontiguousarray(inputs["g_final"].reshape(1, -1)),
        }
        maps.append(m)
    return maps


def kernel(**inputs):
    nc = build()
    maps = make_in_maps(inputs)
    res = run_bass_kernel_spmd(nc, maps, core_ids=list(range(8)))
    outp = np.zeros((4, S, D), np.float32)
    for core in range(8):
        b, j = core // 2, core % 2
        outp[b, j * 1024:(j + 1) * 1024] = res.results[core]["out"]
    return outp
```

```python
import math
from contextlib import ExitStack
import numpy as np
import concourse.bass as bass
import concourse.mybir as mybir
from concourse.bass_utils import run_bass_kernel_spmd

F32 = mybir.dt.float32
BF16 = mybir.dt.bfloat16
I32 = mybir.dt.int32
AF = mybir.ActivationFunctionType
ALU = mybir.AluOpType
AX = mybir.AxisListType

D = 2048
S = 2048
KT = D // 128
NMOD = 6
DFF = 5632
EPS = 1e-6
TWO_PI = 2.0 * math.pi


class Prog:
    ENGS = ("pe", "act", "dve", "pool", "sp")

    def __init__(self, nc, es):
        self.nc = nc
        self.es = es
        self.eng_ops = {e: [] for e in self.ENGS}
        self.last_write = {}
        self.readers = {}
        self.sem = {}
        self.cnt = {}
        self.nsem = 0
        for e in self.ENGS:
            self._new_eng_sem(e)
        self.known = {e: {} for e in self.ENGS}
        self.dma_pool = {}
        self.dma_val = {}
        self.dma_rr = {}
        for q, n in (("sp", 12), ("pool", 12), ("act", 6)):
            self.dma_pool[q] = [self._alloc_sem() for _ in range(n)]
            self.dma_rr[q] = 0
            for s in self.dma_pool[q]:
                self.dma_val[s] = 0
        self.pending = {e: {} for e in self.ENGS}
        self.sem_objs = {}

    def _alloc_sem(self):
        self.nsem += 1
        return self.es.enter_context(self.nc.semaphore("sm%d" % self.nsem))

    def _new_eng_sem(self, e):
        self.sem[e] = self._alloc_sem()
        self.cnt[e] = 0

    def _deps(self, eng, reads, writes):
        deps = {}

        def add(tok):
            if tok is None:
                return
            s, v = tok
            if deps.get(s, 0) < v:
                deps[s] = v
        for r in reads:
            add(self.last_write.get(r))
        for w in writes:
            add(self.last_write.get(w))
            for s, v in self.readers.get(w, {}).items():
                add((s, v))
        for s, v in self.pending[eng].items():
            add((s, v))
        self.pending[eng] = {}
        return deps

    def _finish(self, eng, deps, tok, reads, writes):
        waits = []
        for s, v in deps.items():
            if eng == "pe" and s is self.sem["pe"]:
                continue
            if self.known[eng].get(s, 0) >= v:
                continue
            self.known[eng][s] = v
            waits.append((s, v))
        for w in writes:
            self.last_write[w] = tok
            self.readers[w] = {}
        for r in reads:
            d = self.readers.setdefault(r, {})
            if d.get(tok[0], 0) < tok[1]:
                d[tok[0]] = tok[1]
        return waits

    def op(self, eng, fn, reads=(), writes=()):
        deps = self._deps(eng, reads, writes)
        if self.cnt[eng] >= 6000:
            self._new_eng_sem(eng)
        self.cnt[eng] += 1
        tok = (self.sem[eng], self.cnt[eng])
        waits = self._finish(eng, deps, tok, reads, writes)
        self.eng_ops[eng].append((waits, fn, tok[0], 1))
        return tok

    def dma(self, q, out, in_, reads=(), writes=(), **kw):
        deps = self._deps(q, reads, writes)
        pool = self.dma_pool[q]
        s = pool[self.dma_rr[q] % len(pool)]
        self.dma_rr[q] += 1
        if self.dma_val[s] > 0:
            if deps.get(s, 0) < self.dma_val[s]:
                deps[s] = self.dma_val[s]
        self.dma_val[s] += 16
        tok = (s, self.dma_val[s])
        waits = self._finish(q, deps, tok, reads, writes)
        self.eng_ops[q].append((waits, lambda e: e.dma_start(out=out, in_=in_, **kw), s, 16))
        return tok

    def barrier(self):
        toks = {}
        for e in self.ENGS:
            if self.cnt[e] > 0:
                toks[self.sem[e]] = self.cnt[e]
        for s, v in self.dma_val.items():
            if v > 0:
                toks[s] = v
        for e in self.ENGS:
            for s, v in toks.items():
                if self.pending[e].get(s, 0) < v:
                    self.pending[e][s] = v

    def simulate(self):
        val = {}
        pc = {e: 0 for e in self.ENGS}
        total = sum(len(v) for v in self.eng_ops.values())
        done = 0
        while done < total:
            prog = False
            for e in self.ENGS:
                ops = self.eng_ops[e]
                while pc[e] < len(ops):
                    waits, fn, s, inc = ops[pc[e]]
                    if all(val.get(id(ws), 0) >= wv for ws, wv in waits):
                        val[id(s)] = val.get(id(s), 0) + inc
                        pc[e] += 1
                        done += 1
                        prog = True
                    else:
                        break
            if not prog:
                st = {e: (pc[e], len(self.eng_ops[e])) for e in self.ENGS}
                raise RuntimeError("semaphore deadlock in program: %s" % st)
        return total

    def emit(self, block, final_tokens):
        nc = self.nc
        fin = {}
        for s, v in final_tokens:
            fin[s] = max(fin.get(s, 0), v)

        def run(e, name):
            for waits, fn, s, inc in self.eng_ops[name]:
                for ws, wv in waits:
                    e.wait_ge(ws, wv)
                fn(e).then_inc(s, inc)
            if name == "sp":
                for s, v in fin.items():
                    e.wait_ge(s, v)

        @block.sync
        def _(e):
            run(e, "sp")

        @block.scalar
        def _(e):
            run(e, "act")

        @block.vector
        def _(e):
            run(e, "dve")

        @block.gpsimd
        def _(e):
            run(e, "pool")

        @block.tensor
        def _(e):
            run(e, "pe")


def UTdump(P, src, UT):
    P.op("dve", lambda e: e.tensor_copy(out=UT[:, 0:1024], in_=src[:, 0:1024]), reads=["qT"], writes=["UT"])
    return UT[:, 0:1024]


def sincos(P, ang, angr, sdst, sr, cdst, cr, tf, tfr, ti, tir):
    P.op("dve", lambda e: e.tensor_scalar(out=tf[:], in0=ang[:], scalar1=1.0 / TWO_PI, scalar2=None, op0=ALU.mult),
         reads=[angr], writes=[tfr])
    P.op("dve", lambda e: e.tensor_copy(out=ti[:], in_=tf[:]), reads=[tfr], writes=[tir])
    P.op("dve", lambda e: e.tensor_copy(out=tf[:], in_=ti[:]), reads=[tir], writes=[tfr])
    P.op("dve", lambda e: e.scalar_tensor_tensor(out=sdst[:], in0=tf[:], scalar=-TWO_PI, in1=ang[:],
                                                 op0=ALU.mult, op1=ALU.add), reads=[tfr, angr], writes=[sr])

    def wrap(t, tr):
        P.op("dve", lambda e: e.tensor_scalar(out=tf[:], in0=t[:], scalar1=math.pi, scalar2=-TWO_PI,
                                              op0=ALU.is_gt, op1=ALU.mult), reads=[tr], writes=[tfr])
        P.op("dve", lambda e: e.tensor_tensor(out=t[:], in0=t[:], in1=tf[:], op=ALU.add), reads=[tr, tfr], writes=[tr])
        P.op("dve", lambda e: e.tensor_scalar(out=tf[:], in0=t[:], scalar1=-math.pi, scalar2=TWO_PI,
                                              op0=ALU.is_lt, op1=ALU.mult), reads=[tr], writes=[tfr])
        P.op("dve", lambda e: e.tensor_tensor(out=t[:], in0=t[:], in1=tf[:], op=ALU.add), reads=[tr, tfr], writes=[tr])
    wrap(sdst, sr)
    P.op("dve", lambda e: e.tensor_scalar_add(out=cdst[:], in0=sdst[:], scalar1=math.pi / 2), reads=[sr], writes=[cr])
    wrap(cdst, cr)
    for t, tr in ((sdst, sr), (cdst, cr)):
        P.op("dve", lambda e, t=t: e.tensor_scalar(out=t[:], in0=t[:], scalar1=math.pi, scalar2=-math.pi,
                                                   op0=ALU.min, op1=ALU.max), reads=[tr], writes=[tr])
        P.op("act", lambda e, t=t: e.activation(out=t[:], in_=t[:], func=AF.Sin), reads=[tr], writes=[tr])


def build(debug=None, att_level=9, ssm_full=False, full_tail=False):
    nc = bass.Bass("TRN2", target_bir_lowering=False)
    dram = {}

    def din(name, shape, dt=F32):
        dram[name] = nc.dram_tensor(name, list(shape), dt, kind="ExternalInput").ap()
        return dram[name]

    xb = din("xb", [S, D])
    xh = din("xh", [S // 2, D])
    c_l = din("c_l", [128, KT])
    pos = din("pos", [1, S], I32)
    sel = din("sel", [128, 2])
    w_mod = din("w_mod", [D, NMOD * D])
    b_mod = din("b_mod", [1, NMOD * D])
    g_mix = din("g_mix", [1, D])
    w_in = din("w_in", [D, 4096])
    out = nc.dram_tensor("out", [S // 2, D], F32, kind="ExternalOutput").ap()
    modrow = nc.dram_tensor("modrow", [1, 4 * D], F32, kind="Internal").ap()
    dbg = None
    if debug is not None:
        dbg = nc.dram_tensor("dbg", list(debug), F32, kind="ExternalOutput").ap()

    es = ExitStack()
    with es:
        P = Prog(nc, es)

        def sb(name, shape, dt=F32, st=es):
            return st.enter_context(nc.sbuf_tensor(name, list(shape), dt))

        ps = [es.enter_context(nc.psum_tensor("ps%d" % i, [128, 512], F32)) for i in range(8)]

        hT = sb("hT", [128, KT, S], BF16)
        ident = sb("ident", [128, 128], BF16)
        identf = sb("identf", [128, 128], F32)
        ones_bf = sb("ones_bf", [128, 128], BF16)
        yT = sb("yT", [128, 16, 1024], BF16)
        sqatt = sb("sqatt", [128, 1024], F32)
        sqssm = sb("sqssm", [128, 1024], F32)
        NH_RUN = 8 if debug is None else (1 if debug == (128, 1024) else 0)

        P.op("pool", lambda e: e.memset(identf[:], 1.0), writes=["identf"])
        P.op("pool", lambda e: e.affine_select(out=identf[:], in_=identf[:], pattern=[[-1, 128]],
                                               compare_op=ALU.is_equal, fill=0.0, base=0,
                                               channel_multiplier=1),
             reads=["identf"], writes=["identf"])
        P.op("dve", lambda e: e.tensor_copy(out=ident[:], in_=identf[:]), reads=["identf"], writes=["ident"])
        P.op("dve", lambda e: e.memset(ones_bf[:], 1.0), writes=["ones_bf"])

        final_tokens = []
        ph = ExitStack()
        with ph:
            cs = sb("cs", [128, KT], F32, ph)
            csr = sb("csr", [128, KT, 128], BF16, ph)
            G1 = sb("G1", [128, D], F32, ph)
            sh1 = sb("sh1", [128, D], F32, ph)
            gm = sb("gm", [128, D], F32, ph)
            wm = [sb("wm%d" % i, [128, KT, 512], BF16, ph) for i in range(2)]
            bm = [sb("bm%d" % i, [128, 512], F32, ph) for i in range(2)]
            xt = [sb("xt%d" % i, [128, D], F32, ph) for i in range(2)]
            hf = sb("hf", [128, D], F32, ph)
            hb = [sb("hb%d" % i, [128, D], BF16, ph) for i in range(2)]
            ss = sb("ss", [128, 16], F32, ph)
            rstd = sb("rstd", [128, 16], F32, ph)
            junk = sb("junk", [128, D], BF16, ph)

            P.dma("sp", cs[:], c_l[:, :], writes=["cs"])
            P.dma("sp", gm[:], g_mix[0:1, :].partition_broadcast(128), writes=["gm"])
            P.op("act", lambda e: e.activation(out=cs[:], in_=cs[:], func=AF.Silu), reads=["cs"], writes=["cs"])
            P.op("dve", lambda e: e.tensor_copy(out=csr[:], in_=cs[:].unsqueeze(2).to_broadcast([128, KT, 128])),
                 reads=["cs"], writes=["csr"])
            wmv = w_mod.rearrange("(kt p) n -> p kt n", p=128)

            def mod_chunk(j, evac):
                bi = j % 2
                P.dma("pool", wm[bi][:], wmv[:, :, j * 512:(j + 1) * 512], writes=["wm%d" % bi])
                P.dma("sp", bm[bi][:], b_mod[0:1, j * 512:(j + 1) * 512].partition_broadcast(128),
                      writes=["bm%d" % bi])
                pb = ps[bi]
                for kt in range(KT):
                    P.op("pe", lambda e, kt=kt: e.matmul(pb[:], lhsT=csr[:, kt, :], rhs=wm[bi][:, kt, :],
                                                         start=(kt == 0), stop=(kt == KT - 1)),
                         reads=["csr", "wm%d" % bi], writes=["ps%d" % bi])
                evac(j, pb, bm[bi], "ps%d" % bi, "bm%d" % bi)

            def evac1(j, pb, bmt, pr, br):
                if j < 4:
                    dst = sh1[:, j * 512:(j + 1) * 512]
                    P.op("dve", lambda e: e.tensor_tensor(out=dst, in0=pb[:], in1=bmt[:], op=ALU.add),
                         reads=[pr, br], writes=["sh1"])
                else:
                    jj = j - 4
                    dst = G1[:, jj * 512:(jj + 1) * 512]
                    P.op("dve", lambda e: e.tensor_tensor(out=dst, in0=pb[:], in1=bmt[:], op=ALU.add),
                         reads=[pr, br], writes=["G1"])
                    P.op("dve", lambda e: e.scalar_tensor_tensor(out=dst, in0=dst, scalar=1.0,
                                                                 in1=gm[:, jj * 512:(jj + 1) * 512],
                                                                 op0=ALU.add, op1=ALU.mult),
                         reads=["G1", "gm"], writes=["G1"])
            for j in range(8):
                mod_chunk(j, evac1)

            for tt in range(16):
                bi = tt % 2
                xtb = xt[bi]
                P.dma("sp", xtb[:], xb[tt * 128:(tt + 1) * 128, :], writes=["xt%d" % bi])
                P.op("act", lambda e, tt=tt, xtb=xtb: e.activation(out=junk[:], in_=xtb[:], func=AF.Square,
                                                                    accum_out=ss[:, tt:tt + 1]),
                     reads=["xt%d" % bi], writes=["junk", "ss%d" % tt])
                P.op("dve", lambda e, tt=tt: e.tensor_scalar(out=rstd[:, tt:tt + 1], in0=ss[:, tt:tt + 1],
                                                             scalar1=1.0 / D, scalar2=EPS,
                                                             op0=ALU.mult, op1=ALU.add),
                     reads=["ss%d" % tt], writes=["rstd%d" % tt])
                P.op("act", lambda e, tt=tt: e.activation(out=rstd[:, tt:tt + 1], in_=rstd[:, tt:tt + 1], func=AF.Sqrt),
                     reads=["rstd%d" % tt], writes=["rstd%d" % tt])
                P.op("dve", lambda e, tt=tt: e.reciprocal(out=rstd[:, tt:tt + 1], in_=rstd[:, tt:tt + 1]),
                     reads=["rstd%d" % tt], writes=["rstd%d" % tt])
                P.op("dve", lambda e, tt=tt, xtb=xtb: e.scalar_tensor_tensor(
                    out=hf[:], in0=xtb[:], scalar=rstd[:, tt:tt + 1], in1=G1[:], op0=ALU.mult, op1=ALU.mult),
                    reads=["xt%d" % bi, "rstd%d" % tt, "G1"], writes=["hf"])
                hbb = hb[bi]
                P.op("dve", lambda e, hbb=hbb: e.tensor_tensor(out=hbb[:], in0=hf[:], in1=sh1[:], op=ALU.add),
                     reads=["hf", "sh1"], writes=["hb%d" % bi])
                for g in range(2):
                    pi = 2 + (tt * 2 + g) % 2
                    pT = ps[pi][:].bitcast(BF16)
                    for k8 in range(8):
                        kt = g * 8 + k8
                        P.op("pe", lambda e, kt=kt, k8=k8, pT=pT, hbb=hbb: e.transpose(
                            out=pT[:, k8 * 128:(k8 + 1) * 128], in_=hbb[:, kt * 128:(kt + 1) * 128],
                            identity=ident[:]),
                            reads=["hb%d" % bi, "ident"], writes=["ps%d" % pi])
                    P.op("act", lambda e, g=g, tt=tt, pT=pT: e.activation(
                        out=hT[:, g * 8:(g + 1) * 8, tt * 128:(tt + 1) * 128],
                        in_=pT.rearrange("p (k t) -> p k t", k=8), func=AF.Copy),
                        reads=["ps%d" % pi], writes=["hT"])
            if debug is not None and debug == (128, KT * S):
                dtile = sb("dtile", [128, S], F32, ph)
                for kt in range(KT):
                    P.op("dve", lambda e, kt=kt: e.tensor_copy(out=dtile[:], in_=hT[:, kt, :]),
                         reads=["hT"], writes=["dtile"])
                    final_tokens.append(P.dma("sp", dbg[:, kt * S:(kt + 1) * S], dtile[:], reads=["dtile"]))
        P.barrier()


        g_att_l = din("g_att_l", [128, 8])
        ph = ExitStack()
        with ph:
            cosT = sb("cosT", [128, S], F32, ph)
            sinT = sb("sinT", [128, S], F32, ph)
            pidx = sb("pidx", [128, 1], F32, ph)
            invf = sb("invf", [128, 1], F32, ph)
            nsgn = sb("nsgn", [128, 1], F32, ph)
            neg1 = sb("neg1", [128, 1], F32, ph)
            selt = sb("selt", [128, 2], F32, ph)
            gat = sb("gat", [128, 8], F32, ph)
            permf = sb("permf", [128, 128], F32, ph)
            maskf = sb("maskf", [128, 256], F32, ph)
            maskb = sb("maskb", [128, 256], BF16, ph)
            ph0 = ExitStack()
            posi = sb("posi", [128, S], I32, ph0)
            angf = sb("angf", [128, S], F32, ph0)
            P.dma("sp", posi[:], pos[0:1, :].partition_broadcast(128), writes=["posi"])
            P.dma("sp", selt[:], sel[:, :], writes=["selt"])
            P.dma("sp", gat[:], g_att_l[:, :], writes=["gat"])
            P.op("pool", lambda e: e.iota(pidx[:], pattern=[[0, 1]], base=0, channel_multiplier=1,
                                          allow_small_or_imprecise_dtypes=True), writes=["pidx"])
            P.op("dve", lambda e: e.tensor_scalar(out=invf[:], in0=pidx[:], scalar1=64.0, scalar2=-64.0,
                                                  op0=ALU.is_ge, op1=ALU.mult), reads=["pidx"], writes=["invf"])
            P.op("dve", lambda e: e.tensor_tensor(out=invf[:], in0=invf[:], in1=pidx[:], op=ALU.add),
                 reads=["invf", "pidx"], writes=["invf"])
            P.op("act", lambda e: e.activation(out=invf[:], in_=invf[:], func=AF.Exp, scale=-math.log(10000.0) / 64.0),
                 reads=["invf"], writes=["invf"])
            P.op("dve", lambda e: e.tensor_scalar(out=nsgn[:], in0=pidx[:], scalar1=64.0, scalar2=2.0,
                                                  op0=ALU.is_ge, op1=ALU.mult), reads=["pidx"], writes=["nsgn"])
            P.op("dve", lambda e: e.tensor_scalar_add(out=nsgn[:], in0=nsgn[:], scalar1=-1.0),
                 reads=["nsgn"], writes=["nsgn"])
            P.op("dve", lambda e: e.tensor_copy(out=cosT[:], in_=posi[:]), reads=["posi"], writes=["cosT"])
            P.op("dve", lambda e: e.tensor_scalar(out=cosT[:], in0=cosT[:], scalar1=invf[:, 0:1], scalar2=None,
                                                  op0=ALU.mult), reads=["cosT", "invf"], writes=["cosT"])
            sincos(P, cosT, "cosT", sinT, "sinT", cosT, "cosT", angf, "angf", posi, "posi")
            P.op("dve", lambda e: e.tensor_scalar(out=sinT[:], in0=sinT[:], scalar1=nsgn[:, 0:1], scalar2=None,
                                                  op0=ALU.mult), reads=["sinT", "nsgn"], writes=["sinT"])
            P.op("dve", lambda e: e.tensor_copy(out=permf[:, 0:64], in_=identf[:, 64:128]), reads=["identf"], writes=["permf"])
            P.op("dve", lambda e: e.tensor_copy(out=permf[:, 64:128], in_=identf[:, 0:64]), reads=["identf"], writes=["permf"])
            P.op("pool", lambda e: e.memset(maskf[:], 1.0), writes=["maskf"])
            P.op("pool", lambda e: e.affine_select(out=maskf[:, 0:128], in_=maskf[:, 0:128], pattern=[[1, 128]],
                                                   compare_op=ALU.is_ge, fill=0.0, base=0, channel_multiplier=-1),
                 reads=["maskf"], writes=["maskf"])
            P.op("pool", lambda e: e.affine_select(out=maskf[:, 128:256], in_=maskf[:, 128:256], pattern=[[-1, 128]],
                                                   compare_op=ALU.is_ge, fill=0.0, base=0, channel_multiplier=1),
                 reads=["maskf"], writes=["maskf"])
            P.op("dve", lambda e: e.tensor_copy(out=maskb[:], in_=maskf[:]), reads=["maskf"], writes=["maskb"])

            if debug is not None and debug == (128, 2048):
                final_tokens.append(P.dma("sp", dbg[:, :], sinT[:], reads=["sinT"]))
            ph0.close()
            P.barrier()
            wq = [sb("wq%d" % i, [128, KT, 3, 128], BF16, ph) for i in range(2)]
            qf = sb("qf", [128, 512], F32, ph)
            t1 = sb("t1", [128, 512], F32, ph)
            t2 = sb("t2", [128, 512], F32, ph)
            qT = sb("qT", [128, S], BF16, ph)
            kT = sb("kT", [128, S], BF16, ph)
            vT = sb("vT", [128, S], BF16, ph)
            Vd = sb("Vd", [128, 48, 128], BF16, ph)
            qd = sb("qd", [128, S], BF16, ph)
            kd = sb("kd", [128, S], BF16, ph)
            print("sbuf remaining (att)", nc.sbuf_bytes_remaining)
            Pm = [sb("Pm%d" % i, [128, 256], BF16, ph) for i in range(2)]
            UT = sb("UT", [128, S], F32, ph)
            LT = sb("LT", [128, S], F32, ph)

            wiv = w_in.rearrange("(kt p) n -> p kt n", p=128)
            scale = 1.0 / math.sqrt(128.0)
            cs2m = sb("cs2m", [128, KT], F32, ph)
            cs2b = sb("cs2b", [128, KT], BF16, ph)
            wm2m = sb("wm2m", [128, KT, 128], BF16, ph)
            bm2m = sb("bm2m", [1, 128], F32, ph)
            rowm = sb("rowm", [1, 128], F32, ph)
            print("sbuf remaining (att2)", nc.sbuf_bytes_remaining)
            P.dma("sp", cs2m[:], c_l[:, :], writes=["cs2m"])
            P.op("act", lambda e: e.activation(out=cs2b[:], in_=cs2m[:], func=AF.Silu), reads=["cs2m"], writes=["cs2b"])
            wmv2 = w_mod.rearrange("(kt p) n -> p kt n", p=128)
            MODN = [0]

            def mod_prefetch(idx):
                c0 = 4096 + idx * 128
                P.dma("pool", wm2m[:], wmv2[:, :, c0:c0 + 128], writes=["wm2m"])
                P.dma("sp", bm2m[:], b_mod[0:1, c0:c0 + 128], writes=["bm2m"])

            def mod_step():
                if NH_RUN != 8 or MODN[0] >= 64:
                    return
                idx = MODN[0]
                MODN[0] += 1
                c0 = idx * 128
                for kt in range(KT):
                    P.op("pe", lambda e, kt=kt: e.matmul(ps[3][0:1, 0:128], lhsT=cs2b[:, kt:kt + 1], rhs=wm2m[:, kt, :],
                                                         start=(kt == 0), stop=(kt == KT - 1)),
                         reads=["cs2b", "wm2m"], writes=["ps3"])
                P.op("dve", lambda e: e.tensor_tensor(out=rowm[:], in0=ps[3][0:1, 0:128], in1=bm2m[:], op=ALU.add),
                     reads=["ps3", "bm2m"], writes=["rowm"])
                P.dma("sp", modrow[0:1, c0:c0 + 128], rowm[0:1, :], reads=["rowm"], writes=["modrow"])
                if idx + 1 < 64:
                    mod_prefetch(idx + 1)
            if NH_RUN == 8:
                mod_prefetch(0)
            def load_wq(hh):
                for i3 in range(3):
                    c0 = 1024 * (i3 + 1) + hh * 128
                    P.dma("pool", wq[hh % 2][:, :, i3, :], wiv[:, :, c0:c0 + 128], writes=["wq%d" % (hh % 2)])
            if NH_RUN > 0:
                load_wq(0)
            for h in range(NH_RUN):
                wb = wq[h % 2]
                wr = "wq%d" % (h % 2)
                if h + 1 < NH_RUN:
                    load_wq(h + 1)
                for i3, dstT in ((0, qT), (1, kT), (2, vT)):
                    dn = ("qT", "kT", "vT")[i3]
                    for tc in range(4):
                        if tc % 2 == 0:
                            mod_step()
                        pb = ps[tc % 2]
                        pr = "ps%d" % (tc % 2)
                        for kt in range(KT):
                            P.op("pe", lambda e, kt=kt, pb=pb, i3=i3, tc=tc, wb=wb: e.matmul(
                                pb[:], lhsT=wb[:, kt, i3, :], rhs=hT[:, kt, tc * 512:(tc + 1) * 512],
                                start=(kt == 0), stop=(kt == KT - 1)), reads=[wr, "hT"], writes=[pr])
                        if i3 == 2:
                            P.op("act", lambda e, pb=pb, tc=tc: e.activation(out=vT[:, tc * 512:(tc + 1) * 512], in_=pb[:],
                                                                            func=AF.Copy), reads=[pr], writes=["vT"])
                        else:
                            P.op("act", lambda e, pb=pb: e.activation(out=qf[:], in_=pb[:], func=AF.Copy),
                                 reads=[pr], writes=["qf"])
                            p2 = ps[2 + tc % 2]
                            p2r = "ps%d" % (2 + tc % 2)
                            P.op("pe", lambda e, p2=p2: e.matmul(p2[:], lhsT=permf[:], rhs=qf[:], start=True, stop=True),
                                 reads=["permf", "qf"], writes=[p2r])
                            P.op("dve", lambda e, tc=tc: e.tensor_tensor(out=t1[:], in0=qf[:], in1=cosT[:, tc * 512:(tc + 1) * 512],
                                                                        op=ALU.mult), reads=["qf", "cosT"], writes=["t1"])
                            P.op("dve", lambda e, tc=tc, p2=p2: e.tensor_tensor(out=t2[:], in0=p2[:], in1=sinT[:, tc * 512:(tc + 1) * 512],
                                                                               op=ALU.mult), reads=[p2r, "sinT"], writes=["t2"])
                            P.op("dve", lambda e, tc=tc, dstT=dstT: e.tensor_tensor(out=dstT[:, tc * 512:(tc + 1) * 512], in0=t1[:], in1=t2[:],
                                                                                   op=ALU.add), reads=["t1", "t2"], writes=[dn])
                if att_level < 2:
                    final_tokens.append(P.dma("sp", dbg[:, :], t1[:].bitcast(F32)[:, 0:512] if False else UTdump(P, qT, UT), reads=["UT"]))
                    break
                mod_step()
                tiles = []
                for d in (1, 4, 16):
                    nb = 16 // d
                    for r in range(d):
                        for kb in range(nb):
                            tiles.append((d, r, kb))
                for g in range(6):
                    pi = 2 + g % 2
                    pT = ps[pi][:].bitcast(BF16)
                    for j8 in range(8):
                        d, r, kb = tiles[g * 8 + j8]
                        st = r + d * 128 * kb
                        P.op("pe", lambda e, pT=pT, j8=j8, st=st, d=d: e.transpose(
                            out=pT[:, j8 * 128:(j8 + 1) * 128], in_=vT[:, st:st + 127 * d + 1:d], identity=ident[:]),
                            reads=["vT", "ident"], writes=["ps%d" % pi])
                    P.op("act", lambda e, g=g, pT=pT: e.activation(out=Vd[:, g * 8:(g + 1) * 8, :],
                                                                 in_=pT.rearrange("p (k t) -> p k t", k=8), func=AF.Copy),
                         reads=["ps%d" % pi], writes=["Vd"])
                if att_level < 3:
                    break
                mod_step()
                ti = 0
                sc_i = 0
                for d in (1, 4, 16):
                    nb = 16 // d
                    if d == 1:
                        qsrc, ksrc, qn, kn = qT, kT, "qT", "kT"
                    else:
                        P.op("act", lambda e, d=d: e.activation(out=qd[:].rearrange("p (r j) -> p r j", r=d),
                                                               in_=qT[:].rearrange("p (j r) -> p r j", r=d), func=AF.Copy),
                             reads=["qT"], writes=["qd"])
                        P.op("pool", lambda e, d=d: e.tensor_copy(out=kd[:].rearrange("p (r j) -> p r j", r=d),
                                                                 in_=kT[:].rearrange("p (j r) -> p r j", r=d)),
                             reads=["kT"], writes=["kd"])
                        qsrc, ksrc, qn, kn = qd, kd, "qd", "kd"
                    for r in range(d):
                        for kb in range(nb):
                            nq = 256 if kb + 1 < nb else 128
                            st = r + d * 128 * kb
                            c0 = r * (S // d) + 128 * kb
                            psS = ps[sc_i % 2]
                            psr = "ps%d" % (sc_i % 2)
                            pmb = Pm[sc_i % 2]
                            pmr = "Pm%d" % (sc_i % 2)
                            sc_i += 1
                            P.op("pe", lambda e, psS=psS, c0=c0, nq=nq, qsrc=qsrc, ksrc=ksrc: e.matmul(
                                psS[:, 0:nq], lhsT=ksrc[:, c0:c0 + 128], rhs=qsrc[:, c0:c0 + nq],
                                start=True, stop=True), reads=[kn, qn], writes=[psr])
                            P.op("act", lambda e, psS=psS, pmb=pmb, nq=nq: e.activation(out=pmb[:, 0:nq], in_=psS[:, 0:nq], func=AF.Exp,
                                                                                       scale=scale), reads=[psr], writes=[pmr])
                            P.op("dve", lambda e, pmb=pmb, nq=nq: e.tensor_tensor(out=pmb[:, 0:nq], in0=pmb[:, 0:nq], in1=maskb[:, 0:nq],
                                                                                 op=ALU.mult), reads=[pmr, "maskb"], writes=[pmr])
                            for part in range(nq // 128):
                                qb = kb + part
                                pU = ps[4 + qb % 2]
                                pur = "ps%d" % (4 + qb % 2)
                                pL = ps[6 + qb % 2]
                                plr = "ps%d" % (6 + qb % 2)
                                first = (part == 1) or (kb == 0)
                                last = (part == 0)
                                vt = Vd[:, ti, :]
                                P.op("pe", lambda e, pU=pU, vt=vt, pmb=pmb, part=part, first=first, last=last: e.matmul(
                                    pU[:, 0:128], lhsT=vt, rhs=pmb[:, part * 128:(part + 1) * 128], start=first, stop=last),
                                    reads=["Vd", pmr], writes=[pur])
                                P.op("pe", lambda e, pL=pL, pmb=pmb, part=part, first=first, last=last: e.matmul(
                                    pL[:, 0:128], lhsT=ones_bf[:], rhs=pmb[:, part * 128:(part + 1) * 128], start=first, stop=last),
                                    reads=["ones_bf", pmr], writes=[plr])
                                if last:
                                    qs = r + d * 128 * qb
                                    udst = UT[:, qs:qs + 127 * d + 1:d]
                                    ldst = LT[:, qs:qs + 127 * d + 1:d]
                                    if d == 1:
                                        P.op("dve", lambda e, pU=pU, udst=udst: e.tensor_copy(out=udst, in_=pU[:, 0:128]),
                                             reads=[pur], writes=["UT"])
                                        P.op("dve", lambda e, pL=pL, ldst=ldst: e.tensor_copy(out=ldst, in_=pL[:, 0:128]),
                                             reads=[plr], writes=["LT"])
                                    else:
                                        P.op("dve", lambda e, pU=pU, udst=udst: e.tensor_tensor(out=udst, in0=pU[:, 0:128], in1=udst, op=ALU.add),
                                             reads=[pur, "UT"], writes=["UT"])
                                        P.op("dve", lambda e, pL=pL, ldst=ldst: e.tensor_tensor(out=ldst, in0=pL[:, 0:128], in1=ldst, op=ALU.add),
                                             reads=[plr, "LT"], writes=["LT"])
                            ti += 1
                P.op("dve", lambda e: e.reciprocal(out=LT[:], in_=LT[:]), reads=["LT"], writes=["LT"])
                P.op("dve", lambda e: e.tensor_tensor(out=UT[:], in0=UT[:], in1=LT[:], op=ALU.mult), reads=["UT", "LT"], writes=["UT"])
                P.op("dve", lambda e: e.tensor_scalar(out=LT[:, 0:1024], in0=UT[:, 1024:2048], scalar1=selt[:, 1:2], scalar2=None, op0=ALU.mult),
                     reads=["UT", "selt"], writes=["LT"])
                P.op("dve", lambda e: e.scalar_tensor_tensor(out=LT[:, 1024:2048], in0=UT[:, 0:1024], scalar=selt[:, 0:1], in1=LT[:, 0:1024],
                                                             op0=ALU.mult, op1=ALU.add), reads=["UT", "selt", "LT"], writes=["LT"])
                P.op("act", lambda e: e.activation(out=LT[:, 0:1024], in_=LT[:, 1024:2048], func=AF.Square), reads=["LT"], writes=["LT"])
                if h == 0:
                    P.op("pool", lambda e: e.tensor_copy(out=sqatt[:], in_=LT[:, 0:1024]), reads=["LT"], writes=["sqatt"])
                else:
                    P.op("pool", lambda e: e.tensor_tensor(out=sqatt[:], in0=sqatt[:], in1=LT[:, 0:1024], op=ALU.add),
                         reads=["LT", "sqatt"], writes=["sqatt"])
                P.op("act", lambda e, h=h: e.activation(out=yT[:, 8 + h, :], in_=LT[:, 1024:2048], func=AF.Identity, scale=gat[:, h:h + 1]),
                     reads=["LT", "gat"], writes=["yT"])
                if debug is not None and debug == (128, 1024) and h == 0:
                    final_tokens.append(P.dma("sp", dbg[:, :], LT[:, 1024:2048], reads=["LT"]))
        P.barrier()


        a_re_l = din("a_re_l", [128, 32])
        a_im_l = din("a_im_l", [128, 32])
        ldt_l = din("ldt_l", [128, 32])
        BTr = din("BTr", [64, 16, 64])
        BTi = din("BTi", [64, 16, 64])
        CTr = din("CTr", [64, 64, 16])
        CTi = din("CTi", [64, 64, 16])
        d_l = din("d_l", [128, 8])
        w_glu = din("w_glu", [1024, 1024])
        b_glu_l = din("b_glu_l", [128, 8])
        g_ssm_l = din("g_ssm_l", [128, 8])
        NCT = 8 if (debug is None or ssm_full) else 1
        ph = ExitStack()
        with ph:
            are = sb("are", [128, 32], F32, ph)
            aim = sb("aim", [128, 32], F32, ph)
            dtt = sb("dtt", [128, 32], F32, ph)
            th = sb("th", [128, 32], F32, ph)
            mag = sb("mag", [128, 32], F32, ph)
            cth = sb("cth", [128, 32], F32, ph)
            sth = sb("sth", [128, 32], F32, ph)
            fre = sb("fre", [128, 32], F32, ph)
            fim = sb("fim", [128, 32], F32, ph)
            sm1 = sb("sm1", [128, 32], F32, ph)
            sm2 = sb("sm2", [128, 32], F32, ph)
            smi = sb("smi", [128, 32], I32, ph)
            dcol = sb("dcol", [128, 8], F32, ph)
            bgl = sb("bgl", [128, 8], F32, ph)
            gsl = sb("gsl", [128, 8], F32, ph)
            selt2 = sb("selt2", [128, 2], F32, ph)
            rowmask = sb("rowmask", [128, 4], F32, ph)
            iotaN = sb("iotaN", [128, 32], F32, ph)
            iotaD = sb("iotaD", [128, 64], F32, ph)
            cN = sb("cN", [128, 32, 32], F32, ph)
            sN = sb("sN", [128, 32, 32], F32, ph)
            cD = sb("cD", [128, 32, 64], F32, ph)
            sD = sb("sD", [128, 32, 64], F32, ph)
            Bc = [sb("Bc%d" % i, [128, 8, 128], BF16, ph) for i in range(2)]
            Cc = [sb("Cc%d" % i, [128, 32, 32], BF16, ph) for i in range(2)]
            Clp = [[sb("Clp%d_%d" % (ri, q), [128, 128], BF16, ph) for q in range(4)] for ri in range(2)]
            Blp = [[sb("Blp%d_%d" % (ri, k), [128, 128], BF16, ph) for k in range(2)] for ri in range(2)]
            vgT = sb("vgT", [128, 8, 1024], BF16, ph)
            ysel = sb("ysel", [128, 1024], F32, ph)

            P.dma("sp", are[:], a_re_l[:, :], writes=["are"])
            P.dma("sp", aim[:], a_im_l[:, :], writes=["aim"])
            P.dma("sp", dtt[:], ldt_l[:, :], writes=["dtt"])
            P.dma("sp", dcol[:], d_l[:, :], writes=["dcol"])
            P.dma("sp", bgl[:], b_glu_l[:, :], writes=["bgl"])
            P.dma("sp", gsl[:], g_ssm_l[:, :], writes=["gsl"])
            P.dma("sp", selt2[:], sel[:, :], writes=["selt2"])
            P.op("act", lambda e: e.activation(out=dtt[:], in_=dtt[:], func=AF.Exp), reads=["dtt"], writes=["dtt"])
            P.op("dve", lambda e: e.tensor_tensor(out=th[:], in0=aim[:], in1=dtt[:], op=ALU.mult), reads=["aim", "dtt"], writes=["th"])
            P.op("dve", lambda e: e.tensor_tensor(out=mag[:], in0=are[:], in1=dtt[:], op=ALU.mult), reads=["are", "dtt"], writes=["mag"])
            P.op("act", lambda e: e.activation(out=mag[:], in_=mag[:], func=AF.Exp), reads=["mag"], writes=["mag"])
            P.op("dve", lambda e: e.tensor_copy(out=cth[:], in_=th[:]), reads=["th"], writes=["cth"])
            sincos(P, cth, "cth", sth, "sth", cth, "cth", sm1, "sm1", smi, "smi")
            P.op("dve", lambda e: e.tensor_tensor(out=cth[:], in0=cth[:], in1=mag[:], op=ALU.mult), reads=["cth", "mag"], writes=["cth"])
            P.op("dve", lambda e: e.tensor_scalar_add(out=cth[:], in0=cth[:], scalar1=-1.0), reads=["cth"], writes=["cth"])
            P.op("dve", lambda e: e.tensor_tensor(out=sth[:], in0=sth[:], in1=mag[:], op=ALU.mult), reads=["sth", "mag"], writes=["sth"])
            P.op("dve", lambda e: e.tensor_tensor(out=sm1[:], in0=are[:], in1=are[:], op=ALU.mult), reads=["are"], writes=["sm1"])
            P.op("dve", lambda e: e.tensor_tensor(out=sm2[:], in0=aim[:], in1=aim[:], op=ALU.mult), reads=["aim"], writes=["sm2"])
            P.op("dve", lambda e: e.tensor_tensor(out=sm1[:], in0=sm1[:], in1=sm2[:], op=ALU.add), reads=["sm1", "sm2"], writes=["sm1"])
            P.op("dve", lambda e: e.reciprocal(out=sm1[:], in_=sm1[:]), reads=["sm1"], writes=["sm1"])
            P.op("dve", lambda e: e.tensor_tensor(out=fre[:], in0=cth[:], in1=are[:], op=ALU.mult), reads=["cth", "are"], writes=["fre"])
            P.op("dve", lambda e: e.tensor_tensor(out=sm2[:], in0=sth[:], in1=aim[:], op=ALU.mult), reads=["sth", "aim"], writes=["sm2"])
            P.op("dve", lambda e: e.tensor_tensor(out=fre[:], in0=fre[:], in1=sm2[:], op=ALU.add), reads=["fre", "sm2"], writes=["fre"])
            P.op("dve", lambda e: e.tensor_tensor(out=fre[:], in0=fre[:], in1=sm1[:], op=ALU.mult), reads=["fre", "sm1"], writes=["fre"])
            P.op("dve", lambda e: e.tensor_tensor(out=fim[:], in0=sth[:], in1=are[:], op=ALU.mult), reads=["sth", "are"], writes=["fim"])
            P.op("dve", lambda e: e.tensor_tensor(out=sm2[:], in0=cth[:], in1=aim[:], op=ALU.mult), reads=["cth", "aim"], writes=["sm2"])
            P.op("dve", lambda e: e.tensor_tensor(out=fim[:], in0=fim[:], in1=sm2[:], op=ALU.subtract), reads=["fim", "sm2"], writes=["fim"])
            P.op("dve", lambda e: e.tensor_tensor(out=fim[:], in0=fim[:], in1=sm1[:], op=ALU.mult), reads=["fim", "sm1"], writes=["fim"])
            P.op("pool", lambda e: e.iota(iotaN[:], pattern=[[64, 32]], base=0, channel_multiplier=0,
                                          allow_small_or_imprecise_dtypes=True), writes=["iotaN"])
            P.op("pool", lambda e: e.iota(iotaD[:], pattern=[[1, 64]], base=0, channel_multiplier=0,
                                          allow_small_or_imprecise_dtypes=True), writes=["iotaD"])
            P.op("pool", lambda e: e.memset(rowmask[:], 1.0), writes=["rowmask"])
            P.op("pool", lambda e: e.affine_select(out=rowmask[:], in_=rowmask[:], pattern=[[-32, 4]], compare_op=ALU.is_ge,
                                                   fill=0.0, base=0, channel_multiplier=1), reads=["rowmask"], writes=["rowmask"])
            P.op("pool", lambda e: e.affine_select(out=rowmask[:], in_=rowmask[:], pattern=[[32, 4]], compare_op=ALU.is_ge,
                                                   fill=0.0, base=31, channel_multiplier=-1), reads=["rowmask"], writes=["rowmask"])
            phs = ExitStack()
            tbf = sb("tbf", [128, 32 * 64], F32, phs)
            tbi = sb("tbi", [128, 32 * 64], I32, phs)
            Cf = [sb("Cf%d" % i, [128, 32, 32], F32, phs) for i in range(2)]
            Ct = [sb("Ct%d" % i, [128, 32, 32], F32, phs) for i in range(2)]
            cNf = cN[:].rearrange("p a b -> p (a b)")
            sNf = sN[:].rearrange("p a b -> p (a b)")
            cDf = cD[:].rearrange("p a b -> p (a b)")
            sDf = sD[:].rearrange("p a b -> p (a b)")
            P.op("dve", lambda e: e.tensor_tensor(out=cN[:], in0=th[:].unsqueeze(2).to_broadcast([128, 32, 32]),
                                                  in1=iotaN[:].unsqueeze(1).to_broadcast([128, 32, 32]), op=ALU.mult),
                 reads=["th", "iotaN"], writes=["cN"])
            sincos(P, cNf, "cN", sNf, "sN", cNf, "cN", tbf[:, 0:1024], "tbf", tbi[:, 0:1024], "tbi")
            P.op("dve", lambda e: e.tensor_tensor(out=cD[:], in0=th[:].unsqueeze(2).to_broadcast([128, 32, 64]),
                                                  in1=iotaD[:].unsqueeze(1).to_broadcast([128, 32, 64]), op=ALU.mult),
                 reads=["th", "iotaD"], writes=["cD"])
            sincos(P, cDf, "cD", sDf, "sD", cDf, "cD", tbf, "tbf", tbi, "tbi")
            for ri, CT in ((0, CTr), (1, CTi)):
                P.op("pool", lambda e, ri=ri: e.memset(Cf[ri][:], 0.0), writes=["Cf%d" % ri])
                for g2 in range(2):
                    src = CT.rearrange("(i r) p c -> r p i c", r=2)[g2]
                    P.dma("sp", Cf[ri][64 * g2:64 * g2 + 64, :, 16 * g2:16 * g2 + 16], src, writes=["Cf%d" % ri])
            fre_b = fre[:].unsqueeze(2).to_broadcast([128, 32, 32])
            fim_b = fim[:].unsqueeze(2).to_broadcast([128, 32, 32])
            P.op("dve", lambda e: e.tensor_tensor(out=Ct[0][:], in0=Cf[0][:], in1=fre_b, op=ALU.mult), reads=["Cf0", "fre"], writes=["Ct0"])
            P.op("dve", lambda e: e.tensor_tensor(out=Ct[1][:], in0=Cf[1][:], in1=fim_b, op=ALU.mult), reads=["Cf1", "fim"], writes=["Ct1"])
            P.op("dve", lambda e: e.tensor_tensor(out=Cc[0][:], in0=Ct[0][:], in1=Ct[1][:], op=ALU.subtract), reads=["Ct0", "Ct1"], writes=["Cc0"])
            P.op("dve", lambda e: e.tensor_tensor(out=Ct[0][:], in0=Cf[0][:], in1=fim_b, op=ALU.mult), reads=["Cf0", "fim"], writes=["Ct0"])
            P.op("dve", lambda e: e.tensor_tensor(out=Ct[1][:], in0=Cf[1][:], in1=fre_b, op=ALU.mult), reads=["Cf1", "fre"], writes=["Ct1"])
            P.op("dve", lambda e: e.tensor_tensor(out=Ct[0][:], in0=Ct[0][:], in1=Ct[1][:], op=ALU.add), reads=["Ct0", "Ct1"], writes=["Ct0"])
            P.op("dve", lambda e: e.tensor_scalar(out=Cc[1][:], in0=Ct[0][:], scalar1=-1.0, scalar2=None, op0=ALU.mult),
                 reads=["Ct0"], writes=["Cc1"])
            for ri, BT in ((0, BTr), (1, BTi)):
                P.op("pool", lambda e, ri=ri: e.memset(Bc[ri][:], 0.0), writes=["Bc%d" % ri])
                for q in range(4):
                    for g2 in range(2):
                        src = BT.rearrange("(Q r) c p -> r c Q p", r=8)[2 * q + g2]
                        P.dma("pool", Bc[ri][32 * q + 16 * g2:32 * q + 16 * g2 + 16, :, 64 * g2:64 * g2 + 64], src,
                              writes=["Bc%d" % ri])
                for q in range(4):
                    P.op("pool", lambda e, ri=ri, q=q: e.memset(Clp[ri][q][:], 0.0), writes=["Clp%d_%d" % (ri, q)])
            phs.close()
            P.barrier()

            phl = ExitStack()
            wu = [sb("wu%d" % i, [128, KT, 128], BF16, phl) for i in range(1)]
            uAB = sb("uAB", [128, S], BF16, phl)
            uT = [sb("uT%d" % i, [128, S], BF16, phl) for i in range(1)]
            tabC = [sb("tabC%d" % i, [128, 512], F32, phl) for i in range(2)]
            tabS = [sb("tabS%d" % i, [128, 512], F32, phl) for i in range(2)]
            ttmp = sb("ttmp", [128, 512], F32, phl)
            Wre = [sb("Wre%d" % i, [128, 512], F32, phl) for i in range(2)]
            Wim = [sb("Wim%d" % i, [128, 512], F32, phl) for i in range(2)]
            sre = [sb("sre%d" % i, [128, 512], BF16, phl) for i in range(2)]
            sim = [sb("sim%d" % i, [128, 512], BF16, phl) for i in range(2)]
            r1 = [sb("r1_%d" % i, [128, 512], F32, phl) for i in range(5)]
            print("sbuf remaining (ssm)", nc.sbuf_bytes_remaining)
            wiv = w_in.rearrange("(kt p) n -> p kt n", p=128)
            for ct in range(NCT):
                wub = wu[0]
                wur = "wu0"
                uTb = uT[0]
                uTr = "uT0"
                P.dma("pool", wub[:], wiv[:, :, ct * 128:(ct + 1) * 128], writes=[wur])
                for tc in range(4):
                    pb = ps[tc % 2]
                    pr = "ps%d" % (tc % 2)
                    for kt in range(KT):
                        P.op("pe", lambda e, kt=kt, pb=pb, tc=tc, wub=wub: e.matmul(
                            pb[:], lhsT=wub[:, kt, :], rhs=hT[:, kt, tc * 512:(tc + 1) * 512],
                            start=(kt == 0), stop=(kt == KT - 1)), reads=[wur, "hT"], writes=[pr])
                    P.op("act", lambda e, pb=pb, tc=tc, uTb=uTb: e.activation(out=uTb[:, tc * 512:(tc + 1) * 512], in_=pb[:], func=AF.Copy),
                         reads=[pr], writes=[uTr])
                P.op("dve", lambda e, uTb=uTb: e.tensor_scalar(out=uAB[:, 0:1024], in0=uTb[:, 0:1024], scalar1=selt2[:, 1:2], scalar2=None, op0=ALU.mult),
                     reads=[uTr, "selt2"], writes=["uAB"])
                P.op("dve", lambda e, uTb=uTb: e.tensor_scalar(out=uAB[:, 1024:2048], in0=uTb[:, 1024:2048], scalar1=selt2[:, 1:2], scalar2=None, op0=ALU.mult),
                     reads=[uTr, "selt2", "uAB"], writes=["uAB"])
                P.op("dve", lambda e, uTb=uTb: e.scalar_tensor_tensor(out=uAB[:, 1024:2048], in0=uTb[:, 0:1024], scalar=selt2[:, 0:1], in1=uAB[:, 1024:2048],
                                                                      op0=ALU.mult, op1=ALU.add), reads=[uTr, "selt2", "uAB"], writes=["uAB"])
                for q in range(4):
                    i = 4 * ct + q
                    bl = [Blp[0][i % 2], Blp[1][i % 2]]
                    blr = ["Blp0_%d" % (i % 2), "Blp1_%d" % (i % 2)]
                    for ri in range(2):
                        P.op("dve", lambda e, ri=ri, ct=ct, q=q, bl=bl: e.tensor_scalar(
                            out=bl[ri][:], in0=Bc[ri][:, ct, :], scalar1=rowmask[:, q:q + 1], scalar2=None, op0=ALU.mult),
                            reads=["Bc%d" % ri, "rowmask"], writes=[blr[ri]])
                        P.op("pool", lambda e, ri=ri, i=i, q=q: e.tensor_copy(out=Clp[ri][q][:, 32 * q:32 * q + 32], in_=Cc[ri][:, i, :]),
                             reads=["Cc%d" % ri], writes=["Clp%d_%d" % (ri, q)])
                    for tc in range(4):
                        gc = i * 4 + tc
                        k2 = gc % 2
                        sl = slice(tc * 512, (tc + 1) * 512)
                        tC, tS = tabC[k2], tabS[k2]
                        tCr, tSr = "tabC%d" % k2, "tabS%d" % k2
                        wre, wim = Wre[k2], Wim[k2]
                        wrr, wir = "Wre%d" % k2, "Wim%d" % k2
                        srb, sib = sre[k2], sim[k2]
                        srr, sir = "sre%d" % k2, "sim%d" % k2
                        cNb = cN[:, i, 8 * tc:8 * tc + 8].unsqueeze(2).to_broadcast([128, 8, 64])
                        sNb = sN[:, i, 8 * tc:8 * tc + 8].unsqueeze(2).to_broadcast([128, 8, 64])
                        cDb = cD[:, i, :].unsqueeze(1).to_broadcast([128, 8, 64])
                        sDb = sD[:, i, :].unsqueeze(1).to_broadcast([128, 8, 64])
                        tC3 = tC[:].rearrange("p (n d) -> p n d", d=64)
                        tS3 = tS[:].rearrange("p (n d) -> p n d", d=64)
                        tt3 = ttmp[:].rearrange("p (n d) -> p n d", d=64)
                        P.op("pool", lambda e, a=cNb, b=cDb, o=tC3: e.tensor_tensor(out=o, in0=a, in1=b, op=ALU.mult), reads=["cN", "cD"], writes=[tCr])
                        P.op("pool", lambda e, a=sNb, b=sDb, o=tt3: e.tensor_tensor(out=o, in0=a, in1=b, op=ALU.mult), reads=["sN", "sD"], writes=["ttmp"])
                        P.op("pool", lambda e, tC=tC: e.tensor_tensor(out=tC[:], in0=tC[:], in1=ttmp[:], op=ALU.subtract), reads=[tCr, "ttmp"], writes=[tCr])
                        P.op("pool", lambda e, a=sNb, b=cDb, o=tS3: e.tensor_tensor(out=o, in0=a, in1=b, op=ALU.mult), reads=["sN", "cD"], writes=[tSr])
                        P.op("pool", lambda e, a=cNb, b=sDb, o=tt3: e.tensor_tensor(out=o, in0=a, in1=b, op=ALU.mult), reads=["cN", "sD"], writes=["ttmp"])
                        P.op("pool", lambda e, tS=tS: e.tensor_tensor(out=tS[:], in0=tS[:], in1=ttmp[:], op=ALU.add), reads=[tSr, "ttmp"], writes=[tSr])
                        for ri in range(2):
                            P.op("pe", lambda e, ri=ri, bl=bl, uTb=uTb, sl=sl: e.matmul(ps[ri][:], lhsT=bl[ri][:], rhs=uAB[:, sl], start=True, stop=True),
                                 reads=[blr[ri], "uAB"], writes=["ps%d" % ri])
                        P.op("dve", lambda e, wre=wre, tC=tC: e.tensor_tensor(out=wre[:], in0=ps[0][:], in1=tC[:], op=ALU.mult), reads=["ps0", tCr], writes=[wrr])
                        P.op("dve", lambda e, tS=tS: e.tensor_tensor(out=r1[0][:], in0=ps[1][:], in1=tS[:], op=ALU.mult), reads=["ps1", tSr], writes=["r1_0"])
                        P.op("dve", lambda e, wim=wim, tC=tC: e.tensor_tensor(out=wim[:], in0=ps[1][:], in1=tC[:], op=ALU.mult), reads=["ps1", tCr], writes=[wir])
                        P.op("dve", lambda e, tS=tS: e.tensor_tensor(out=r1[1][:], in0=ps[0][:], in1=tS[:], op=ALU.mult), reads=["ps0", tSr], writes=["r1_1"])
                        P.op("dve", lambda e, wre=wre: e.tensor_tensor(out=wre[:], in0=wre[:], in1=r1[0][:], op=ALU.add), reads=[wrr, "r1_0"], writes=[wrr])
                        P.op("dve", lambda e, wim=wim: e.tensor_tensor(out=wim[:], in0=wim[:], in1=r1[1][:], op=ALU.subtract), reads=[wir, "r1_1"], writes=[wir])
                        for W, Wp, wn, wpn in ((wre, Wre[1 - k2], wrr, "Wre%d" % (1 - k2)), (wim, Wim[1 - k2], wir, "Wim%d" % (1 - k2))):
                            init = 0.0 if tc == 0 else Wp[:, 511:512]
                            P.op("dve", lambda e, W=W, init=init, i=i: e.tensor_tensor_scan(
                                out=W[:], data0=mag[:, i:i + 1].to_broadcast([128, 512]), data1=W[:], initial=init,
                                op0=ALU.mult, op1=ALU.add), reads=[wn, wpn, "mag"], writes=[wn])
                        if tc < 2:
                            continue
                        P.op("dve", lambda e, wre=wre, tC=tC: e.tensor_tensor(out=ttmp[:], in0=wre[:], in1=tC[:], op=ALU.mult), reads=[wrr, tCr], writes=["ttmp"])
                        P.op("dve", lambda e, wim=wim, tS=tS: e.tensor_tensor(out=r1[2][:], in0=wim[:], in1=tS[:], op=ALU.mult), reads=[wir, tSr], writes=["r1_2"])
                        P.op("dve", lambda e, wre=wre, tS=tS: e.tensor_tensor(out=r1[3][:], in0=wre[:], in1=tS[:], op=ALU.mult), reads=[wrr, tSr], writes=["r1_3"])
                        P.op("dve", lambda e, wim=wim, tC=tC: e.tensor_tensor(out=r1[4][:], in0=wim[:], in1=tC[:], op=ALU.mult), reads=[wir, tCr], writes=["r1_4"])
                        P.op("dve", lambda e, srb=srb: e.tensor_tensor(out=srb[:], in0=ttmp[:], in1=r1[2][:], op=ALU.subtract), reads=["ttmp", "r1_2"], writes=[srr])
                        P.op("dve", lambda e, sib=sib: e.tensor_tensor(out=sib[:], in0=r1[3][:], in1=r1[4][:], op=ALU.add), reads=["r1_3", "r1_4"], writes=[sir])
                        yb = ps[4 + tc]
                        ybr = "ps%d" % (4 + tc)
                        P.op("pe", lambda e, yb=yb, q=q, srb=srb: e.matmul(yb[:], lhsT=Clp[0][q][:], rhs=srb[:], start=(q == 0), stop=False),
                             reads=["Clp0_%d" % q, srr], writes=[ybr])
                        P.op("pe", lambda e, yb=yb, q=q, sib=sib: e.matmul(yb[:], lhsT=Clp[1][q][:], rhs=sib[:], start=False, stop=(q == 3)),
                             reads=["Clp1_%d" % q, sir], writes=[ybr])
                for tc in (2, 3):
                    sl = slice(tc * 512, (tc + 1) * 512)
                    yb = ps[4 + tc]
                    ybr = "ps%d" % (4 + tc)
                    o2 = slice((tc - 2) * 512, (tc - 1) * 512)
                    P.op("dve", lambda e, yb=yb, sl=sl, ct=ct, o2=o2: e.scalar_tensor_tensor(
                        out=ysel[:, o2], in0=uAB[:, sl], scalar=dcol[:, ct:ct + 1], in1=yb[:], op0=ALU.mult, op1=ALU.add),
                        reads=["uAB", "dcol", ybr], writes=["ysel"])
                P.op("act", lambda e, ct=ct: e.activation(out=vgT[:, ct, :], in_=ysel[:], func=AF.Gelu), reads=["ysel"], writes=["vgT"])
                if debug is not None and debug == (128, 1025) and ct == 0:
                    final_tokens.append(P.dma("sp", dbg[:, 0:1024], ysel[:], reads=["ysel"]))
            phl.close()
            P.barrier()
            if NCT == 8:
                wgl = sb("wgl", [128, 8, 1024], BF16, ph)
                gte = sb("gte", [128, 512], F32, ph)
                osm = sb("osm", [128, 1024], F32, ph)
                P.dma("pool", wgl[:], w_glu.rearrange("(kt p) n -> p kt n", p=128), writes=["wgl"])
                for jt in range(8):
                    for hc in range(2):
                        sl = slice(hc * 512, (hc + 1) * 512)
                        pb = ps[hc]
                        pr = "ps%d" % hc
                        for ct in range(8):
                            P.op("pe", lambda e, pb=pb, ct=ct, jt=jt, sl=sl: e.matmul(
                                pb[:], lhsT=wgl[:, ct, jt * 128:(jt + 1) * 128], rhs=vgT[:, ct, sl], start=(ct == 0), stop=(ct == 7)),
                                reads=["wgl", "vgT"], writes=[pr])
                        P.op("act", lambda e, pb=pb, jt=jt: e.activation(out=gte[:], in_=pb[:], func=AF.Sigmoid, bias=bgl[:, jt:jt + 1]),
                             reads=[pr, "bgl"], writes=["gte"])
                        P.op("dve", lambda e, jt=jt, sl=sl: e.tensor_tensor(out=osm[:, sl], in0=vgT[:, jt, sl], in1=gte[:], op=ALU.mult),
                             reads=["vgT", "gte"], writes=["osm"])
                    P.op("act", lambda e: e.activation(out=ysel[:], in_=osm[:], func=AF.Square), reads=["osm"], writes=["ysel"])
                    if jt == 0:
                        P.op("pool", lambda e: e.tensor_copy(out=sqssm[:], in_=ysel[:]), reads=["ysel"], writes=["sqssm"])
                    else:
                        P.op("pool", lambda e: e.tensor_tensor(out=sqssm[:], in0=sqssm[:], in1=ysel[:], op=ALU.add),
                             reads=["ysel", "sqssm"], writes=["sqssm"])
                    P.op("act", lambda e, jt=jt: e.activation(out=yT[:, jt, :], in_=osm[:], func=AF.Identity, scale=gsl[:, jt:jt + 1]),
                         reads=["osm", "gsl"], writes=["yT"])
                    if debug is not None and debug == (128, 1026) and jt == 0:
                        final_tokens.append(P.dma("sp", dbg[:, 0:1024], osm[:], reads=["osm"]))
        P.barrier()


        if debug is None or full_tail:
            w_out = din("w_out", [D, D])
            g_ffn = din("g_ffn", [1, D])
            w_gate = din("w_gate", [D, DFF])
            w_up = din("w_up", [D, DFF])
            w_down = din("w_down", [DFF, D])
            g_final = din("g_final", [1, D])
            x1v = hT[:].rearrange("p a b -> p (a b)").bitcast(F32).rearrange("p (t d) -> p t d", d=D)
            h2T = yT
            wmv = w_mod.rearrange("(kt p) n -> p kt n", p=128)

            mgc = [0]

            def mod_group(st, chunks, dsts, post=None):
                mgc[0] += 1
                sfx = "_g%d" % mgc[0]
                cs2 = sb("cs2" + sfx, [128, KT], F32, st)
                csr2 = sb("csr2" + sfx, [128, KT, 128], BF16, st)
                wm2 = [sb("wm2_%d%s" % (i, sfx), [128, KT, 512], BF16, st) for i in range(2)]
                bm2 = [sb("bm2_%d%s" % (i, sfx), [128, 512], F32, st) for i in range(2)]
                P.dma("sp", cs2[:], c_l[:, :], writes=["cs2"])
                P.op("act", lambda e: e.activation(out=cs2[:], in_=cs2[:], func=AF.Silu), reads=["cs2"], writes=["cs2"])
                P.op("dve", lambda e: e.tensor_copy(out=csr2[:], in_=cs2[:].unsqueeze(2).to_broadcast([128, KT, 128])),
                     reads=["cs2"], writes=["csr2"])
                for n, j in enumerate(chunks):
                    bi = n % 2
                    dst, dn = dsts[n // 4]
                    jj = n % 4
                    P.dma("pool", wm2[bi][:], wmv[:, :, j * 512:(j + 1) * 512], writes=["wm2_%d" % bi])
                    P.dma("sp", bm2[bi][:], b_mod[0:1, j * 512:(j + 1) * 512].partition_broadcast(128), writes=["bm2_%d" % bi])
                    pb = ps[bi]
                    for kt in range(KT):
                        P.op("pe", lambda e, kt=kt, pb=pb, bi=bi: e.matmul(pb[:], lhsT=csr2[:, kt, :], rhs=wm2[bi][:, kt, :],
                                                                        start=(kt == 0), stop=(kt == KT - 1)),
                             reads=["csr2", "wm2_%d" % bi], writes=["ps%d" % bi])
                    P.op("dve", lambda e, pb=pb, bi=bi, dst=dst, jj=jj: e.tensor_tensor(
                        out=dst[:, jj * 512:(jj + 1) * 512], in0=pb[:], in1=bm2[bi][:], op=ALU.add),
                        reads=["ps%d" % bi, "bm2_%d" % bi], writes=[dn])

            ph = ExitStack()
            with ph:
                onesf = sb("onesf", [128, 1], F32, ph)
                rsq = sb("rsq", [128, 16], F32, ph)
                gt1 = sb("gt1", [128, D], F32, ph)
                P.op("dve", lambda e: e.memset(onesf[:], 1.0), writes=["onesf"])
                for k, (sq, sqn) in enumerate(((sqssm, "sqssm"), (sqatt, "sqatt"))):
                    for tt in range(8):
                        P.op("pe", lambda e, k=k, tt=tt, sq=sq: e.matmul(ps[7][:, k * 8 + tt:k * 8 + tt + 1], lhsT=sq[:, tt * 128:(tt + 1) * 128],
                                                                      rhs=onesf[:, 0:1], start=True, stop=True),
                             reads=[sqn, "onesf"], writes=["ps7"])
                P.op("dve", lambda e: e.tensor_scalar(out=rsq[:], in0=ps[7][:, 0:16], scalar1=1.0 / 1024.0, scalar2=EPS,
                                                      op0=ALU.mult, op1=ALU.add), reads=["ps7"], writes=["rsq"])
                P.op("act", lambda e: e.activation(out=rsq[:], in_=rsq[:], func=AF.Sqrt), reads=["rsq"], writes=["rsq"])
                P.op("dve", lambda e: e.reciprocal(out=rsq[:], in_=rsq[:]), reads=["rsq"], writes=["rsq"])
                P.dma("sp", gt1[:], modrow[0:1, 0:D].partition_broadcast(128), reads=["modrow"], writes=["gt1"])
                for tt in range(8):
                    P.dma("sp", x1v[:, tt, :], xh[tt * 128:(tt + 1) * 128, :], writes=["hT"])
                wo = [sb("wo%d" % i, [128, KT, 512], BF16, ph) for i in range(2)]
                tA = sb("tA", [128, 512], F32, ph)
                tB = sb("tB", [128, 512], F32, ph)
                wov = w_out.rearrange("(kt p) n -> p kt n", p=128)
                for dc in range(4):
                    wb = wo[dc % 2]
                    wr = "wo%d" % (dc % 2)
                    dsl = slice(dc * 512, (dc + 1) * 512)
                    P.dma("pool", wb[:], wov[:, :, dsl], writes=[wr])
                    for tt in range(8):
                        k2 = (dc * 8 + tt) % 2
                        pS, pA = ps[2 * k2], ps[2 * k2 + 1]
                        pSr, pAr = "ps%d" % (2 * k2), "ps%d" % (2 * k2 + 1)
                        for ct in range(8):
                            P.op("pe", lambda e, pS=pS, ct=ct, tt=tt, wb=wb: e.matmul(
                                pS[:], lhsT=yT[:, ct, tt * 128:(tt + 1) * 128], rhs=wb[:, ct, :], start=(ct == 0), stop=(ct == 7)),
                                reads=["yT", wr], writes=[pSr])
                        for ct in range(8, 16):
                            P.op("pe", lambda e, pA=pA, ct=ct, tt=tt, wb=wb: e.matmul(
                                pA[:], lhsT=yT[:, ct, tt * 128:(tt + 1) * 128], rhs=wb[:, ct, :], start=(ct == 8), stop=(ct == 15)),
                                reads=["yT", wr], writes=[pAr])
                        P.op("act", lambda e, pS=pS, tt=tt: e.activation(out=tA[:], in_=pS[:], func=AF.Identity, scale=rsq[:, tt:tt + 1]),
                             reads=[pSr, "rsq"], writes=["tA"])
                        P.op("dve", lambda e, pA=pA, tt=tt: e.scalar_tensor_tensor(out=tB[:], in0=pA[:], scalar=rsq[:, 8 + tt:9 + tt], in1=tA[:],
                                                                                 op0=ALU.mult, op1=ALU.add), reads=[pAr, "rsq", "tA"], writes=["tB"])
                        P.op("dve", lambda e, dsl=dsl: e.tensor_tensor(out=tB[:], in0=tB[:], in1=gt1[:, dsl], op=ALU.mult),
                             reads=["tB", "gt1"], writes=["tB"])
                        P.op("dve", lambda e, tt=tt, dsl=dsl: e.tensor_tensor(out=x1v[:, tt, dsl], in0=x1v[:, tt, dsl], in1=tB[:], op=ALU.add),
                             reads=["tB", "hT"], writes=["hT"])
            P.barrier()

            ph = ExitStack()
            with ph:
                gt2 = sb("gt2", [128, D], F32, ph)
                ss2 = sb("ss2", [128, 8], F32, ph)
                rstd2 = sb("rstd2", [128, 8], F32, ph)
                phn = ExitStack()
                G2 = sb("G2", [128, D], F32, phn)
                sh2 = sb("sh2", [128, D], F32, phn)
                gf = sb("gf", [128, D], F32, phn)
                hf2 = sb("hf2", [128, D], F32, phn)
                hb2 = [sb("hb2_%d" % i, [128, D], BF16, phn) for i in range(2)]
                junk2 = sb("junk2", [128, D], BF16, phn)
                P.dma("sp", sh2[:], modrow[0:1, D:2 * D].partition_broadcast(128), reads=["modrow"], writes=["sh2"])
                P.dma("sp", G2[:], modrow[0:1, 2 * D:3 * D].partition_broadcast(128), reads=["modrow"], writes=["G2"])
                P.dma("sp", gt2[:], modrow[0:1, 3 * D:4 * D].partition_broadcast(128), reads=["modrow"], writes=["gt2"])
                P.dma("sp", gf[:], g_ffn[0:1, :].partition_broadcast(128), writes=["gf"])
                P.op("dve", lambda e: e.scalar_tensor_tensor(out=G2[:], in0=G2[:], scalar=1.0, in1=gf[:], op0=ALU.add, op1=ALU.mult),
                     reads=["G2", "gf"], writes=["G2"])
                for tt in range(8):
                    bi = tt % 2
                    P.op("act", lambda e, tt=tt: e.activation(out=junk2[:], in_=x1v[:, tt, :], func=AF.Square, accum_out=ss2[:, tt:tt + 1]),
                         reads=["hT"], writes=["junk2", "ss2_%d" % tt])
                    P.op("dve", lambda e, tt=tt: e.tensor_scalar(out=rstd2[:, tt:tt + 1], in0=ss2[:, tt:tt + 1], scalar1=1.0 / D, scalar2=EPS,
                                                                 op0=ALU.mult, op1=ALU.add), reads=["ss2_%d" % tt], writes=["rstd2_%d" % tt])
                    P.op("act", lambda e, tt=tt: e.activation(out=rstd2[:, tt:tt + 1], in_=rstd2[:, tt:tt + 1], func=AF.Sqrt),
                         reads=["rstd2_%d" % tt], writes=["rstd2_%d" % tt])
                    P.op("dve", lambda e, tt=tt: e.reciprocal(out=rstd2[:, tt:tt + 1], in_=rstd2[:, tt:tt + 1]),
                         reads=["rstd2_%d" % tt], writes=["rstd2_%d" % tt])
                    P.op("dve", lambda e, tt=tt: e.scalar_tensor_tensor(out=hf2[:], in0=x1v[:, tt, :], scalar=rstd2[:, tt:tt + 1], in1=G2[:],
                                                                        op0=ALU.mult, op1=ALU.mult), reads=["hT", "rstd2_%d" % tt, "G2"], writes=["hf2"])
                    hbb = hb2[bi]
                    P.op("dve", lambda e, hbb=hbb: e.tensor_tensor(out=hbb[:], in0=hf2[:], in1=sh2[:], op=ALU.add),
                         reads=["hf2", "sh2"], writes=["hb2_%d" % bi])
                    for g in range(2):
                        pi = 2 + (tt * 2 + g) % 2
                        pT = ps[pi][:].bitcast(BF16)
                        for k8 in range(8):
                            kt = g * 8 + k8
                            P.op("pe", lambda e, kt=kt, k8=k8, pT=pT, hbb=hbb: e.transpose(
                                out=pT[:, k8 * 128:(k8 + 1) * 128], in_=hbb[:, kt * 128:(kt + 1) * 128], identity=ident[:]),
                                reads=["hb2_%d" % bi, "ident"], writes=["ps%d" % pi])
                        P.op("act", lambda e, g=g, tt=tt, pT=pT: e.activation(
                            out=h2T[:, g * 8:(g + 1) * 8, tt * 128:(tt + 1) * 128], in_=pT.rearrange("p (k t) -> p k t", k=8), func=AF.Copy),
                            reads=["ps%d" % pi], writes=["yT"])
                phn.close()
                P.barrier()
                wg = [sb("wg%d" % i, [128, KT, 256], BF16, ph) for i in range(2)]
                wu2 = [sb("wu2_%d" % i, [128, KT, 256], BF16, ph) for i in range(2)]
                wd = sb("wd", [128, 4, D], BF16, ph)
                actT = sb("actT", [128, 4, 1024], BF16, ph)
                sg = [sb("sg%d" % i, [128, 512], F32, ph) for i in range(2)]
                tD = [sb("tD%d" % i, [128, 512], F32, ph) for i in range(2)]
                print("sbuf remaining (ffn)", nc.sbuf_bytes_remaining)
                wgv = w_gate.rearrange("(kt p) n -> p kt n", p=128)
                wuv = w_up.rearrange("(kt p) n -> p kt n", p=128)
                wdv = w_down.rearrange("(f p) d -> p f d", p=128)
                NFB = DFF // 512
                wi = 0
                ei = 0
                for fb in range(NFB):
                    for half in range(2):
                        k2 = wi % 2
                        wi += 1
                        c0 = fb * 512 + half * 256
                        P.dma("pool", wg[k2][:], wgv[:, :, c0:c0 + 256], writes=["wg%d" % k2])
                        P.dma("pool", wu2[k2][:], wuv[:, :, c0:c0 + 256], writes=["wu2_%d" % k2])
                        for f2 in range(2):
                            fi = half * 2 + f2
                            for hc in range(2):
                                tsl = slice(hc * 512, (hc + 1) * 512)
                                pG, pU_ = ps[hc], ps[2 + hc]
                                for kt in range(KT):
                                    P.op("pe", lambda e, pG=pG, kt=kt, k2=k2, f2=f2, tsl=tsl: e.matmul(
                                        pG[:], lhsT=wg[k2][:, kt, f2 * 128:(f2 + 1) * 128], rhs=h2T[:, kt, tsl], start=(kt == 0), stop=(kt == KT - 1)),
                                        reads=["wg%d" % k2, "yT"], writes=["ps%d" % hc])
                                for kt in range(KT):
                                    P.op("pe", lambda e, pU_=pU_, kt=kt, k2=k2, f2=f2, tsl=tsl: e.matmul(
                                        pU_[:], lhsT=wu2[k2][:, kt, f2 * 128:(f2 + 1) * 128], rhs=h2T[:, kt, tsl], start=(kt == 0), stop=(kt == KT - 1)),
                                        reads=["wu2_%d" % k2, "yT"], writes=["ps%d" % (2 + hc)])
                                P.op("act", lambda e, pG=pG, hc=hc: e.activation(out=sg[hc][:], in_=pG[:], func=AF.Silu),
                                     reads=["ps%d" % hc], writes=["sg%d" % hc])
                                P.op("dve", lambda e, pU_=pU_, hc=hc, fi=fi, tsl=tsl: e.tensor_tensor(out=actT[:, fi, tsl], in0=pU_[:], in1=sg[hc][:], op=ALU.mult),
                                     reads=["ps%d" % (2 + hc), "sg%d" % hc], writes=["actT"])
                    P.dma("pool", wd[:], wdv[:, fb * 4:(fb + 1) * 4, :], writes=["wd"])
                    for tt in range(8):
                        for dc in range(4):
                            dsl = slice(dc * 512, (dc + 1) * 512)
                            pk = 4 + ei % 4
                            tk = ei % 2
                            ei += 1
                            pD = ps[pk]
                            for fi in range(4):
                                P.op("pe", lambda e, pD=pD, fi=fi, tt=tt, dsl=dsl: e.matmul(
                                    pD[:], lhsT=actT[:, fi, tt * 128:(tt + 1) * 128], rhs=wd[:, fi, dsl], start=(fi == 0), stop=(fi == 3)),
                                    reads=["actT", "wd"], writes=["ps%d" % pk])
                            P.op("dve", lambda e, pD=pD, tk=tk, dsl=dsl: e.tensor_tensor(out=tD[tk][:], in0=pD[:], in1=gt2[:, dsl], op=ALU.mult),
                                 reads=["ps%d" % pk, "gt2"], writes=["tD%d" % tk])
                            P.op("dve", lambda e, tk=tk, tt=tt, dsl=dsl: e.tensor_tensor(out=x1v[:, tt, dsl], in0=x1v[:, tt, dsl], in1=tD[tk][:], op=ALU.add),
                                 reads=["tD%d" % tk, "hT"], writes=["hT"])
            P.barrier()

            ph = ExitStack()
            with ph:
                gfin = sb("gfin", [128, D], F32, ph)
                ss3 = sb("ss3", [128, 8], F32, ph)
                junk3 = sb("junk3", [128, D], BF16, ph)
                ot = [sb("ot%d" % i, [128, D], F32, ph) for i in range(2)]
                P.dma("sp", gfin[:], g_final[0:1, :].partition_broadcast(128), writes=["gfin"])
                for tt in range(8):
                    bi = tt % 2
                    P.op("act", lambda e, tt=tt: e.activation(out=junk3[:], in_=x1v[:, tt, :], func=AF.Square, accum_out=ss3[:, tt:tt + 1]),
                         reads=["hT"], writes=["junk3", "ss3_%d" % tt])
                    P.op("dve", lambda e, tt=tt: e.tensor_scalar(out=ss3[:, tt:tt + 1], in0=ss3[:, tt:tt + 1], scalar1=1.0 / D, scalar2=EPS,
                                                                 op0=ALU.mult, op1=ALU.add), reads=["ss3_%d" % tt], writes=["ss3_%d" % tt])
                    P.op("act", lambda e, tt=tt: e.activation(out=ss3[:, tt:tt + 1], in_=ss3[:, tt:tt + 1], func=AF.Sqrt),
                         reads=["ss3_%d" % tt], writes=["ss3_%d" % tt])
                    P.op("dve", lambda e, tt=tt: e.reciprocal(out=ss3[:, tt:tt + 1], in_=ss3[:, tt:tt + 1]),
                         reads=["ss3_%d" % tt], writes=["ss3_%d" % tt])
                    P.op("dve", lambda e, tt=tt, bi=bi: e.scalar_tensor_tensor(out=ot[bi][:], in0=x1v[:, tt, :], scalar=ss3[:, tt:tt + 1], in1=gfin[:],
                                                                             op0=ALU.mult, op1=ALU.mult), reads=["hT", "ss3_%d" % tt, "gfin"], writes=["ot%d" % bi])
                    final_tokens.append(P.dma("sp", out[tt * 128:(tt + 1) * 128, :], ot[bi][:], reads=["ot%d" % bi]))

        if debug is not None and not full_tail:
            zt = sb("zt", [128, D], F32)
            P.op("dve", lambda e: e.memset(zt[:], 0.0), writes=["zt"])
            for tt in range(8):
                final_tokens.append(P.dma("sp", out[tt * 128:(tt + 1) * 128, :], zt[:], reads=["zt"]))

        print('ops', P.simulate(), {e: len(v) for e, v in P.eng_ops.items()})
        with nc.Block() as block:
            P.emit(block, final_tokens)
    return nc


def make_in_maps(inputs):
    x = np.ascontiguousarray(inputs["x"], dtype=np.float32)
    c = np.asarray(inputs["c"], dtype=np.float32)
    pos = np.asarray(inputs["positions"], dtype=np.int32)
    maps = []
    for core in range(8):
        b, j = core // 2, core % 2
        selv = np.zeros((128, 2), np.float32)
        selv[:, j] = 1.0
        m = {
            "xb": x[b],
            "xh": np.ascontiguousarray(x[b, j * 1024:(j + 1) * 1024]),
            "c_l": np.ascontiguousarray(c[b].reshape(KT, 128).T),
            "pos": np.ascontiguousarray(pos[b].reshape(1, S)),
            "sel": selv,
            "w_mod": np.ascontiguousarray(inputs["w_mod"][0]),
            "b_mod": np.ascontiguousarray(inputs["b_mod"][0].reshape(1, -1)),
            "g_mix": np.ascontiguousarray(inputs["g_mix"][0].reshape(1, -1)),
            "w_in": np.ascontiguousarray(inputs["w_in"][0]),
            "g_att_l": np.ascontiguousarray(inputs["g_attn_out"][0].reshape(8, 128).T),
            "a_re_l": np.ascontiguousarray(inputs["ssm_a_re"][0].reshape(32, 128).T),
            "a_im_l": np.ascontiguousarray(inputs["ssm_a_im"][0].reshape(32, 128).T),
            "ldt_l": np.ascontiguousarray(np.repeat(inputs["ssm_log_dt"][0], 64).reshape(32, 128).T),
            "BTr": np.ascontiguousarray(inputs["ssm_b_re"][0].transpose(0, 2, 1)),
            "BTi": np.ascontiguousarray(inputs["ssm_b_im"][0].transpose(0, 2, 1)),
            "CTr": np.ascontiguousarray(inputs["ssm_c_re"][0].transpose(0, 2, 1)),
            "CTi": np.ascontiguousarray(inputs["ssm_c_im"][0].transpose(0, 2, 1)),
            "d_l": np.ascontiguousarray(inputs["ssm_d"][0].reshape(8, 128).T),
            "w_glu": np.ascontiguousarray(inputs["w_glu"][0]),
            "b_glu_l": np.ascontiguousarray(inputs["b_glu"][0].reshape(8, 128).T),
            "g_ssm_l": np.ascontiguousarray(inputs["g_ssm_out"][0].reshape(8, 128).T),
            "w_out": np.ascontiguousarray(inputs["w_out"][0]),
            "g_ffn": np.ascontiguousarray(inputs["g_ffn"][0].reshape(1, -1)),
            "w_gate": np.ascontiguousarray(inputs["w_gate"][0]),
            "w_up": np.ascontiguousarray(inputs["w_up"][0]),
            "w_down": np.ascontiguousarray(inputs["w_down"][0]),
            "g_final": np.ascontiguousarray(inputs["g_final"].reshape(1, -1)),
        }
        maps.append(m)
    return maps


def kernel(**inputs):
    nc = build()
    maps = make_in_maps(inputs)
    res = run_bass_kernel_spmd(nc, maps, core_ids=list(range(8)))
    outp = np.zeros((4, S, D), np.float32)
    for core in range(8):
        b, j = core // 2, core % 2
        outp[b, j * 1024:(j + 1) * 1024] = res.results[core]["out"]
    return outp
```
